# Optimizing a Trainium2 kernel written in Bass

```python
import jax, jax.numpy as jnp
from jax import lax
import numpy as np

D_MODEL = 2048
BATCH = 4
SEQ = 2048
DEPTH = 4
DEC_BATCH = 8
DEC_SEQ = 8
PAST_LEN = 16384
PAGE_SIZE = 128

N_MIXERS = 2
N_A_LAYERS = (DEPTH + 1) // 2
N_B_LAYERS = DEPTH // 2
WINDOWS = (128, 512, 2048)
DILATIONS = (1, 4, 16)
N_GROUPS = 3
H_G = 4
HD_A = 128
BLOCK = 128
CONV_W = 3
N_MEM = 256
H_M = 4
HD_M = D_MODEL // H_M
D_FF = 5632
EPS = 1e-6
NEG = -1e30

kernel_name = "hybrid_dilated_shortconv_decoder_step"


def _rmsnorm(x, g):
    x32 = x.astype(jnp.float32)
    y = x32 * lax.rsqrt(jnp.mean(x32 * x32, axis=-1, keepdims=True) + EPS)
    return (y * g.astype(jnp.float32)).astype(x.dtype)


def _causal_dwconv(u, prev, w):
    ext = jnp.concatenate([prev.astype(u.dtype), u], axis=1)
    T = u.shape[1]
    out = w[0] * ext[:, 0:T]
    for k in range(1, CONV_W):
        out = out + w[k] * ext[:, k:k + T]
    return out, ext[:, ext.shape[1] - (CONV_W - 1):]


def _dilated_band(q, k, v, dil, n_back):
    N, S, H, E = q.shape
    L = S // dil
    nb = -(-L // BLOCK)
    Lp = nb * BLOCK

    def prep(a):
        a = a.astype(jnp.float32).reshape(N, L, dil, H, E).transpose(0, 2, 1, 3, 4)
        a = jnp.pad(a, ((0, 0), (0, 0), (0, Lp - L), (0, 0), (0, 0)))
        return a.reshape(N, dil, nb, BLOCK, H, E)

    def band(a):
        prev = jnp.pad(a, ((0, 0), (0, 0), (1, 0), (0, 0), (0, 0), (0, 0)))[:, :, :nb]
        return jnp.concatenate([prev, a], axis=3)

    qb = prep(q)
    kk, vv = band(prep(k)), band(prep(v))
    s = jnp.einsum('nrbqhe,nrbkhe->nrbhqk', qb, kk) * (HD_A ** -0.5)
    qi = jnp.arange(BLOCK)[:, None] + BLOCK
    ki = jnp.arange(2 * BLOCK)[None, :]
    dist = qi - ki
    kpos = jnp.arange(nb)[:, None, None] * BLOCK - BLOCK + ki[None]
    valid = ((dist >= 0) & (dist <= n_back))[None] & (kpos >= 0)
    s = jnp.where(valid[None, None, :, None], s, NEG)
    lse = jax.nn.logsumexp(s, axis=-1)
    p = jnp.exp(s - lse[..., None])
    o = jnp.einsum('nrbhqk,nrbkhe->nrbqhe', p, vv)
    o = o.reshape(N, dil, Lp, H, E)[:, :, :L].transpose(0, 2, 1, 3, 4).reshape(N, S, H, E)
    lse = lse.transpose(0, 1, 2, 4, 3).reshape(N, dil, Lp, H)[:, :, :L]
    lse = lse.transpose(0, 2, 1, 3).reshape(N, S, H)
    return o, lse


def _dilated_gather(q, k_ext, v_ext, n_buf, dil, n_back):
    T = q.shape[1]
    idx = n_buf + jnp.arange(T)[:, None] - dil * jnp.arange(n_back + 1)[None, :]
    valid = idx >= 0
    idx = jnp.maximum(idx, 0)
    kg = k_ext.astype(jnp.float32)[:, idx]
    vg = v_ext.astype(jnp.float32)[:, idx]
    s = jnp.einsum('nthe,ntmhe->nthm', q.astype(jnp.float32), kg) * (HD_A ** -0.5)
    s = jnp.where(valid[None, :, None, :], s, NEG)
    lse = jax.nn.logsumexp(s, axis=-1)
    p = jnp.exp(s - lse[..., None])
    return jnp.einsum('nthm,ntmhe->nthe', p, vg), lse


def _merge_groups(outs, lses, w_out, dtype):
    wts = jax.nn.softmax(jnp.stack(lses), axis=0)
    o = jnp.einsum('gnth,gnthe->nthe', wts, jnp.stack(outs))
    N, T = o.shape[0], o.shape[1]
    return o.reshape(N, T, H_G * HD_A).astype(dtype) @ w_out


def _mixer_a_prompt(h, w_in, w_out):
    N, S, _ = h.shape
    qkv = (h @ w_in).reshape(N, S, N_GROUPS, 3, H_G, HD_A)
    outs, lses, bufs = [], [], []
    for g in range(N_GROUPS):
        o, lse = _dilated_band(qkv[:, :, g, 0], qkv[:, :, g, 1], qkv[:, :, g, 2],
                               DILATIONS[g], WINDOWS[g] // DILATIONS[g])
        outs.append(o)
        lses.append(lse)
        bufs.append(qkv[:, S - min(WINDOWS[g], S):, g, 1:])
    return _merge_groups(outs, lses, w_out, h.dtype), bufs


def _mixer_a_sample(h, bufs, w_in, w_out):
    N, T, _ = h.shape
    qkv = (h @ w_in).reshape(N, T, N_GROUPS, 3, H_G, HD_A)
    outs, lses, new_bufs = [], [], []
    for g in range(N_GROUPS):
        buf = bufs[g].astype(h.dtype)
        n_buf = buf.shape[1]
        ext = jnp.concatenate([buf, qkv[:, :, g, 1:]], axis=1)
        o, lse = _dilated_gather(qkv[:, :, g, 0], ext[:, :, 0], ext[:, :, 1], n_buf,
                                 DILATIONS[g], WINDOWS[g] // DILATIONS[g])
        outs.append(o)
        lses.append(lse)
        keep = min(WINDOWS[g], PAST_LEN + T)
        new_bufs.append(ext[:, ext.shape[1] - keep:])
    return _merge_groups(outs, lses, w_out, h.dtype), new_bufs


def _mixer_b(h, prev, w_in, w_conv, w_out):
    N, T, _ = h.shape
    u = (h @ w_in).reshape(N, T, 3, D_MODEL)
    gate_b, gate_c, val = u[:, :, 0], u[:, :, 1], u[:, :, 2]
    c, st = _causal_dwconv(gate_c * val, prev, w_conv)
    return (gate_b * c) @ w_out, st


def _mem_kv(mem, g, w_kv):
    N, M, _ = mem.shape
    return (_rmsnorm(mem, g) @ w_kv).reshape(N, M, 2, H_M, HD_M)


def _mem_attend(h, kv, w_q, w_o):
    N, T, _ = h.shape
    q = (h @ w_q).reshape(N, T, H_M, HD_M).astype(jnp.float32)
    k = kv[:, :, 0].astype(jnp.float32)
    v = kv[:, :, 1].astype(jnp.float32)
    s = jnp.einsum('nthe,nmhe->nhtm', q, k) * (HD_M ** -0.5)
    p = jax.nn.softmax(s, axis=-1)
    o = jnp.einsum('nhtm,nmhe->nthe', p, v)
    return o.reshape(N, T, H_M * HD_M).astype(h.dtype) @ w_o


def _conv_ffn(h, prev, w_up, w_conv, w_down):
    u, st = _causal_dwconv(h @ w_up, prev, w_conv)
    a, b = jnp.split(u, 2, axis=-1)
    return (jax.nn.silu(a) * b) @ w_down, st


def setup_inputs(seed: int = 0) -> dict:
    key = jax.random.key(seed)
    ks = jax.random.split(key, 32)
    f32 = jnp.float32

    def nrm(k, shape, scale=1.0):
        return jax.random.normal(k, shape, f32) * scale

    def gain(k, shape):
        return 1.0 + 0.05 * jax.random.normal(k, shape, f32)

    nbuf = [min(w, PAST_LEN) for w in WINDOWS]
    qkv_a = N_GROUPS * 3 * H_G * HD_A
    return {
        "x_prompt": nrm(ks[0], (BATCH, SEQ, D_MODEL)),
        "x_sample": nrm(ks[1], (DEC_BATCH, DEC_SEQ, D_MODEL)),
        "mem_prompt": nrm(ks[2], (BATCH, N_MEM, D_MODEL)),
        "cache_win0_kv": nrm(ks[3], (N_A_LAYERS, DEC_BATCH, nbuf[0], 2, H_G, HD_A)),
        "cache_win1_kv": nrm(ks[4], (N_A_LAYERS, DEC_BATCH, nbuf[1], 2, H_G, HD_A)),
        "cache_win2_kv": nrm(ks[5], (N_A_LAYERS, DEC_BATCH, nbuf[2], 2, H_G, HD_A)),
        "state_conv_b": nrm(ks[6], (N_B_LAYERS, DEC_BATCH, CONV_W - 1, D_MODEL)),
        "state_ffn_conv": nrm(ks[7], (DEPTH, DEC_BATCH, CONV_W - 1, 2 * D_FF)),
        "cache_mem_kv": nrm(ks[8], (DEPTH, DEC_BATCH, N_MEM, 2, H_M, HD_M)),
        "g_mix": gain(ks[9], (DEPTH, D_MODEL)),
        "g_mem_q": gain(ks[10], (DEPTH, D_MODEL)),
        "g_mem_kv": gain(ks[11], (DEPTH, D_MODEL)),
        "g_ffn": gain(ks[12], (DEPTH, D_MODEL)),
        "g_final": gain(ks[13], (D_MODEL,)),
        "w_in_a": nrm(ks[14], (N_A_LAYERS, D_MODEL, qkv_a), D_MODEL ** -0.5),
        "w_out_a": nrm(ks[15], (N_A_LAYERS, H_G * HD_A, D_MODEL), (H_G * HD_A) ** -0.5),
        "w_in_b": nrm(ks[16], (N_B_LAYERS, D_MODEL, 3 * D_MODEL), D_MODEL ** -0.5),
        "conv_b": nrm(ks[17], (N_B_LAYERS, CONV_W, D_MODEL), CONV_W ** -0.5),
        "w_out_b": nrm(ks[18], (N_B_LAYERS, D_MODEL, D_MODEL), D_MODEL ** -0.5),
        "w_q_mem": nrm(ks[19], (DEPTH, D_MODEL, H_M * HD_M), D_MODEL ** -0.5),
        "w_kv_mem": nrm(ks[20], (DEPTH, D_MODEL, 2 * H_M * HD_M), D_MODEL ** -0.5),
        "w_o_mem": nrm(ks[21], (DEPTH, H_M * HD_M, D_MODEL), (H_M * HD_M) ** -0.5),
        "w_up": nrm(ks[22], (DEPTH, D_MODEL, 2 * D_FF), D_MODEL ** -0.5),
        "conv_ffn": nrm(ks[23], (DEPTH, CONV_W, 2 * D_FF), CONV_W ** -0.5),
        "w_down": nrm(ks[24], (DEPTH, D_FF, D_MODEL), D_FF ** -0.5),
    }


def reference(x_prompt, x_sample, mem_prompt, cache_win0_kv, cache_win1_kv, cache_win2_kv,
              state_conv_b, state_ffn_conv, cache_mem_kv, g_mix, g_mem_q, g_mem_kv, g_ffn, g_final,
              w_in_a, w_out_a, w_in_b, conv_b, w_out_b, w_q_mem, w_kv_mem, w_o_mem,
              w_up, conv_ffn, w_down):
    n_p = x_prompt.shape[0]
    dt = x_prompt.dtype
    yp, ys = x_prompt, x_sample
    win_p = [[] for _ in range(N_GROUPS)]
    win_s = [[] for _ in range(N_GROUPS)]
    conv_p, conv_s, ffn_p, ffn_s, mem_p = [], [], [], [], []
    for i in range(DEPTH):
        j = i // N_MIXERS
        hp, hs = _rmsnorm(yp, g_mix[i]), _rmsnorm(ys, g_mix[i])
        if i % N_MIXERS == 0:
            op, bp = _mixer_a_prompt(hp, w_in_a[j], w_out_a[j])
            osm, bs = _mixer_a_sample(hs, (cache_win0_kv[j], cache_win1_kv[j], cache_win2_kv[j]),
                                      w_in_a[j], w_out_a[j])
            for g in range(N_GROUPS):
                win_p[g].append(bp[g])
                win_s[g].append(bs[g])
        else:
            zp = jnp.zeros((n_p, CONV_W - 1, D_MODEL), dt)
            op, sp = _mixer_b(hp, zp, w_in_b[j], conv_b[j], w_out_b[j])
            osm, ss = _mixer_b(hs, state_conv_b[j], w_in_b[j], conv_b[j], w_out_b[j])
            conv_p.append(sp)
            conv_s.append(ss)
        yp = yp + op
        ys = ys + osm
        kv_p = _mem_kv(mem_prompt, g_mem_kv[i], w_kv_mem[i])
        mem_p.append(kv_p)
        yp = yp + _mem_attend(_rmsnorm(yp, g_mem_q[i]), kv_p, w_q_mem[i], w_o_mem[i])
        ys = ys + _mem_attend(_rmsnorm(ys, g_mem_q[i]), cache_mem_kv[i], w_q_mem[i], w_o_mem[i])
        zf = jnp.zeros((n_p, CONV_W - 1, 2 * D_FF), dt)
        fp, sfp = _conv_ffn(_rmsnorm(yp, g_ffn[i]), zf, w_up[i], conv_ffn[i], w_down[i])
        fs, sfs = _conv_ffn(_rmsnorm(ys, g_ffn[i]), state_ffn_conv[i], w_up[i], conv_ffn[i], w_down[i])
        ffn_p.append(sfp)
        ffn_s.append(sfs)
        yp = yp + fp
        ys = ys + fs
    y_prompt = _rmsnorm(yp, g_final)
    y_sample = _rmsnorm(ys, g_final)
    p_win0, p_win1, p_win2 = jnp.stack(win_p[0]), jnp.stack(win_p[1]), jnp.stack(win_p[2])
    s_win0, s_win1, s_win2 = jnp.stack(win_s[0]), jnp.stack(win_s[1]), jnp.stack(win_s[2])
    p_conv_b, s_conv_b = jnp.stack(conv_p), jnp.stack(conv_s)
    p_ffn_conv, s_ffn_conv = jnp.stack(ffn_p), jnp.stack(ffn_s)
    p_mem_kv = jnp.stack(mem_p)
    return (y_prompt, y_sample, p_win0, p_win1, p_win2, p_conv_b, p_ffn_conv, p_mem_kv,
            s_win0, s_win1, s_win2, s_conv_b, s_ffn_conv)
```

```python
import numpy as np
import ml_dtypes
import concourse.bass as bass
import concourse.mybir as mybir
from concourse.bass_utils import run_bass_kernel_spmd

F32 = mybir.dt.float32
BF16 = mybir.dt.bfloat16
AF = mybir.ActivationFunctionType
ALU = mybir.AluOpType

D = 2048
KC = 16
DFF = 5632
NCOL = 1050
CT = [(0, 350), (350, 700), (700, 1050)]
SP0, SS0, SS1 = 1024, 1026, 1034
WIN = (128, 512, 2048)
DIL = (1, 4, 16)
MW = 2448
MWG = (1152, 1536, 2448)
NEG = -30000.0
EPS = 1e-6
LIM = 4000


class Ctx:
    def __init__(self, nc):
        self.nc = nc
        self.engs = {'pe': nc.tensor, 'act': nc.scalar, 'dve': nc.vector, 'pool': nc.gpsimd, 'sp': nc.sync}
        self.count = {e: 0 for e in self.engs}
        self.sems = {e: [] for e in self.engs}
        self.waited = {e: {} for e in self.engs}
        self.semobj = {}
        self.lastw = {}
        self.readers = {}
        self.dsems = [nc.alloc_semaphore(f"dq{i}") for i in range(40)]
        self.dtot = [0] * len(self.dsems)
        self.dnext = 0
        for s in self.dsems:
            self.semobj[id(s)] = s

    def _tok_of(self, e):
        n = self.count[e]
        if n == 0:
            return None
        n -= 1
        return (id(self.sems[e][n // LIM]), n % LIM + 1)

    def _wait(self, e, tok):
        if tok is None:
            return
        sid, val = tok
        if self.waited[e].get(sid, 0) >= val:
            return
        self.engs[e].wait_ge(self.semobj[sid], val)
        self.waited[e][sid] = val

    def begin(self, e, reads, writes):
        for r in reads:
            self._wait(e, self.lastw.get(r))
        for w in writes:
            self._wait(e, self.lastw.get(w))
            rd = self.readers.get(w)
            if rd:
                for sid, val in rd.items():
                    self._wait(e, (sid, val))

    def _register(self, tok, reads, writes):
        sid, val = tok
        for r in reads:
            d = self.readers.setdefault(r, {})
            if d.get(sid, 0) < val:
                d[sid] = val
        for w in writes:
            self.lastw[w] = tok
            self.readers[w] = {}

    def end(self, e, inst, reads, writes):
        n = self.count[e]
        si = n // LIM
        if si >= len(self.sems[e]):
            s = self.nc.alloc_semaphore(f"e_{e}_{si}")
            self.sems[e].append(s)
            self.semobj[id(s)] = s
        sem = self.sems[e][si]
        val = n % LIM + 1
        inst.then_inc(sem, 1)
        self.count[e] = n + 1
        tok = (id(sem), val)
        if e == 'pe':
            self.waited[e][id(sem)] = val
        self._register(tok, reads, writes)

    def op(self, e, fn, reads=(), writes=()):
        self.begin(e, reads, writes)
        inst = fn(self.engs[e])
        self.end(e, inst, reads, writes)

    def mmg(self, fns, reads=(), writes=()):
        self.begin('pe', reads, writes)
        inst = None
        for f in fns:
            inst = f(self.nc.tensor)
        self.end('pe', inst, reads, writes)

    def dma(self, q, out, in_, reads=(), writes=()):
        self.begin(q, reads, writes)
        j = self.dnext
        self.dnext = (j + 1) % len(self.dsems)
        sem = self.dsems[j]
        if self.dtot[j] > 0:
            self._wait(q, (id(sem), self.dtot[j]))
        self.engs[q].dma_start(out=out, in_=in_).then_inc(sem, 16)
        self.dtot[j] += 16
        self._register((id(sem), self.dtot[j]), reads, writes)

    def barrier(self):
        for e in self.engs:
            for e2 in self.engs:
                if e2 != e:
                    self._wait(e, self._tok_of(e2))
            for j, s in enumerate(self.dsems):
                if self.dtot[j] > 0:
                    self._wait(e, (id(s), self.dtot[j]))


def _mask_np():
    m = np.full((3, 128, MW), NEG, np.float32)
    i = np.arange(128)[:, None]
    c = np.arange(MW)[None, :]
    delta = c - i - 384
    for g in range(3):
        d = DIL[g]
        valid = (delta >= 0) & (delta <= 128 * d) & (delta % d == 0)
        m[g][valid] = 0.0
    return m


MASKNP = _mask_np()


def _blk_valid(g, vq0, n, vk0, nk):
    c0 = vq0 - vk0 + 384
    if c0 < 0 or c0 + n > MW:
        return False
    return bool((MASKNP[g][:nk, c0:c0 + n] == 0.0).any())


def build_program(phases=("mix", "mem", "ffn", "final"), npass=2, layers=(0, 1, 2, 3)):
    nc = bass.Bass("TRN2", target_bir_lowering=False)
    cx = Ctx(nc)
    DECL.clear()

    def din(name, shape, dt=F32):
        DECL.add(name)
        return nc.dram_tensor(name, list(shape), dt, kind="ExternalInput").ap()

    class Lazy:
        def __init__(self, name, shape):
            self.name, self.shape, self.ap = name, shape, None

        def __getitem__(self, idx):
            if self.ap is None:
                self.ap = din(self.name, self.shape)
            return self.ap[idx]

    def dinl(name, shape):
        return Lazy(name, shape)

    def dout(name, shape):
        return nc.dram_tensor(name, list(shape), F32, kind="ExternalOutput").ap()

    xp = din("xp", [2048, D]); xs = din("xs", [8, D]); mem = din("mem", [256, D])
    cw = [din("cw0", [2, 128, 1024]), din("cw1", [2, 512, 1024]), din("cw2", [2, 2048, 1024])]
    cmem = din("cmem", [4, 256, 4096])
    gfm = din("gfm", [128, 17 * 16])
    cffn = din("cffn", [128, 4 * 3 * 88]); cb_in = din("cbw", [128, 2 * 3 * 16])
    sfc_in = din("sfc", [128, 4 * 88 * 2]); scb_in = din("scb", [128, 2 * 16 * 2])
    msk_in = [din("msk%d" % g, [128, MWG[g]], BF16) for g in range(3)]
    idf_in = din("idf", [128, 128]); idb_in = din("idb", [128, 128], BF16); onb_in = din("onb", [128, 128], BF16)
    w_in_a = dinl("w_in_a", [2, D, 4608]); w_out_a = dinl("w_out_a", [2, 512, D])
    w_in_b = dinl("w_in_b", [2, D, 3 * D]); w_out_b = dinl("w_out_b", [2, D, D])
    w_q_mem = dinl("w_q_mem", [4, D, D]); w_kv_mem = dinl("w_kv_mem", [4, D, 2 * D]); w_o_mem = dinl("w_o_mem", [4, D, D])
    w_up = dinl("w_up", [4, D, 2 * DFF]); w_down = dinl("w_down", [4, DFF, D])

    yp = dout("yp", [2048, D]); ys = dout("ys", [8, D])
    pw = [dout("pw0", [2, 128, 1024]), dout("pw1", [2, 512, 1024]), dout("pw2", [2, 2048, 1024])]
    pcb = dout("pcb", [2, 2, D]); pfc = dout("pfc", [4, 2, 2 * DFF]); pmk = dout("pmk", [4, 256, 4096])
    sw = [dout("sw0", [2, 128, 1024]), dout("sw1", [2, 512, 1024]), dout("sw2", [2, 2048, 1024])]
    scbo = dout("scbo", [2, 2, D]); sfco = dout("sfco", [4, 2, 2 * DFF])
    kth = nc.dram_tensor("kth", [2, 4, 3, 128, 1024], BF16).ap()
    vh = nc.dram_tensor("vh", [2, 4, 3, 1024, 128], BF16).ap()

    ph = _patch_alloc(nc)
    sb = ph
    X = sb([128, KC, NCOL], F32)
    H = sb([128, KC, NCOL + 2], BF16)
    STG = [sb([128, KC, 128], F32) for _ in range(2)]
    BFW = [sb([128, KC, 128], BF16) for _ in range(3)]
    RS = sb([128, NCOL], F32)
    SQ = [sb([128, 352], BF16) for _ in range(3)]
    G_ = sb([128, 17, 16], F32)
    CWF = sb([128, 4, 3, 88], F32)
    CWB = sb([128, 2, 3, 16], F32)
    SFC = sb([128, 4, 88, 2], F32)
    SCB = sb([128, 2, 16, 2], F32)
    IDF = sb([128, 128], F32); IDB = sb([128, 128], BF16); ONB = sb([128, 128], BF16)
    HS = sb([128, 12, KC, 2], BF16)
    CBF = sb([128, 2, 88], F32)
    CBO = sb([128, 2, 128], F32)
    PS = []
    for _ in range(8):
        _cm = nc.psum_tensor("ps%d" % len(PS), [128, 512], F32)
        PS.append(_cm.__enter__())
    SETS = [[0, 1, 2], [3, 4, 5]]
    HALL = [("H", c) for c in range(KC)]
    state = {"wload": 0, "wcast": 0, "pset": 0, "sq": 0}

    class WS:
        def __init__(self, blocks):
            self.blocks = blocks
            self.loaded = 0
            self.casted = 0
            self.slots = {}

        def _load(self, i):
            ap, kcn = self.blocks[i]
            s = state["wload"] % 2
            state["wload"] += 1
            cx.dma('sp', STG[s][:, 0:kcn, :], ap.rearrange("(kc p) n -> p kc n", p=128), writes=[("STG", s)])
            self.slots[i] = [s, None]

        def _cast(self, i):
            ap, kcn = self.blocks[i]
            s = self.slots[i][0]
            b = state["wcast"] % 3
            state["wcast"] += 1
            cx.op('pool', lambda e: e.tensor_copy(out=BFW[b][:, 0:kcn, :], in_=STG[s][:, 0:kcn, :]),
                  reads=[("STG", s)], writes=[("BFW", b)])
            self.slots[i][1] = b

        def get(self, i):
            n = len(self.blocks)
            while self.casted <= min(i + 1, n - 1):
                while self.loaded <= min(self.casted + 1, n - 1):
                    self._load(self.loaded)
                    self.loaded += 1
                self._cast(self.casted)
                self.casted += 1
            b = self.slots[i][1]
            return BFW[b], ("BFW", b)

    def nextset():
        s = SETS[state["pset"] % 2]
        state["pset"] += 1
        return s

    def proj_fm(wb, wkey, kcn, rhs_fn, rkeys, pset, tiles=CT, ext=0):
        for ti, (a, b) in enumerate(tiles):
            n = b - a + ext
            ps = PS[pset[ti]]
            cx.mmg([(lambda e, kc=kc: e.matmul(ps[:, 0:n], lhsT=wb[:, kc, :], rhs=rhs_fn(kc, a, n),
                                               start=(kc == 0), stop=(kc == kcn - 1))) for kc in range(kcn)],
                   reads=[wkey] + rkeys, writes=[("PS", pset[ti])])

    def add_to_x(nchunk, pset):
        for ti, (a, b) in enumerate(CT):
            ps = PS[pset[ti]]
            cx.op('dve', lambda e: e.tensor_tensor(out=X[:, nchunk, a:b], in0=ps[:, 0:b - a], in1=X[:, nchunk, a:b], op=ALU.add),
                  reads=[("PS", pset[ti]), ("X", nchunk)], writes=[("X", nchunk)])

    def load_consts():
        for t, src, key in ((G_, gfm, "G"), (CWF, cffn, "CWF"), (CWB, cb_in, "CWB"), (SFC, sfc_in, "SFC"),
                            (SCB, scb_in, "SCB"), (IDF, idf_in, "IDF"), (IDB, idb_in, "IDB"), (ONB, onb_in, "ONB")):
            shp = list(t.shape)
            dst = t[:]
            if len(shp) == 3:
                srcv = src.rearrange("p (a b) -> p a b", a=shp[1])
            elif len(shp) == 4:
                srcv = src.rearrange("p (a b c) -> p a b c", a=shp[1], b=shp[2])
            else:
                srcv = src
            cx.dma('sp', dst, srcv, writes=[key])
        cx.op('pool', lambda e: e.memset(H[:], 0.0), writes=HALL)
        cx.op('pool', lambda e: e.memset(CBO[:], 0.0), writes=["CBO"])

    CONSTK = ["G", "CWF", "CWB", "SFC", "SCB", "IDF", "IDB", "ONB"]

    def load_x(p):
        XS = [sb([128, D], F32) for _ in range(2)]
        XSS = sb([8, D], F32)
        cx.op('pool', lambda e: e.memset(X[:, :, 1024:NCOL], 0.0), writes=[("X", c) for c in range(KC)])
        bank = 0
        for tb in range(8):
            s = tb % 2
            cx.dma('sp', XS[s][:], xp[p * 1024 + tb * 128: p * 1024 + (tb + 1) * 128, :], writes=[("XS", s)])
            for c4 in range(4):
                pb = 6 + bank % 2
                bank += 1
                cx.mmg([(lambda e, k=k: e.transpose(out=PS[pb][:, k * 128:(k + 1) * 128],
                                                    in_=XS[s][:, (c4 * 4 + k) * 128:(c4 * 4 + k + 1) * 128], identity=IDF[:]))
                        for k in range(4)], reads=[("XS", s), "IDF"], writes=[("PS", pb)])
                eng = 'act' if c4 % 2 == 0 else 'dve'
                outap = X[:, c4 * 4:c4 * 4 + 4, tb * 128:(tb + 1) * 128]
                inap = PS[pb][:].rearrange("p (a b) -> p a b", a=4)
                if eng == 'act':
                    cx.op('act', lambda e: e.copy(out=outap, in_=inap), reads=[("PS", pb)],
                          writes=[("X", c4 * 4 + k) for k in range(4)])
                else:
                    cx.op('dve', lambda e: e.tensor_copy(out=outap, in_=inap), reads=[("PS", pb)],
                          writes=[("X", c4 * 4 + k) for k in range(4)])
        if p == 0:
            cx.dma('sp', XSS[:], xs, writes=["XSS"])
            pb = 6
            cx.mmg([(lambda e, c=c: e.transpose(out=PS[pb][:, c * 8:(c + 1) * 8], in_=XSS[0:8, c * 128:(c + 1) * 128],
                                                identity=IDF[0:8, 0:8])) for c in range(KC)],
                   reads=["XSS", "IDF"], writes=[("PS", pb)])
            cx.op('act', lambda e: e.copy(out=X[:, :, SS0:SS1], in_=PS[pb][:, 0:128].rearrange("p (a b) -> p a b", a=KC)),
                  reads=[("PS", pb)], writes=[("X", c) for c in range(KC)])
        cx.barrier()

    def rstd_cols(src_fn, tiles, rs_ap_fn, skeys):
        for (a, b) in tiles:
            n = b - a
            for c in range(KC):
                q = state["sq"] % 3
                state["sq"] += 1
                cx.op('act', lambda e: e.activation(out=SQ[q][:, 0:n], in_=src_fn(c, a, b), func=AF.Square),
                      reads=[skeys[c]], writes=[("SQ", q)])
                cx.mmg([lambda e: e.matmul(PS[7][:, 0:n], lhsT=ONB[:], rhs=SQ[q][:, 0:n], start=(c == 0), stop=(c == KC - 1))],
                       reads=[("SQ", q), "ONB"], writes=[("PS", 7)])
            cx.op('act', lambda e: e.activation(out=rs_ap_fn(a, b), in_=PS[7][:, 0:n], func=AF.Sqrt, scale=1.0 / D, bias=EPSAP[:]),
                  reads=[("PS", 7), "EPS"], writes=["RS"])
            cx.op('dve', lambda e: e.reciprocal(out=rs_ap_fn(a, b), in_=rs_ap_fn(a, b)), reads=["RS"], writes=["RS"])

    EPSAP = sb([128, 1], F32)

    def norm(p, gi, save):
        rstd_cols(lambda c, a, b: X[:, c, a:b], CT, lambda a, b: RS[:, a:b], [("X", c) for c in range(KC)])
        k = 0
        for (a, b) in CT:
            for c in range(KC):
                eng = 'dve'
                k += 1
                cx.op(eng, lambda e: e.scalar_tensor_tensor(out=H[:, c, 2 + a:2 + b], in0=X[:, c, a:b], scalar=G_[:, gi, c:c + 1],
                                                            in1=RS[:, a:b], op0=ALU.mult, op1=ALU.mult),
                      reads=[("X", c), "RS", "G"], writes=[("H", c)])
        if save is not None:
            if p == 0:
                cx.op('act', lambda e: e.copy(out=HS[:, save, :, :], in_=H[:, :, 2 + 1022:2 + 1024]), reads=HALL, writes=[("HS", save)])
            else:
                cx.op('act', lambda e: e.copy(out=H[:, :, 0:2], in_=HS[:, save, :, :]), reads=[("HS", save)], writes=HALL)

    def emit_conv_out(dst_p, dst_s, p, li, nchunk, key):
        dst = dst_s if p == 0 else dst_p
        for r in range(2):
            cx.mmg([lambda e: e.transpose(out=PS[6][0:nchunk, 0:128], in_=CBF[:, r, 0:nchunk], identity=IDF[:])],
                   reads=[key, "IDF"], writes=[("PS", 6)])
            cx.op('act', lambda e: e.copy(out=CBO[0:nchunk, r, :], in_=PS[6][0:nchunk, 0:128]), reads=[("PS", 6)], writes=["CBO"])
            cx.dma('act', dst[li, r, :].rearrange("(c q) -> c q", q=128), CBO[0:nchunk, r, :], reads=["CBO"])

    def ffn(p, layer):
        norm(p, 12 + layer, 6 + layer)
        Gq = [sb([128, 11, NCOL], BF16) for _ in range(2)]
        TA = sb([128, 3, 350], F32); SA = sb([128, 3, 350], F32); TB = sb([128, 3, 350], F32)
        blocks = []
        for q in range(4):
            for j in range(11):
                a = 11 * q + j
                blocks.append((w_up[layer, :, a * 128:(a + 1) * 128], KC))
                blocks.append((w_up[layer, :, DFF + a * 128:DFF + (a + 1) * 128], KC))
            for n in range(KC):
                blocks.append((w_down[layer, q * 1408:(q + 1) * 1408, n * 128:(n + 1) * 128], 11))
        ws = WS(blocks)
        bi = 0
        lo = 334 if p == 0 else 324
        for q in range(4):
            G = Gq[q % 2]
            for j in range(11):
                for half in range(2):
                    chunk = 11 * q + j + 44 * half
                    wb, wkey = ws.get(bi); bi += 1
                    pset = nextset()
                    proj_fm(wb, wkey, KC, lambda kc, a, n: H[:, kc, a:a + n], HALL, pset, ext=2)
                    T = TA if half == 0 else TB
                    tk = "TA" if half == 0 else "TB"
                    for ti, (a, b) in enumerate(CT):
                        ps = PS[pset[ti]]
                        pk = ("PS", pset[ti])
                        if ti == 2:
                            if p == 0:
                                cx.op('act', lambda e: e.copy(out=ps[:, 326:328], in_=SFC[:, layer, chunk, :]), reads=["SFC", pk], writes=[pk])
                            cx.op('act', lambda e: e.copy(out=CBF[:, :, chunk], in_=ps[:, lo:lo + 2]), reads=[pk], writes=["CBF"])
                        cx.op('act', lambda e: e.mul(out=T[:, ti, :], in_=ps[:, 2:352], mul=CWF[:, layer, 2, chunk:chunk + 1]),
                              reads=[pk, "CWF"], writes=[(tk, ti)])
                        cx.op('dve', lambda e: e.scalar_tensor_tensor(out=T[:, ti, :], in0=ps[:, 1:351], scalar=CWF[:, layer, 1, chunk:chunk + 1],
                                                                      in1=T[:, ti, :], op0=ALU.mult, op1=ALU.add),
                              reads=[pk, "CWF", (tk, ti)], writes=[(tk, ti)])
                        cx.op('dve', lambda e: e.scalar_tensor_tensor(out=T[:, ti, :], in0=ps[:, 0:350], scalar=CWF[:, layer, 0, chunk:chunk + 1],
                                                                      in1=T[:, ti, :], op0=ALU.mult, op1=ALU.add),
                              reads=[pk, "CWF", (tk, ti)], writes=[(tk, ti)])
                        if half == 0:
                            cx.op('act', lambda e: e.activation(out=SA[:, ti, :], in_=T[:, ti, :], func=AF.Silu),
                                  reads=[(tk, ti)], writes=[("SA", ti)])
                        else:
                            cx.op('pool', lambda e: e.tensor_tensor(out=G[:, j, a:b], in0=SA[:, ti, :], in1=TB[:, ti, :], op=ALU.mult),
                                  reads=[("SA", ti), ("TB", ti)], writes=[("G", q % 2, j)])
            for n in range(KC):
                wb, wkey = ws.get(bi); bi += 1
                pset = nextset()
                proj_fm(wb, wkey, 11, lambda kc, a, nn: G[:, kc, a:a + nn], [("G", q % 2, j) for j in range(11)], pset)
                add_to_x(n, pset)
        emit_conv_out(pfc, sfco, p, layer, 88, "CBF")
        cx.barrier()

    def mixer_b(p, layer):
        j = layer // 2
        norm(p, layer, layer)
        Y = sb([128, KC, NCOL], BF16)
        T1 = sb([128, 3, 352], F32); T2 = sb([128, 3, 352], F32); T3 = sb([128, 3, 350], F32)
        blocks = []
        for c in range(KC):
            blocks.append((w_in_b[j, :, D + c * 128:D + (c + 1) * 128], KC))
            blocks.append((w_in_b[j, :, 2 * D + c * 128:2 * D + (c + 1) * 128], KC))
            blocks.append((w_in_b[j, :, c * 128:(c + 1) * 128], KC))
        for n in range(KC):
            blocks.append((w_out_b[j, :, n * 128:(n + 1) * 128], KC))
        ws = WS(blocks)
        bi = 0
        lo = 334 if p == 0 else 324
        rf = lambda kc, a, n: H[:, kc, a:a + n]
        for c in range(KC):
            wb, wkey = ws.get(bi); bi += 1
            pset = nextset()
            proj_fm(wb, wkey, KC, rf, HALL, pset, ext=2)
            for ti in range(3):
                cx.op('act', lambda e: e.copy(out=T1[:, ti, :], in_=PS[pset[ti]][:, 0:352]), reads=[("PS", pset[ti])], writes=[("T1", ti)])
            wb, wkey = ws.get(bi); bi += 1
            pset = nextset()
            proj_fm(wb, wkey, KC, rf, HALL, pset, ext=2)
            for ti in range(3):
                cx.op('dve', lambda e: e.tensor_tensor(out=T2[:, ti, :], in0=PS[pset[ti]][:, 0:352], in1=T1[:, ti, :], op=ALU.mult),
                      reads=[("PS", pset[ti]), ("T1", ti)], writes=[("T2", ti)])
                if ti == 2:
                    if p == 0:
                        cx.op('act', lambda e: e.copy(out=T2[:, 2, 326:328], in_=SCB[:, j, c, :]), reads=["SCB", ("T2", 2)], writes=[("T2", 2)])
                    cx.op('act', lambda e: e.copy(out=CBF[:, :, c], in_=T2[:, 2, lo:lo + 2]), reads=[("T2", 2)], writes=["CBF"])
                cx.op('act', lambda e: e.mul(out=T3[:, ti, :], in_=T2[:, ti, 2:352], mul=CWB[:, j, 2, c:c + 1]),
                      reads=[("T2", ti), "CWB"], writes=[("T3", ti)])
                cx.op('dve', lambda e: e.scalar_tensor_tensor(out=T3[:, ti, :], in0=T2[:, ti, 1:351], scalar=CWB[:, j, 1, c:c + 1],
                                                               in1=T3[:, ti, :], op0=ALU.mult, op1=ALU.add),
                      reads=[("T2", ti), "CWB", ("T3", ti)], writes=[("T3", ti)])
                cx.op('dve', lambda e: e.scalar_tensor_tensor(out=T3[:, ti, :], in0=T2[:, ti, 0:350], scalar=CWB[:, j, 0, c:c + 1],
                                                               in1=T3[:, ti, :], op0=ALU.mult, op1=ALU.add),
                      reads=[("T2", ti), "CWB", ("T3", ti)], writes=[("T3", ti)])
            wb, wkey = ws.get(bi); bi += 1
            pset = nextset()
            proj_fm(wb, wkey, KC, rf, HALL, pset, ext=2)
            for ti, (a, b) in enumerate(CT):
                cx.op('dve', lambda e: e.tensor_tensor(out=Y[:, c, a:b], in0=PS[pset[ti]][:, 2:352], in1=T3[:, ti, :], op=ALU.mult),
                      reads=[("PS", pset[ti]), ("T3", ti)], writes=[("Y", c)])
        for n in range(KC):
            wb, wkey = ws.get(bi); bi += 1
            pset = nextset()
            proj_fm(wb, wkey, KC, lambda kc, a, nn: Y[:, kc, a:a + nn], [("Y", c) for c in range(KC)], pset)
            add_to_x(n, pset)
        emit_conv_out(pcb, scbo, p, j, 16, "CBF")
        cx.barrier()

    def attend(n, s_groups, u_list, scale, rz, out_fn, ekeybase):
        nb = len(s_groups)
        zidx = u_list[-1]
        uidx = u_list[:-1]
        sbanks = [b for b in range(8) if b not in u_list][:3]
        for bi, (nk, sfn, sreads, vaps, vreads) in enumerate(s_groups):
            sbk = sbanks[bi % len(sbanks)]
            S = PS[sbk]
            cx.mmg(sfn(S), reads=sreads, writes=[("PS", sbk)])
            ek = bi % 3
            cx.op('act', lambda e: e.activation(out=EB[ek][0:nk, 0:n], in_=S[0:nk, 0:n], func=AF.Exp, scale=scale),
                  reads=[("PS", sbk)], writes=[("E", ek)])
            for ui, u in enumerate(uidx):
                cx.mmg([lambda e: e.matmul(PS[u][:, 0:n], lhsT=vaps[ui], rhs=EB[ek][0:nk, 0:n], start=(bi == 0), stop=(bi == nb - 1))],
                       reads=[("E", ek)] + vreads, writes=[("PS", u)])
            cx.mmg([lambda e: e.matmul(PS[zidx][:, 0:n], lhsT=ONB[0:nk, :], rhs=EB[ek][0:nk, 0:n], start=(bi == 0), stop=(bi == nb - 1))],
                   reads=[("E", ek), "ONB"], writes=[("PS", zidx)])
        cx.op('dve', lambda e: e.reciprocal(out=rz[:, 0:n], in_=PS[zidx][:, 0:n]), reads=[("PS", zidx)], writes=["RZ"])
        for ui, u in enumerate(uidx):
            oap, okeys = out_fn(ui)
            cx.op('dve', lambda e: e.tensor_tensor(out=oap, in0=PS[u][:, 0:n], in1=rz[:, 0:n], op=ALU.mult),
                  reads=[("PS", u), "RZ"], writes=okeys)

    EB = [sb([128, 512], BF16) for _ in range(3)]
    RZ = sb([128, 512], F32)

    def mem_attn(p, layer):
        MK = sb([128, KC, 256], BF16)
        MV = sb([128, 2, D], BF16)
        _mk = nc_mark(nc)
        MS = sb([128, D], F32); MEMN = sb([128, KC, 256], F32); HM = sb([128, KC, 256], BF16)
        RSM = sb([128, 256], F32); TMO = [sb([128, 2, 128], F32) for _ in range(2)]
        for tb in range(2):
            cx.dma('sp', MS[:], mem[tb * 128:(tb + 1) * 128, :], writes=["MS"])
            for c4 in range(4):
                pb = 6
                cx.mmg([(lambda e, k=k: e.transpose(out=PS[pb][:, k * 128:(k + 1) * 128],
                                                    in_=MS[:, (c4 * 4 + k) * 128:(c4 * 4 + k + 1) * 128], identity=IDF[:]))
                        for k in range(4)], reads=["MS", "IDF"], writes=[("PS", pb)])
                cx.op('act', lambda e: e.copy(out=MEMN[:, c4 * 4:c4 * 4 + 4, tb * 128:(tb + 1) * 128],
                                              in_=PS[pb][:].rearrange("p (a b) -> p a b", a=4)),
                      reads=[("PS", pb)], writes=["MEMN"])
        rstd_cols(lambda c, a, b: MEMN[:, c, a:b], [(0, 256)], lambda a, b: RSM[:, a:b], ["MEMN"] * KC)
        for c in range(KC):
            cx.op('dve', lambda e: e.scalar_tensor_tensor(out=HM[:, c, :], in0=MEMN[:, c, :], scalar=G_[:, 8 + layer, c:c + 1],
                                                          in1=RSM[:, :], op0=ALU.mult, op1=ALU.mult),
                  reads=["MEMN", "RS", "G"], writes=["HM"])
        blocks = [(w_kv_mem[layer, :, cb * 128:(cb + 1) * 128], KC) for cb in range(32)]
        ws = WS(blocks)
        pm = pmk[layer].rearrange("(tb q) n -> q tb n", q=128)
        for cb in range(32):
            wb, wkey = ws.get(cb)
            if cb < 16:
                cx.mmg([(lambda e, kc=kc: e.matmul(PS[5][:, 0:256], lhsT=wb[:, kc, :], rhs=HM[:, kc, :], start=(kc == 0), stop=(kc == KC - 1)))
                        for kc in range(KC)], reads=[wkey, "HM"], writes=[("PS", 5)])
                cx.op('act', lambda e: e.copy(out=MK[:, cb, :], in_=PS[5][:, 0:256]), reads=[("PS", 5)], writes=["MK"])
            pb = 6 + cb % 2
            for tb in range(2):
                cx.mmg([(lambda e, kc=kc: e.matmul(PS[pb][:, tb * 128:(tb + 1) * 128], lhsT=HM[:, kc, tb * 128:(tb + 1) * 128], rhs=wb[:, kc, :],
                                                   start=(kc == 0), stop=(kc == KC - 1))) for kc in range(KC)],
                       reads=[wkey, "HM"], writes=[("PS", pb)])
            t = TMO[cb % 2]
            if p == 0 or cb >= 16:
                cx.op('act', lambda e: e.copy(out=t[:], in_=PS[pb][:, 0:256].rearrange("p (a b) -> p a b", a=2)),
                      reads=[("PS", pb)], writes=[("TMO", cb % 2)])
            if p == 0:
                cx.dma('act', pm[:, :, cb * 128:(cb + 1) * 128], t[:], reads=[("TMO", cb % 2)])
            if cb >= 16:
                cx.op('dve', lambda e: e.tensor_copy(out=MV[:, :, (cb - 16) * 128:(cb - 15) * 128], in_=t[:]),
                      reads=[("TMO", cb % 2)], writes=["MV"])
        cx.barrier()
        nc_release(nc, _mk)
        if STOPAT[0] == 1:
            return
        norm(p, 4 + layer, None)
        QT = sb([128, KC, NCOL], BF16)
        blocks = [(w_q_mem[layer, :, n * 128:(n + 1) * 128], KC) for n in range(KC)]
        blocks += [(w_o_mem[layer, :, n * 128:(n + 1) * 128], KC) for n in range(KC)]
        ws = WS(blocks)
        for n in range(KC):
            wb, wkey = ws.get(n)
            pset = nextset()
            proj_fm(wb, wkey, KC, lambda kc, a, nn: H[:, kc, 2 + a:2 + a + nn], HALL, pset)
            for ti, (a, b) in enumerate(CT):
                cx.op('act', lambda e: e.copy(out=QT[:, n, a:b], in_=PS[pset[ti]][:, 0:350]), reads=[("PS", pset[ti])], writes=[("QT", n)])
        if STOPAT[0] == 2:
            cx.barrier()
            return
        scale = 512.0 ** -0.5
        QALL = [("QT", n) for n in range(KC)]

        def run(h, q0, q1, kfn, vfn, kkeys):
            n = q1 - q0
            groups = []
            for kb in range(2):
                def sfn(S, kb=kb):
                    return [(lambda e, ec=ec: e.matmul(S[:, 0:n], lhsT=kfn(ec, kb), rhs=QT[:, h * 4 + ec, q0:q1], start=(ec == 0), stop=(ec == 3)))
                            for ec in range(4)]
                groups.append((128, sfn, kkeys + QALL, [vfn(ec, kb) for ec in range(4)], kkeys))
            attend(n, groups, [3, 4, 5, 6, 7], scale, RZ,
                   lambda ui: (H[:, h * 4 + ui, 2 + q0:2 + q1], [("H", h * 4 + ui)]), "E")

        for h in range(4):
            for (q0, q1) in ((0, 512), (512, 1024)):
                run(h, q0, q1, lambda ec, kb: MK[:, h * 4 + ec, kb * 128:(kb + 1) * 128],
                    lambda ec, kb: MV[:, kb, h * 512 + ec * 128:h * 512 + (ec + 1) * 128], ["MK", "MV"])
        if STOPAT[0] == 3:
            cx.barrier()
            return
        if p == 0:
            CK = sb([128, 2, 512], F32); CV = CK
            SK = sb([128, 4, 256], BF16); SV = sb([128, 2, 512], BF16)
            cm = cmem[layer].rearrange("(kb q) n -> q kb n", q=128)
            for h in range(4):
                cx.dma('sp', CK[:], cm[:, :, h * 512:(h + 1) * 512], writes=["CK"])
                for kb in range(2):
                    cx.mmg([(lambda e, ec=ec: e.transpose(out=PS[0][:, ec * 128:(ec + 1) * 128], in_=CK[:, kb, ec * 128:(ec + 1) * 128], identity=IDF[:]))
                            for ec in range(4)], reads=["CK", "IDF"], writes=[("PS", 0)])
                    cx.op('act', lambda e: e.copy(out=SK[:, :, kb * 128:(kb + 1) * 128], in_=PS[0][:].rearrange("p (a b) -> p a b", a=4)),
                          reads=[("PS", 0)], writes=["SK"])
                cx.dma('sp', CV[:], cm[:, :, D + h * 512:D + (h + 1) * 512], writes=["CK"])
                cx.op('pool', lambda e: e.tensor_copy(out=SV[:], in_=CV[:]), reads=["CK"], writes=["SV"])
                run(h, SS0, SS1, lambda ec, kb: SK[:, ec, kb * 128:(kb + 1) * 128],
                    lambda ec, kb: SV[:, kb, ec * 128:(ec + 1) * 128], ["SK", "SV"])
        cx.op('pool', lambda e: e.memset(H[:, :, 0:2], 0.0), writes=HALL)
        z0 = 2 + (SP0 if p == 0 else 1024)
        cx.op('pool', lambda e: e.memset(H[:, :, z0:z0 + 2] if p == 0 else H[:, :, z0:NCOL + 2], 0.0), writes=HALL)
        if p == 0:
            cx.op('pool', lambda e: e.memset(H[:, :, 2 + SS1:NCOL + 2], 0.0), writes=HALL)
        for n in range(KC):
            wb, wkey = ws.get(KC + n)
            pset = nextset()
            proj_fm(wb, wkey, KC, lambda kc, a, nn: H[:, kc, 2 + a:2 + a + nn], HALL, pset)
            add_to_x(n, pset)
        cx.barrier()

    def mixer_a(p, layer):
        j = layer // 2
        norm(p, layer, None)
        MSK = [sb([128, MWG[g]], BF16) for g in range(3)]
        OTA = sb([128, 4, NCOL], BF16)
        QT = sb([128, 3, NCOL], BF16); KT = sb([128, 3, NCOL], BF16)
        VT = sb([128, 3, 9, 128], BF16)
        TM = [sb([128, 9, 128], F32) for _ in range(1)]
        if p == 1:
            HK = sb([128, 1664], BF16); HV = sb([128, 13, 128], BF16)
        else:
            CS = sb([128, 16, 128], F32); SKT = sb([128, 2048], BF16); SVb = sb([128, 16, 128], BF16)
        for g in range(3):
            cx.dma('sp', MSK[g][:], msk_in[g], writes=["MSK"])
        cx.op('pool', lambda e: e.memset(OTA[:], 0.0), writes=["OTA"])
        for t in TM:
            cx.op('pool', lambda e, t=t: e.memset(t[:], 0.0), writes=["TM0", "TM1"])
        if p == 0:
            for g in range(3):
                W = WIN[g]
                cx.dma('sp', sw[g][j, 0:W - 8, :], cw[g][j, 8:W, :])
        scale = 128.0 ** -0.5
        hoff = (0, 128, 640)
        hboff = (0, 1, 5)
        tmi = 0
        ntb = 9 if p == 0 else 8

        def proj_tm(wb, wkey, g, kv, h):
            nonlocal tmi
            t = TM[0]
            tk = "TM0"
            tmi += 1
            for tb in range(ntb):
                pb = 6 + (tb // 4) % 2
                m = 128 if tb < 8 else 8
                c0 = 2 + (tb * 128 if tb < 8 else SS0)
                cx.mmg([(lambda e, kc=kc: e.matmul(PS[pb][0:m, (tb % 4) * 128:(tb % 4 + 1) * 128], lhsT=H[:, kc, c0:c0 + m], rhs=wb[:, kc, :],
                                                   start=(kc == 0), stop=(kc == KC - 1))) for kc in range(KC)],
                       reads=[wkey] + HALL, writes=[("PS", pb)])
                if tb % 4 == 3:
                    cx.op('act', lambda e: e.copy(out=t[:, tb - 3:tb + 1, :], in_=PS[pb][:].rearrange("p (a b) -> p a b", a=4)),
                          reads=[("PS", pb)], writes=[tk])
                elif tb == 8:
                    cx.op('act', lambda e: e.copy(out=t[0:8, 8, :], in_=PS[pb][0:8, 0:128]), reads=[("PS", pb)], writes=[tk])
            col0 = kv * 512 + h * 128
            W = WIN[g]
            first = 2048 - W
            tb0 = max(0, (first - p * 1024) // 128)
            if p * 1024 + 1024 > first and tb0 < 8:
                row0 = p * 1024 + tb0 * 128 - first
                nb = 8 - tb0
                cx.dma('act', pw[g][j, row0:row0 + nb * 128, col0:col0 + 128].rearrange("(b q) e -> q b e", q=128),
                       t[:, tb0:8, :], reads=[tk])
            if p == 0:
                cx.dma('act', sw[g][j, W - 8:W, col0:col0 + 128], t[0:8, 8, :], reads=[tk])
            if kv == 1:
                cx.op('dve', lambda e: e.tensor_copy(out=VT[:, g, 0:ntb, :], in_=t[:, 0:ntb, :]), reads=[tk], writes=["VT"])

        for h in range(4):
            blocks = []
            for g in range(3):
                for qkv in range(3):
                    c0 = g * 1536 + qkv * 512 + h * 128
                    blocks.append((w_in_a[j, :, c0:c0 + 128], KC))
            ws = WS(blocks)
            for g in range(3):
                for qkv in range(3):
                    wb, wkey = ws.get(g * 3 + qkv)
                    if qkv < 2:
                        pset = nextset()
                        proj_fm(wb, wkey, KC, lambda kc, a, nn: H[:, kc, 2 + a:2 + a + nn], HALL, pset)
                        dstT = QT if qkv == 0 else KT
                        dk = "QTA" if qkv == 0 else "KTA"
                        for ti, (a, b) in enumerate(CT):
                            cx.op('act', lambda e: e.copy(out=dstT[:, g, a:b], in_=PS[pset[ti]][:, 0:350]), reads=[("PS", pset[ti])], writes=[dk])
                    if qkv >= 1:
                        proj_tm(wb, wkey, g, qkv - 1, h)
            if p == 0:
                for g in range(3):
                    cx.dma('sp', kth[j, h, g], KT[:, g, 0:1024], reads=["KTA"], writes=["KTH"])
                    cx.dma('sp', vh[j, h, g].rearrange("(b q) e -> q b e", q=128), VT[:, g, 0:8, :], reads=["VT"], writes=["VH"])
            else:
                for g in range(3):
                    ng = min(WIN[g], 1024)
                    cx.dma('sp', HK[:, hoff[g]:hoff[g] + ng], kth[j, h, g][:, 1024 - ng:1024], reads=["KTH"], writes=["HK"])
                    cx.dma('sp', HV[:, hboff[g]:hboff[g] + ng // 128, :],
                           vh[j, h, g][1024 - ng:1024, :].rearrange("(b q) e -> q b e", q=128), reads=["VH"], writes=["HV"])

            def mkblock(g, nk, kap, vap, c0, n, q0, keys):
                def sfn(S):
                    return [lambda e: e.matmul(S[0:nk, 0:n], lhsT=kap, rhs=QT[:, g, q0:q0 + n], start=True, stop=False),
                            lambda e: e.matmul(S[0:nk, 0:n], lhsT=IDB[0:nk, 0:nk], rhs=MSK[g][0:nk, c0:c0 + n], start=False, stop=True)]
                return (nk, sfn, keys + ["QTA", "MSK", "IDB"], [vap], keys)

            for (q0, q1) in ((0, 512), (512, 1024)):
                n = q1 - q0
                vq0 = 1024 * p + q0
                groups = []
                for g in range(3):
                    if p == 1:
                        ng = min(WIN[g], 1024)
                        for bb in range(ng // 128):
                            vk0 = 1024 - ng + bb * 128
                            if _blk_valid(g, vq0, n, vk0, 128):
                                groups.append(mkblock(g, 128, HK[:, hoff[g] + bb * 128:hoff[g] + (bb + 1) * 128], HV[:, hboff[g] + bb, :],
                                                      vq0 - vk0 + 384, n, q0, ["HK", "HV"]))
                    for tb in range(8):
                        vk0 = 1024 * p + tb * 128
                        if _blk_valid(g, vq0, n, vk0, 128):
                            groups.append(mkblock(g, 128, KT[:, g, tb * 128:(tb + 1) * 128], VT[:, g, tb, :],
                                                  vq0 - vk0 + 384, n, q0, ["KTA", "VT"]))
                attend(n, groups, [4, 5], scale, RZ, lambda ui: (OTA[:, h, q0:q1], ["OTA"]), "E")
            if p == 0:
                groups = []
                n = 8
                for g in range(3):
                    W = WIN[g]
                    nbk = W // 128
                    def prep(g=g, W=W, nbk=nbk):
                        cx.dma('sp', CS[:, 0:nbk, :], cw[g][j, :, h * 128:(h + 1) * 128].rearrange("(b q) e -> q b e", q=128), writes=["CS"])
                        for b4 in range(0, nbk, 4):
                            m4 = min(4, nbk - b4)
                            cx.mmg([(lambda e, k=k: e.transpose(out=PS[7][:, k * 128:(k + 1) * 128], in_=CS[:, b4 + k, :], identity=IDF[:]))
                                    for k in range(m4)], reads=["CS", "IDF"], writes=[("PS", 7)])
                            cx.op('act', lambda e: e.copy(out=SKT[:, b4 * 128:(b4 + m4) * 128], in_=PS[7][:, 0:m4 * 128]),
                                  reads=[("PS", 7)], writes=["SKT"])
                        cx.dma('sp', CS[:, 0:nbk, :], cw[g][j, :, 512 + h * 128:512 + (h + 1) * 128].rearrange("(b q) e -> q b e", q=128),
                               reads=[], writes=["CS"])
                        cx.op('pool', lambda e: e.tensor_copy(out=SVb[:, 0:nbk, :], in_=CS[:, 0:nbk, :]), reads=["CS"], writes=["SVB"])
                    first = True
                    for kb in range(nbk):
                        if _blk_valid(g, W, n, kb * 128, 128):
                            blk = mkblock(g, 128, SKT[:, kb * 128:(kb + 1) * 128], SVb[:, kb, :], W - kb * 128 + 384, n, SS0, ["SKT", "SVB"])
                            groups.append(blk + ((prep if first else None),))
                            first = False
                    blk = mkblock(g, 8, KT[:, g, SS0:SS1], VT[0:8, g, 8, :], 384, n, SS0, ["KTA", "VT"])
                    groups.append(blk + ((prep if first else None),))
                attend_lazy(n, groups, [4, 5], scale, lambda ui: (OTA[:, h, SS0:SS1], ["OTA"]))
        blocks = [(w_out_a[j, :, nn * 128:(nn + 1) * 128], 4) for nn in range(KC)]
        ws = WS(blocks)
        for nn in range(KC):
            wb, wkey = ws.get(nn)
            pset = nextset()
            proj_fm(wb, wkey, 4, lambda kc, a, m: OTA[:, kc, a:a + m], ["OTA"], pset)
            add_to_x(nn, pset)
        cx.barrier()

    def attend_lazy(n, groups, u_list, scale, out_fn):
        g2 = []
        nb = len(groups)
        zidx = u_list[-1]
        uidx = u_list[:-1]
        sbanks = [0, 1, 2]
        for bi, (nk, sfn, sreads, vaps, vreads, prep) in enumerate(groups):
            if prep is not None:
                prep()
            sbk = sbanks[bi % 3]
            S = PS[sbk]
            cx.mmg(sfn(S), reads=sreads, writes=[("PS", sbk)])
            ek = bi % 3
            cx.op('act', lambda e: e.activation(out=EB[ek][0:nk, 0:n], in_=S[0:nk, 0:n], func=AF.Exp, scale=scale),
                  reads=[("PS", sbk)], writes=[("E", ek)])
            for ui, u in enumerate(uidx):
                cx.mmg([lambda e: e.matmul(PS[u][:, 0:n], lhsT=vaps[ui], rhs=EB[ek][0:nk, 0:n], start=(bi == 0), stop=(bi == nb - 1))],
                       reads=[("E", ek)] + vreads, writes=[("PS", u)])
            cx.mmg([lambda e: e.matmul(PS[zidx][:, 0:n], lhsT=ONB[0:nk, :], rhs=EB[ek][0:nk, 0:n], start=(bi == 0), stop=(bi == nb - 1))],
                   reads=[("E", ek), "ONB"], writes=[("PS", zidx)])
        cx.op('dve', lambda e: e.reciprocal(out=RZ[:, 0:n], in_=PS[zidx][:, 0:n]), reads=[("PS", zidx)], writes=["RZ"])
        for ui, u in enumerate(uidx):
            oap, okeys = out_fn(ui)
            cx.op('dve', lambda e: e.tensor_tensor(out=oap, in0=PS[u][:, 0:n], in1=RZ[:, 0:n], op=ALU.mult),
                  reads=[("PS", u), "RZ"], writes=okeys)

    def final(p):
        rstd_cols(lambda c, a, b: X[:, c, a:b], CT, lambda a, b: RS[:, a:b], [("X", c) for c in range(KC)])
        YT = sb([128, KC, 128], F32)
        YS = [sb([128, D], F32) for _ in range(2)]
        XK = [("X", c) for c in range(KC)]
        units = [(tb * 128, 128) for tb in range(8)] + ([(SS0, 8)] if p == 0 else [])
        for ui, (c0, m) in enumerate(units):
            for c in range(KC):
                eng = 'dve'
                cx.op(eng, lambda e: e.scalar_tensor_tensor(out=YT[:, c, 0:m], in0=X[:, c, c0:c0 + m], scalar=G_[:, 16, c:c + 1],
                                                            in1=RS[:, c0:c0 + m], op0=ALU.mult, op1=ALU.mult),
                      reads=[("X", c), "RS", "G"], writes=["YT"])
            s = ui % 2
            for c4 in range(4):
                pb = 6 + c4 % 2
                cx.mmg([(lambda e, k=k: e.transpose(out=PS[pb][0:m, k * 128:(k + 1) * 128], in_=YT[:, c4 * 4 + k, 0:m], identity=IDF[:]))
                        for k in range(4)], reads=["YT", "IDF"], writes=[("PS", pb)])
                cx.op('act', lambda e: e.copy(out=YS[s][0:m, c4 * 512:(c4 + 1) * 512], in_=PS[pb][0:m, :]), reads=[("PS", pb)], writes=[("YS", s)])
            if m == 128:
                cx.dma('act', yp[p * 1024 + c0:p * 1024 + c0 + 128, :], YS[s][:], reads=[("YS", s)])
            else:
                cx.dma('act', ys, YS[s][0:8, :], reads=[("YS", s)])
        cx.barrier()

    cx.op('pool', lambda e: e.memset(EPSAP[:], EPS), writes=["EPS"])
    load_consts()
    for p in range(npass):
        mark = nc_mark(nc)
        load_x(p)
        nc_release(nc, mark)
        for layer in layers:
            if "mix" in phases:
                m2 = nc_mark(nc)
                if layer % 2 == 0:
                    mixer_a(p, layer)
                else:
                    mixer_b(p, layer)
                nc_release(nc, m2)
            if "mem" in phases:
                m2 = nc_mark(nc)
                mem_attn(p, layer)
                nc_release(nc, m2)
            if "ffn" in phases:
                m2 = nc_mark(nc)
                ffn(p, layer)
                nc_release(nc, m2)
        if "final" in phases:
            m2 = nc_mark(nc)
            final(p)
            nc_release(nc, m2)
    cx.barrier()
    return nc


_ALLOCS = []
STOPAT = [0]
DECL = set()
_CNT = [0]


def nc_mark(nc):
    return len(_ALLOCS)


def nc_release(nc, mark):
    while len(_ALLOCS) > mark:
        t = _ALLOCS.pop()
        t.__exit__(None, None, None)


def _patch_alloc(nc):
    def alloc(shape, dt):
        _CNT[0] += 1
        cm = nc.sbuf_tensor("t%d" % _CNT[0], list(shape), dt)
        t = cm.__enter__()
        _ALLOCS.append(cm)
        return t
    return alloc


def _fm(a, lead):
    return a


def _host_inputs(inp):
    f32 = np.float32
    bf = ml_dtypes.bfloat16
    gains = np.concatenate([inp["g_mix"], inp["g_mem_q"], inp["g_mem_kv"], inp["g_ffn"], inp["g_final"][None]], axis=0).astype(f32)
    gfm = np.ascontiguousarray(gains.reshape(17, 16, 128).transpose(2, 0, 1)).reshape(128, -1)
    cffn = np.ascontiguousarray(inp["conv_ffn"].astype(f32).reshape(4, 3, 88, 128).transpose(3, 0, 1, 2)).reshape(128, -1)
    cbw = np.ascontiguousarray(inp["conv_b"].astype(f32).reshape(2, 3, 16, 128).transpose(3, 0, 1, 2)).reshape(128, -1)
    common = {
        "gfm": gfm, "cffn": cffn, "cbw": cbw,
        "idf": np.eye(128, dtype=f32), "idb": np.eye(128, dtype=f32).astype(bf), "onb": np.ones((128, 128), f32).astype(bf),
    }
    for g in range(3):
        common["msk%d" % g] = np.ascontiguousarray(MASKNP[g][:, :MWG[g]]).astype(bf)
    for k in ("w_in_a", "w_out_a", "w_in_b", "w_out_b", "w_q_mem", "w_kv_mem", "w_o_mem", "w_up", "w_down"):
        common[k] = np.ascontiguousarray(inp[k], dtype=f32)
    maps = []
    for c in range(8):
        s, b = c % 4, c
        m = dict(common)
        m["xp"] = np.ascontiguousarray(inp["x_prompt"][s], dtype=f32)
        m["xs"] = np.ascontiguousarray(inp["x_sample"][b], dtype=f32)
        m["mem"] = np.ascontiguousarray(inp["mem_prompt"][s], dtype=f32)
        for g, nm in enumerate(("cache_win0_kv", "cache_win1_kv", "cache_win2_kv")):
            m["cw%d" % g] = np.ascontiguousarray(inp[nm][:, b], dtype=f32).reshape(2, WIN[g], 1024)
        m["cmem"] = np.ascontiguousarray(inp["cache_mem_kv"][:, b], dtype=f32).reshape(4, 256, 4096)
        sfc = inp["state_ffn_conv"][:, b].astype(f32)
        m["sfc"] = np.ascontiguousarray(sfc.reshape(4, 2, 88, 128).transpose(3, 0, 2, 1)).reshape(128, -1)
        scb = inp["state_conv_b"][:, b].astype(f32)
        m["scb"] = np.ascontiguousarray(scb.reshape(2, 2, 16, 128).transpose(3, 0, 2, 1)).reshape(128, -1)
        maps.append(m)
    return maps


_NC_CACHE = {}


def kernel(**inputs):
    inp = {k: np.asarray(v) for k, v in inputs.items()}
    if "nc" not in _NC_CACHE:
        _NC_CACHE["nc"] = build_program()
    nc = _NC_CACHE["nc"]
    maps = [{k: v for k, v in m.items() if k in DECL} for m in _host_inputs(inp)]
    res = run_bass_kernel_spmd(nc, maps, core_ids=list(range(8)))
    R = res.results
    f32 = np.float32
    y_prompt = np.stack([R[s]["yp"] for s in range(4)]).astype(f32)
    y_sample = np.stack([R[b]["ys"] for b in range(8)]).astype(f32)
    outs = [y_prompt, y_sample]
    for g in range(3):
        outs.append(np.stack([R[s]["pw%d" % g] for s in range(4)], axis=1).reshape(2, 4, WIN[g], 2, 4, 128).astype(f32))
    outs.append(np.stack([R[s]["pcb"] for s in range(4)], axis=1).astype(f32))
    outs.append(np.stack([R[s]["pfc"] for s in range(4)], axis=1).astype(f32))
    outs.append(np.stack([R[s]["pmk"] for s in range(4)], axis=1).reshape(4, 4, 256, 2, 4, 512).astype(f32))
    for g in range(3):
        outs.append(np.stack([R[b]["sw%d" % g] for b in range(8)], axis=1).reshape(2, 8, WIN[g], 2, 4, 128).astype(f32))
    outs.append(np.stack([R[b]["scbo"] for b in range(8)], axis=1).astype(f32))
    outs.append(np.stack([R[b]["sfco"] for b in range(8)], axis=1).astype(f32))
    return tuple(outs)
```

```python
import numpy as np
import ml_dtypes
import concourse.bass as bass
import concourse.mybir as mybir
from concourse.bass_utils import run_bass_kernel_spmd

F32 = mybir.dt.float32
BF16 = mybir.dt.bfloat16
AF = mybir.ActivationFunctionType
ALU = mybir.AluOpType

D = 2048
KC = 16
DFF = 5632
NCOL = 1050
CT = [(0, 350), (350, 700), (700, 1050)]
SP0, SS0, SS1 = 1024, 1026, 1034
WIN = (128, 512, 2048)
DIL = (1, 4, 16)
MW = 2448
MWG = (1152, 1536, 2448)
NEG = -30000.0
EPS = 1e-6
LIM = 4000


class Ctx:
    def __init__(self, nc):
        self.nc = nc
        self.engs = {'pe': nc.tensor, 'act': nc.scalar, 'dve': nc.vector, 'pool': nc.gpsimd, 'sp': nc.sync}
        self.count = {e: 0 for e in self.engs}
        self.sems = {e: [] for e in self.engs}
        self.waited = {e: {} for e in self.engs}
        self.semobj = {}
        self.lastw = {}
        self.readers = {}
        self.dsems = [nc.alloc_semaphore(f"dq{i}") for i in range(40)]
        self.dtot = [0] * len(self.dsems)
        self.dnext = 0
        for s in self.dsems:
            self.semobj[id(s)] = s

    def _tok_of(self, e):
        n = self.count[e]
        if n == 0:
            return None
        n -= 1
        return (id(self.sems[e][n // LIM]), n % LIM + 1)

    def _wait(self, e, tok):
        if tok is None:
            return
        sid, val = tok
        if self.waited[e].get(sid, 0) >= val:
            return
        self.engs[e].wait_ge(self.semobj[sid], val)
        self.waited[e][sid] = val

    def begin(self, e, reads, writes):
        for r in reads:
            self._wait(e, self.lastw.get(r))
        for w in writes:
            self._wait(e, self.lastw.get(w))
            rd = self.readers.get(w)
            if rd:
                for sid, val in rd.items():
                    self._wait(e, (sid, val))

    def _register(self, tok, reads, writes):
        sid, val = tok
        for r in reads:
            d = self.readers.setdefault(r, {})
            if d.get(sid, 0) < val:
                d[sid] = val
        for w in writes:
            self.lastw[w] = tok
            self.readers[w] = {}

    def end(self, e, inst, reads, writes):
        n = self.count[e]
        si = n // LIM
        if si >= len(self.sems[e]):
            s = self.nc.alloc_semaphore(f"e_{e}_{si}")
            self.sems[e].append(s)
            self.semobj[id(s)] = s
        sem = self.sems[e][si]
        val = n % LIM + 1
        inst.then_inc(sem, 1)
        self.count[e] = n + 1
        tok = (id(sem), val)
        if e == 'pe':
            self.waited[e][id(sem)] = val
        self._register(tok, reads, writes)

    def op(self, e, fn, reads=(), writes=()):
        self.begin(e, reads, writes)
        inst = fn(self.engs[e])
        self.end(e, inst, reads, writes)

    def mmg(self, fns, reads=(), writes=()):
        self.begin('pe', reads, writes)
        inst = None
        for f in fns:
            inst = f(self.nc.tensor)
        self.end('pe', inst, reads, writes)

    def dma(self, q, out, in_, reads=(), writes=()):
        self.begin(q, reads, writes)
        j = self.dnext
        self.dnext = (j + 1) % len(self.dsems)
        sem = self.dsems[j]
        if self.dtot[j] > 0:
            self._wait(q, (id(sem), self.dtot[j]))
        self.engs[q].dma_start(out=out, in_=in_).then_inc(sem, 16)
        self.dtot[j] += 16
        self._register((id(sem), self.dtot[j]), reads, writes)

    def barrier(self):
        for e in self.engs:
            for e2 in self.engs:
                if e2 != e:
                    self._wait(e, self._tok_of(e2))
            for j, s in enumerate(self.dsems):
                if self.dtot[j] > 0:
                    self._wait(e, (id(s), self.dtot[j]))


def _mask_np():
    m = np.full((3, 128, MW), NEG, np.float32)
    i = np.arange(128)[:, None]
    c = np.arange(MW)[None, :]
    delta = c - i - 384
    for g in range(3):
        d = DIL[g]
        valid = (delta >= 0) & (delta <= 128 * d) & (delta % d == 0)
        m[g][valid] = 0.0
    return m


MASKNP = _mask_np()


def _blk_valid(g, vq0, n, vk0, nk):
    c0 = vq0 - vk0 + 384
    if c0 < 0 or c0 + n > MW:
        return False
    return bool((MASKNP[g][:nk, c0:c0 + n] == 0.0).any())


def build_program(phases=("mix", "mem", "ffn", "final"), npass=2, layers=(0, 1, 2, 3)):
    nc = bass.Bass("TRN2", target_bir_lowering=False)
    cx = Ctx(nc)
    DECL.clear()

    def din(name, shape, dt=F32):
        DECL.add(name)
        return nc.dram_tensor(name, list(shape), dt, kind="ExternalInput").ap()

    class Lazy:
        def __init__(self, name, shape):
            self.name, self.shape, self.ap = name, shape, None

        def __getitem__(self, idx):
            if self.ap is None:
                self.ap = din(self.name, self.shape)
            return self.ap[idx]

    def dinl(name, shape):
        return Lazy(name, shape)

    def dout(name, shape):
        return nc.dram_tensor(name, list(shape), F32, kind="ExternalOutput").ap()

    xp = din("xp", [2048, D]); xs = din("xs", [8, D]); mem = din("mem", [256, D])
    cw = [din("cw0", [2, 128, 1024]), din("cw1", [2, 512, 1024]), din("cw2", [2, 2048, 1024])]
    cmem = din("cmem", [4, 256, 4096])
    gfm = din("gfm", [128, 17 * 16])
    cffn = din("cffn", [128, 4 * 3 * 88]); cb_in = din("cbw", [128, 2 * 3 * 16])
    sfc_in = din("sfc", [128, 4 * 88 * 2]); scb_in = din("scb", [128, 2 * 16 * 2])
    msk_in = [din("msk%d" % g, [128, MWG[g]], BF16) for g in range(3)]
    idf_in = din("idf", [128, 128]); idb_in = din("idb", [128, 128], BF16); onb_in = din("onb", [128, 128], BF16)
    w_in_a = dinl("w_in_a", [2, D, 4608]); w_out_a = dinl("w_out_a", [2, 512, D])
    w_in_b = dinl("w_in_b", [2, D, 3 * D]); w_out_b = dinl("w_out_b", [2, D, D])
    w_q_mem = dinl("w_q_mem", [4, D, D]); w_kv_mem = dinl("w_kv_mem", [4, D, 2 * D]); w_o_mem = dinl("w_o_mem", [4, D, D])
    w_up = dinl("w_up", [4, D, 2 * DFF]); w_down = dinl("w_down", [4, DFF, D])

    yp = dout("yp", [2048, D]); ys = dout("ys", [8, D])
    pw = [dout("pw0", [2, 128, 1024]), dout("pw1", [2, 512, 1024]), dout("pw2", [2, 2048, 1024])]
    pcb = dout("pcb", [2, 2, D]); pfc = dout("pfc", [4, 2, 2 * DFF]); pmk = dout("pmk", [4, 256, 4096])
    sw = [dout("sw0", [2, 128, 1024]), dout("sw1", [2, 512, 1024]), dout("sw2", [2, 2048, 1024])]
    scbo = dout("scbo", [2, 2, D]); sfco = dout("sfco", [4, 2, 2 * DFF])
    kth = nc.dram_tensor("kth", [2, 4, 3, 128, 1024], BF16).ap()
    vh = nc.dram_tensor("vh", [2, 4, 3, 1024, 128], BF16).ap()
    mks = nc.dram_tensor("mks", [4, 128, KC * 256], BF16).ap()
    mvs = nc.dram_tensor("mvs", [4, 128, 2 * D], BF16).ap()

    ph = _patch_alloc(nc)
    sb = ph
    X = sb([128, KC, NCOL], F32)
    H = sb([128, KC, NCOL + 2], BF16)
    WB = {"stg": [], "bfw": []}
    AT = {"eb": [], "rz": None}

    def alloc_wbuf(nstg, nbf):
        WB["stg"] = [sb([128, KC, 128], F32) for _ in range(nstg)]
        WB["bfw"] = [sb([128, KC, 128], BF16) for _ in range(nbf)]

    def alloc_attn():
        AT["eb"] = [sb([128, 512], BF16) for _ in range(3)]
        AT["rz"] = sb([128, 512], F32)
    RS = sb([128, NCOL], F32)
    SQ = [sb([128, 352], BF16) for _ in range(2)]
    G_ = sb([128, 17, 16], F32)
    CWF = sb([128, 4, 3, 88], F32)
    CWB = sb([128, 2, 3, 16], F32)
    SFC = sb([128, 4, 88, 2], F32)
    SCB = sb([128, 2, 16, 2], F32)
    IDF = sb([128, 128], F32); IDB = sb([128, 128], BF16); ONB = sb([128, 128], BF16)
    HS = sb([128, 12, KC, 2], BF16)
    CBF = sb([128, 2, 88], F32)
    CBO = sb([128, 2, 128], F32)
    PS = []
    for _ in range(8):
        _cm = nc.psum_tensor("ps%d" % len(PS), [128, 512], F32)
        PS.append(_cm.__enter__())
    SETS = [[0, 1, 2], [3, 4, 5]]
    HALL = [("H", c) for c in range(KC)]
    state = {"wload": 0, "wcast": 0, "pset": 0, "sq": 0}

    class WS:
        def __init__(self, blocks):
            self.blocks = blocks
            self.loaded = 0
            self.casted = 0
            self.slots = {}

        def _load(self, i):
            ap, kcn = self.blocks[i]
            s = state["wload"] % len(WB["stg"])
            state["wload"] += 1
            cx.dma('sp', WB["stg"][s][:, 0:kcn, :], ap.rearrange("(kc p) n -> p kc n", p=128), writes=[("STG", s)])
            self.slots[i] = [s, None]

        def _cast(self, i):
            ap, kcn = self.blocks[i]
            s = self.slots[i][0]
            b = state["wcast"] % len(WB["bfw"])
            state["wcast"] += 1
            cx.op('pool', lambda e: e.tensor_copy(out=WB["bfw"][b][:, 0:kcn, :], in_=WB["stg"][s][:, 0:kcn, :]),
                  reads=[("STG", s)], writes=[("BFW", b)])
            self.slots[i][1] = b

        def get(self, i):
            n = len(self.blocks)
            while self.casted <= min(i + 1, n - 1):
                while self.loaded <= min(self.casted + len(WB["stg"]) - 1, n - 1):
                    self._load(self.loaded)
                    self.loaded += 1
                self._cast(self.casted)
                self.casted += 1
            b = self.slots[i][1]
            return WB["bfw"][b], ("BFW", b)

    def nextset():
        s = SETS[state["pset"] % 2]
        state["pset"] += 1
        return s

    def proj_fm(wb, wkey, kcn, rhs_fn, rkeys, pset, tiles=CT, ext=0):
        for ti, (a, b) in enumerate(tiles):
            n = b - a + ext
            ps = PS[pset[ti]]
            cx.mmg([(lambda e, kc=kc: e.matmul(ps[:, 0:n], lhsT=wb[:, kc, :], rhs=rhs_fn(kc, a, n),
                                               start=(kc == 0), stop=(kc == kcn - 1))) for kc in range(kcn)],
                   reads=[wkey] + rkeys, writes=[("PS", pset[ti])])

    def add_to_x(nchunk, pset):
        for ti, (a, b) in enumerate(CT):
            ps = PS[pset[ti]]
            cx.op('dve', lambda e: e.tensor_tensor(out=X[:, nchunk, a:b], in0=ps[:, 0:b - a], in1=X[:, nchunk, a:b], op=ALU.add),
                  reads=[("PS", pset[ti]), ("X", nchunk)], writes=[("X", nchunk)])

    def load_consts():
        for t, src, key in ((G_, gfm, "G"), (CWF, cffn, "CWF"), (CWB, cb_in, "CWB"), (SFC, sfc_in, "SFC"),
                            (SCB, scb_in, "SCB"), (IDF, idf_in, "IDF"), (IDB, idb_in, "IDB"), (ONB, onb_in, "ONB")):
            shp = list(t.shape)
            dst = t[:]
            if len(shp) == 3:
                srcv = src.rearrange("p (a b) -> p a b", a=shp[1])
            elif len(shp) == 4:
                srcv = src.rearrange("p (a b c) -> p a b c", a=shp[1], b=shp[2])
            else:
                srcv = src
            cx.dma('sp', dst, srcv, writes=[key])
        cx.op('pool', lambda e: e.memset(H[:], 0.0), writes=HALL)
        cx.op('pool', lambda e: e.memset(CBO[:], 0.0), writes=["CBO"])

    CONSTK = ["G", "CWF", "CWB", "SFC", "SCB", "IDF", "IDB", "ONB"]

    def load_x(p):
        XS = [sb([128, D], F32) for _ in range(2)]
        XSS = sb([8, D], F32)
        cx.op('pool', lambda e: e.memset(X[:, :, 1024:NCOL], 0.0), writes=[("X", c) for c in range(KC)])
        bank = 0
        for tb in range(8):
            s = tb % 2
            cx.dma('sp', XS[s][:], xp[p * 1024 + tb * 128: p * 1024 + (tb + 1) * 128, :], writes=[("XS", s)])
            for c4 in range(4):
                pb = 6 + bank % 2
                bank += 1
                cx.mmg([(lambda e, k=k: e.transpose(out=PS[pb][:, k * 128:(k + 1) * 128],
                                                    in_=XS[s][:, (c4 * 4 + k) * 128:(c4 * 4 + k + 1) * 128], identity=IDF[:]))
                        for k in range(4)], reads=[("XS", s), "IDF"], writes=[("PS", pb)])
                eng = 'act' if c4 % 2 == 0 else 'dve'
                outap = X[:, c4 * 4:c4 * 4 + 4, tb * 128:(tb + 1) * 128]
                inap = PS[pb][:].rearrange("p (a b) -> p a b", a=4)
                if eng == 'act':
                    cx.op('act', lambda e: e.copy(out=outap, in_=inap), reads=[("PS", pb)],
                          writes=[("X", c4 * 4 + k) for k in range(4)])
                else:
                    cx.op('dve', lambda e: e.tensor_copy(out=outap, in_=inap), reads=[("PS", pb)],
                          writes=[("X", c4 * 4 + k) for k in range(4)])
        if p == 0:
            cx.dma('sp', XSS[:], xs, writes=["XSS"])
            pb = 6
            cx.mmg([(lambda e, c=c: e.transpose(out=PS[pb][:, c * 8:(c + 1) * 8], in_=XSS[0:8, c * 128:(c + 1) * 128],
                                                identity=IDF[0:8, 0:8])) for c in range(KC)],
                   reads=["XSS", "IDF"], writes=[("PS", pb)])
            cx.op('act', lambda e: e.copy(out=X[:, :, SS0:SS1], in_=PS[pb][:, 0:128].rearrange("p (a b) -> p a b", a=KC)),
                  reads=[("PS", pb)], writes=[("X", c) for c in range(KC)])
        cx.barrier()

    def rstd_cols(src_fn, tiles, rs_ap_fn, skeys):
        for (a, b) in tiles:
            n = b - a
            for c in range(KC):
                q = state["sq"] % 2
                state["sq"] += 1
                cx.op('act', lambda e: e.activation(out=SQ[q][:, 0:n], in_=src_fn(c, a, b), func=AF.Square),
                      reads=[skeys[c]], writes=[("SQ", q)])
                cx.mmg([lambda e: e.matmul(PS[7][:, 0:n], lhsT=ONB[:], rhs=SQ[q][:, 0:n], start=(c == 0), stop=(c == KC - 1))],
                       reads=[("SQ", q), "ONB"], writes=[("PS", 7)])
            cx.op('act', lambda e: e.activation(out=rs_ap_fn(a, b), in_=PS[7][:, 0:n], func=AF.Sqrt, scale=1.0 / D, bias=EPSAP[:]),
                  reads=[("PS", 7), "EPS"], writes=["RS"])
            cx.op('dve', lambda e: e.reciprocal(out=rs_ap_fn(a, b), in_=rs_ap_fn(a, b)), reads=["RS"], writes=["RS"])

    EPSAP = sb([128, 1], F32)

    def norm(p, gi, save):
        rstd_cols(lambda c, a, b: X[:, c, a:b], CT, lambda a, b: RS[:, a:b], [("X", c) for c in range(KC)])
        k = 0
        for (a, b) in CT:
            for c in range(KC):
                eng = 'dve'
                k += 1
                cx.op(eng, lambda e: e.scalar_tensor_tensor(out=H[:, c, 2 + a:2 + b], in0=X[:, c, a:b], scalar=G_[:, gi, c:c + 1],
                                                            in1=RS[:, a:b], op0=ALU.mult, op1=ALU.mult),
                      reads=[("X", c), "RS", "G"], writes=[("H", c)])
        if save is not None:
            if p == 0:
                cx.op('act', lambda e: e.copy(out=HS[:, save, :, :], in_=H[:, :, 2 + 1022:2 + 1024]), reads=HALL, writes=[("HS", save)])
            else:
                cx.op('act', lambda e: e.copy(out=H[:, :, 0:2], in_=HS[:, save, :, :]), reads=[("HS", save)], writes=HALL)

    def emit_conv_out(dst_p, dst_s, p, li, nchunk, key):
        dst = dst_s if p == 0 else dst_p
        for r in range(2):
            cx.mmg([lambda e: e.transpose(out=PS[6][0:nchunk, 0:128], in_=CBF[:, r, 0:nchunk], identity=IDF[:])],
                   reads=[key, "IDF"], writes=[("PS", 6)])
            cx.op('act', lambda e: e.copy(out=CBO[0:nchunk, r, :], in_=PS[6][0:nchunk, 0:128]), reads=[("PS", 6)], writes=["CBO"])
            cx.dma('act', dst[li, r, :].rearrange("(c q) -> c q", q=128), CBO[0:nchunk, r, :], reads=["CBO"])

    def ffn(p, layer):
        alloc_wbuf(3, 3)
        norm(p, 12 + layer, 6 + layer)
        Gq = [sb([128, 11, NCOL], BF16) for _ in range(2)]
        TA = sb([128, 3, 350], F32); SA = sb([128, 3, 350], BF16); TB = sb([128, 3, 350], F32)
        blocks = []
        for q in range(4):
            for j in range(11):
                a = 11 * q + j
                blocks.append((w_up[layer, :, a * 128:(a + 1) * 128], KC))
                blocks.append((w_up[layer, :, DFF + a * 128:DFF + (a + 1) * 128], KC))
            for n in range(KC):
                blocks.append((w_down[layer, q * 1408:(q + 1) * 1408, n * 128:(n + 1) * 128], 11))
        ws = WS(blocks)
        bi = 0
        lo = 334 if p == 0 else 324
        for q in range(4):
            G = Gq[q % 2]
            for j in range(11):
                for half in range(2):
                    chunk = 11 * q + j + 44 * half
                    wb, wkey = ws.get(bi); bi += 1
                    pset = nextset()
                    proj_fm(wb, wkey, KC, lambda kc, a, n: H[:, kc, a:a + n], HALL, pset, ext=2)
                    T = TA if half == 0 else TB
                    tk = "TA" if half == 0 else "TB"
                    for ti, (a, b) in enumerate(CT):
                        ps = PS[pset[ti]]
                        pk = ("PS", pset[ti])
                        if ti == 2:
                            if p == 0:
                                cx.op('act', lambda e: e.copy(out=ps[:, 326:328], in_=SFC[:, layer, chunk, :]), reads=["SFC", pk], writes=[pk])
                            cx.op('act', lambda e: e.copy(out=CBF[:, :, chunk], in_=ps[:, lo:lo + 2]), reads=[pk], writes=["CBF"])
                        cx.op('act', lambda e: e.mul(out=T[:, ti, :], in_=ps[:, 2:352], mul=CWF[:, layer, 2, chunk:chunk + 1]),
                              reads=[pk, "CWF"], writes=[(tk, ti)])
                        cx.op('dve', lambda e: e.scalar_tensor_tensor(out=T[:, ti, :], in0=ps[:, 1:351], scalar=CWF[:, layer, 1, chunk:chunk + 1],
                                                                      in1=T[:, ti, :], op0=ALU.mult, op1=ALU.add),
                              reads=[pk, "CWF", (tk, ti)], writes=[(tk, ti)])
                        cx.op('dve', lambda e: e.scalar_tensor_tensor(out=T[:, ti, :], in0=ps[:, 0:350], scalar=CWF[:, layer, 0, chunk:chunk + 1],
                                                                      in1=T[:, ti, :], op0=ALU.mult, op1=ALU.add),
                              reads=[pk, "CWF", (tk, ti)], writes=[(tk, ti)])
                        if half == 0:
                            cx.op('act', lambda e: e.activation(out=SA[:, ti, :], in_=T[:, ti, :], func=AF.Silu),
                                  reads=[(tk, ti)], writes=[("SA", ti)])
                        else:
                            cx.op('pool', lambda e: e.tensor_tensor(out=G[:, j, a:b], in0=SA[:, ti, :], in1=TB[:, ti, :], op=ALU.mult),
                                  reads=[("SA", ti), ("TB", ti)], writes=[("G", q % 2, j)])
            for n in range(KC):
                wb, wkey = ws.get(bi); bi += 1
                pset = nextset()
                proj_fm(wb, wkey, 11, lambda kc, a, nn: G[:, kc, a:a + nn], [("G", q % 2, j) for j in range(11)], pset)
                add_to_x(n, pset)
        emit_conv_out(pfc, sfco, p, layer, 88, "CBF")
        cx.barrier()

    def mixer_b(p, layer):
        j = layer // 2
        alloc_wbuf(4, 3)
        norm(p, layer, layer)
        Y = sb([128, KC, NCOL], BF16)
        T1 = sb([128, 3, 352], F32); T2 = sb([128, 3, 352], F32); T3 = sb([128, 3, 350], F32)
        blocks = []
        for c in range(KC):
            blocks.append((w_in_b[j, :, D + c * 128:D + (c + 1) * 128], KC))
            blocks.append((w_in_b[j, :, 2 * D + c * 128:2 * D + (c + 1) * 128], KC))
            blocks.append((w_in_b[j, :, c * 128:(c + 1) * 128], KC))
        for n in range(KC):
            blocks.append((w_out_b[j, :, n * 128:(n + 1) * 128], KC))
        ws = WS(blocks)
        bi = 0
        lo = 334 if p == 0 else 324
        rf = lambda kc, a, n: H[:, kc, a:a + n]
        for c in range(KC):
            wb, wkey = ws.get(bi); bi += 1
            pset = nextset()
            proj_fm(wb, wkey, KC, rf, HALL, pset, ext=2)
            for ti in range(3):
                cx.op('act', lambda e: e.copy(out=T1[:, ti, :], in_=PS[pset[ti]][:, 0:352]), reads=[("PS", pset[ti])], writes=[("T1", ti)])
            wb, wkey = ws.get(bi); bi += 1
            pset = nextset()
            proj_fm(wb, wkey, KC, rf, HALL, pset, ext=2)
            for ti in range(3):
                cx.op('dve', lambda e: e.tensor_tensor(out=T2[:, ti, :], in0=PS[pset[ti]][:, 0:352], in1=T1[:, ti, :], op=ALU.mult),
                      reads=[("PS", pset[ti]), ("T1", ti)], writes=[("T2", ti)])
                if ti == 2:
                    if p == 0:
                        cx.op('act', lambda e: e.copy(out=T2[:, 2, 326:328], in_=SCB[:, j, c, :]), reads=["SCB", ("T2", 2)], writes=[("T2", 2)])
                    cx.op('act', lambda e: e.copy(out=CBF[:, :, c], in_=T2[:, 2, lo:lo + 2]), reads=[("T2", 2)], writes=["CBF"])
                cx.op('act', lambda e: e.mul(out=T3[:, ti, :], in_=T2[:, ti, 2:352], mul=CWB[:, j, 2, c:c + 1]),
                      reads=[("T2", ti), "CWB"], writes=[("T3", ti)])
                cx.op('dve', lambda e: e.scalar_tensor_tensor(out=T3[:, ti, :], in0=T2[:, ti, 1:351], scalar=CWB[:, j, 1, c:c + 1],
                                                               in1=T3[:, ti, :], op0=ALU.mult, op1=ALU.add),
                      reads=[("T2", ti), "CWB", ("T3", ti)], writes=[("T3", ti)])
                cx.op('dve', lambda e: e.scalar_tensor_tensor(out=T3[:, ti, :], in0=T2[:, ti, 0:350], scalar=CWB[:, j, 0, c:c + 1],
                                                               in1=T3[:, ti, :], op0=ALU.mult, op1=ALU.add),
                      reads=[("T2", ti), "CWB", ("T3", ti)], writes=[("T3", ti)])
            wb, wkey = ws.get(bi); bi += 1
            pset = nextset()
            proj_fm(wb, wkey, KC, rf, HALL, pset, ext=2)
            for ti, (a, b) in enumerate(CT):
                cx.op('dve', lambda e: e.tensor_tensor(out=Y[:, c, a:b], in0=PS[pset[ti]][:, 2:352], in1=T3[:, ti, :], op=ALU.mult),
                      reads=[("PS", pset[ti]), ("T3", ti)], writes=[("Y", c)])
        for n in range(KC):
            wb, wkey = ws.get(bi); bi += 1
            pset = nextset()
            proj_fm(wb, wkey, KC, lambda kc, a, nn: Y[:, kc, a:a + nn], [("Y", c) for c in range(KC)], pset)
            add_to_x(n, pset)
        emit_conv_out(pcb, scbo, p, j, 16, "CBF")
        cx.barrier()

    def attend(n, s_groups, u_list, scale, rz, out_fn, ekeybase):
        nb = len(s_groups)
        zidx = u_list[-1]
        uidx = u_list[:-1]
        sbanks = [b for b in range(8) if b not in u_list][:3]
        for bi, (nk, sfn, sreads, vaps, vreads) in enumerate(s_groups):
            sbk = sbanks[bi % len(sbanks)]
            S = PS[sbk]
            cx.mmg(sfn(S), reads=sreads, writes=[("PS", sbk)])
            ek = bi % 3
            cx.op('act', lambda e: e.activation(out=AT["eb"][ek][0:nk, 0:n], in_=S[0:nk, 0:n], func=AF.Exp, scale=scale),
                  reads=[("PS", sbk)], writes=[("E", ek)])
            for ui, u in enumerate(uidx):
                cx.mmg([lambda e: e.matmul(PS[u][:, 0:n], lhsT=vaps[ui], rhs=AT["eb"][ek][0:nk, 0:n], start=(bi == 0), stop=(bi == nb - 1))],
                       reads=[("E", ek)] + vreads, writes=[("PS", u)])
            cx.mmg([lambda e: e.matmul(PS[zidx][:, 0:n], lhsT=ONB[0:nk, :], rhs=AT["eb"][ek][0:nk, 0:n], start=(bi == 0), stop=(bi == nb - 1))],
                   reads=[("E", ek), "ONB"], writes=[("PS", zidx)])
        cx.op('dve', lambda e: e.reciprocal(out=AT["rz"][:, 0:n], in_=PS[zidx][:, 0:n]), reads=[("PS", zidx)], writes=["RZ"])
        for ui, u in enumerate(uidx):
            oap, okeys = out_fn(ui)
            cx.op('dve', lambda e: e.tensor_tensor(out=oap, in0=PS[u][:, 0:n], in1=AT["rz"][:, 0:n], op=ALU.mult),
                  reads=[("PS", u), "RZ"], writes=okeys)


    def mem_attn(p, layer):
        alloc_wbuf(2, 3)
        alloc_attn()
        MK = sb([128, KC, 256], BF16)
        MV = sb([128, 2, D], BF16)
        _mk = nc_mark(nc)
        if p == 1:
            cx.dma('sp', MK[:], mks[layer].rearrange("p (a b) -> p a b", a=KC), reads=["MKS"], writes=["MK"])
            cx.dma('sp', MV[:], mvs[layer].rearrange("p (a b) -> p a b", a=2), reads=["MKS"], writes=["MV"])
        else:
            mem_kv(p, layer, MK, MV)
        cx.barrier()
        nc_release(nc, _mk)
        mem_rest(p, layer, MK, MV)

    def mem_kv(p, layer, MK, MV):
        MS = sb([128, D], F32); MEMN = sb([128, KC, 256], F32); HM = sb([128, KC, 256], BF16)
        RSM = sb([128, 256], F32); TMO = [sb([128, 2, 128], F32) for _ in range(2)]
        for tb in range(2):
            cx.dma('sp', MS[:], mem[tb * 128:(tb + 1) * 128, :], writes=["MS"])
            for c4 in range(4):
                pb = 6
                cx.mmg([(lambda e, k=k: e.transpose(out=PS[pb][:, k * 128:(k + 1) * 128],
                                                    in_=MS[:, (c4 * 4 + k) * 128:(c4 * 4 + k + 1) * 128], identity=IDF[:]))
                        for k in range(4)], reads=["MS", "IDF"], writes=[("PS", pb)])
                cx.op('act', lambda e: e.copy(out=MEMN[:, c4 * 4:c4 * 4 + 4, tb * 128:(tb + 1) * 128],
                                              in_=PS[pb][:].rearrange("p (a b) -> p a b", a=4)),
                      reads=[("PS", pb)], writes=["MEMN"])
        rstd_cols(lambda c, a, b: MEMN[:, c, a:b], [(0, 256)], lambda a, b: RSM[:, a:b], ["MEMN"] * KC)
        for c in range(KC):
            cx.op('dve', lambda e: e.scalar_tensor_tensor(out=HM[:, c, :], in0=MEMN[:, c, :], scalar=G_[:, 8 + layer, c:c + 1],
                                                          in1=RSM[:, :], op0=ALU.mult, op1=ALU.mult),
                  reads=["MEMN", "RS", "G"], writes=["HM"])
        blocks = [(w_kv_mem[layer, :, cb * 128:(cb + 1) * 128], KC) for cb in range(32)]
        ws = WS(blocks)
        pm = pmk[layer].rearrange("(tb q) n -> q tb n", q=128)
        for cb in range(32):
            wb, wkey = ws.get(cb)
            if cb < 16:
                cx.mmg([(lambda e, kc=kc: e.matmul(PS[5][:, 0:256], lhsT=wb[:, kc, :], rhs=HM[:, kc, :], start=(kc == 0), stop=(kc == KC - 1)))
                        for kc in range(KC)], reads=[wkey, "HM"], writes=[("PS", 5)])
                cx.op('act', lambda e: e.copy(out=MK[:, cb, :], in_=PS[5][:, 0:256]), reads=[("PS", 5)], writes=["MK"])
            pb = 6 + cb % 2
            for tb in range(2):
                cx.mmg([(lambda e, kc=kc: e.matmul(PS[pb][:, tb * 128:(tb + 1) * 128], lhsT=HM[:, kc, tb * 128:(tb + 1) * 128], rhs=wb[:, kc, :],
                                                   start=(kc == 0), stop=(kc == KC - 1))) for kc in range(KC)],
                       reads=[wkey, "HM"], writes=[("PS", pb)])
            t = TMO[cb % 2]
            if p == 0 or cb >= 16:
                cx.op('act', lambda e: e.copy(out=t[:], in_=PS[pb][:, 0:256].rearrange("p (a b) -> p a b", a=2)),
                      reads=[("PS", pb)], writes=[("TMO", cb % 2)])
            if p == 0:
                cx.dma('act', pm[:, :, cb * 128:(cb + 1) * 128], t[:], reads=[("TMO", cb % 2)])
            if cb >= 16:
                cx.op('dve', lambda e: e.tensor_copy(out=MV[:, :, (cb - 16) * 128:(cb - 15) * 128], in_=t[:]),
                      reads=[("TMO", cb % 2)], writes=["MV"])
        cx.dma('sp', mks[layer].rearrange("p (a b) -> p a b", a=KC), MK[:], reads=["MK"], writes=["MKS"])
        cx.dma('sp', mvs[layer].rearrange("p (a b) -> p a b", a=2), MV[:], reads=["MV"], writes=["MKS"])

    def mem_rest(p, layer, MK, MV):
        norm(p, 4 + layer, None)
        QT = sb([128, KC, NCOL], BF16)
        blocks = [(w_q_mem[layer, :, n * 128:(n + 1) * 128], KC) for n in range(KC)]
        blocks += [(w_o_mem[layer, :, n * 128:(n + 1) * 128], KC) for n in range(KC)]
        ws = WS(blocks)
        for n in range(KC):
            wb, wkey = ws.get(n)
            pset = nextset()
            proj_fm(wb, wkey, KC, lambda kc, a, nn: H[:, kc, 2 + a:2 + a + nn], HALL, pset)
            for ti, (a, b) in enumerate(CT):
                cx.op('act', lambda e: e.copy(out=QT[:, n, a:b], in_=PS[pset[ti]][:, 0:350]), reads=[("PS", pset[ti])], writes=[("QT", n)])
        if STOPAT[0] == 2:
            cx.barrier()
            return
        scale = 512.0 ** -0.5
        QALL = [("QT", n) for n in range(KC)]

        def run(h, q0, q1, kfn, vfn, kkeys):
            n = q1 - q0
            groups = []
            for kb in range(2):
                def sfn(S, kb=kb):
                    return [(lambda e, ec=ec: e.matmul(S[:, 0:n], lhsT=kfn(ec, kb), rhs=QT[:, h * 4 + ec, q0:q1], start=(ec == 0), stop=(ec == 3)))
                            for ec in range(4)]
                groups.append((128, sfn, kkeys + QALL, [vfn(ec, kb) for ec in range(4)], kkeys))
            attend(n, groups, [3, 4, 5, 6, 7], scale, None,
                   lambda ui: (H[:, h * 4 + ui, 2 + q0:2 + q1], [("H", h * 4 + ui)]), "E")

        for h in range(4):
            for (q0, q1) in ((0, 512), (512, 1024)):
                run(h, q0, q1, lambda ec, kb: MK[:, h * 4 + ec, kb * 128:(kb + 1) * 128],
                    lambda ec, kb: MV[:, kb, h * 512 + ec * 128:h * 512 + (ec + 1) * 128], ["MK", "MV"])
        if STOPAT[0] == 3:
            cx.barrier()
            return
        if p == 0:
            CK = sb([128, 2, 512], F32); CV = CK
            SK = sb([128, 4, 256], BF16); SV = sb([128, 2, 512], BF16)
            cm = cmem[layer].rearrange("(kb q) n -> q kb n", q=128)
            for h in range(4):
                cx.dma('sp', CK[:], cm[:, :, h * 512:(h + 1) * 512], writes=["CK"])
                for kb in range(2):
                    cx.mmg([(lambda e, ec=ec: e.transpose(out=PS[0][:, ec * 128:(ec + 1) * 128], in_=CK[:, kb, ec * 128:(ec + 1) * 128], identity=IDF[:]))
                            for ec in range(4)], reads=["CK", "IDF"], writes=[("PS", 0)])
                    cx.op('act', lambda e: e.copy(out=SK[:, :, kb * 128:(kb + 1) * 128], in_=PS[0][:].rearrange("p (a b) -> p a b", a=4)),
                          reads=[("PS", 0)], writes=["SK"])
                cx.dma('sp', CV[:], cm[:, :, D + h * 512:D + (h + 1) * 512], writes=["CK"])
                cx.op('pool', lambda e: e.tensor_copy(out=SV[:], in_=CV[:]), reads=["CK"], writes=["SV"])
                run(h, SS0, SS1, lambda ec, kb: SK[:, ec, kb * 128:(kb + 1) * 128],
                    lambda ec, kb: SV[:, kb, ec * 128:(ec + 1) * 128], ["SK", "SV"])
        cx.op('pool', lambda e: e.memset(H[:, :, 0:2], 0.0), writes=HALL)
        z0 = 2 + (SP0 if p == 0 else 1024)
        cx.op('pool', lambda e: e.memset(H[:, :, z0:z0 + 2] if p == 0 else H[:, :, z0:NCOL + 2], 0.0), writes=HALL)
        if p == 0:
            cx.op('pool', lambda e: e.memset(H[:, :, 2 + SS1:NCOL + 2], 0.0), writes=HALL)
        for n in range(KC):
            wb, wkey = ws.get(KC + n)
            pset = nextset()
            proj_fm(wb, wkey, KC, lambda kc, a, nn: H[:, kc, 2 + a:2 + a + nn], HALL, pset)
            add_to_x(n, pset)
        cx.barrier()

    def mixer_a(p, layer):
        j = layer // 2
        alloc_wbuf(2, 3)
        alloc_attn()
        norm(p, layer, None)
        MSK = [sb([128, MWG[g]], BF16) for g in range(3)]
        OTA = sb([128, 4, NCOL], BF16)
        QT = sb([128, 3, NCOL], BF16); KT = sb([128, 3, NCOL], BF16)
        VT = sb([128, 3, 9, 128], BF16)
        TM = [sb([128, 9, 128], F32) for _ in range(1)]
        if p == 1:
            HK = sb([128, 1664], BF16); HV = sb([128, 13, 128], BF16)
        else:
            CS = sb([128, 16, 128], F32); SKT = sb([128, 2048], BF16); SVb = sb([128, 16, 128], BF16)
        for g in range(3):
            cx.dma('sp', MSK[g][:], msk_in[g], writes=["MSK"])
        cx.op('pool', lambda e: e.memset(OTA[:], 0.0), writes=["OTA"])
        for t in TM:
            cx.op('pool', lambda e, t=t: e.memset(t[:], 0.0), writes=["TM0", "TM1"])
        if p == 0:
            for g in range(3):
                W = WIN[g]
                cx.dma('sp', sw[g][j, 0:W - 8, :], cw[g][j, 8:W, :])
        scale = 128.0 ** -0.5
        hoff = (0, 128, 640)
        hboff = (0, 1, 5)
        tmi = 0
        ntb = 9 if p == 0 else 8

        def proj_tm(wb, wkey, g, kv, h):
            nonlocal tmi
            t = TM[0]
            tk = "TM0"
            tmi += 1
            for tb in range(ntb):
                pb = 6 + (tb // 4) % 2
                m = 128 if tb < 8 else 8
                c0 = 2 + (tb * 128 if tb < 8 else SS0)
                cx.mmg([(lambda e, kc=kc: e.matmul(PS[pb][0:m, (tb % 4) * 128:(tb % 4 + 1) * 128], lhsT=H[:, kc, c0:c0 + m], rhs=wb[:, kc, :],
                                                   start=(kc == 0), stop=(kc == KC - 1))) for kc in range(KC)],
                       reads=[wkey] + HALL, writes=[("PS", pb)])
                if tb % 4 == 3:
                    cx.op('act', lambda e: e.copy(out=t[:, tb - 3:tb + 1, :], in_=PS[pb][:].rearrange("p (a b) -> p a b", a=4)),
                          reads=[("PS", pb)], writes=[tk])
                elif tb == 8:
                    cx.op('act', lambda e: e.copy(out=t[0:8, 8, :], in_=PS[pb][0:8, 0:128]), reads=[("PS", pb)], writes=[tk])
            col0 = kv * 512 + h * 128
            W = WIN[g]
            first = 2048 - W
            tb0 = max(0, (first - p * 1024) // 128)
            if p * 1024 + 1024 > first and tb0 < 8:
                row0 = p * 1024 + tb0 * 128 - first
                nb = 8 - tb0
                cx.dma('act', pw[g][j, row0:row0 + nb * 128, col0:col0 + 128].rearrange("(b q) e -> q b e", q=128),
                       t[:, tb0:8, :], reads=[tk])
            if p == 0:
                cx.dma('act', sw[g][j, W - 8:W, col0:col0 + 128], t[0:8, 8, :], reads=[tk])
            if kv == 1:
                cx.op('dve', lambda e: e.tensor_copy(out=VT[:, g, 0:ntb, :], in_=t[:, 0:ntb, :]), reads=[tk], writes=["VT"])

        for h in range(4):
            blocks = []
            for g in range(3):
                for qkv in range(3):
                    c0 = g * 1536 + qkv * 512 + h * 128
                    blocks.append((w_in_a[j, :, c0:c0 + 128], KC))
            ws = WS(blocks)
            for g in range(3):
                for qkv in range(3):
                    wb, wkey = ws.get(g * 3 + qkv)
                    if qkv < 2:
                        pset = nextset()
                        proj_fm(wb, wkey, KC, lambda kc, a, nn: H[:, kc, 2 + a:2 + a + nn], HALL, pset)
                        dstT = QT if qkv == 0 else KT
                        dk = "QTA" if qkv == 0 else "KTA"
                        for ti, (a, b) in enumerate(CT):
                            cx.op('act', lambda e: e.copy(out=dstT[:, g, a:b], in_=PS[pset[ti]][:, 0:350]), reads=[("PS", pset[ti])], writes=[dk])
                    if qkv >= 1:
                        proj_tm(wb, wkey, g, qkv - 1, h)
            if p == 0:
                for g in range(3):
                    cx.dma('sp', kth[j, h, g], KT[:, g, 0:1024], reads=["KTA"], writes=["KTH"])
                    cx.dma('sp', vh[j, h, g].rearrange("(b q) e -> q b e", q=128), VT[:, g, 0:8, :], reads=["VT"], writes=["VH"])
            else:
                for g in range(3):
                    ng = min(WIN[g], 1024)
                    cx.dma('sp', HK[:, hoff[g]:hoff[g] + ng], kth[j, h, g][:, 1024 - ng:1024], reads=["KTH"], writes=["HK"])
                    cx.dma('sp', HV[:, hboff[g]:hboff[g] + ng // 128, :],
                           vh[j, h, g][1024 - ng:1024, :].rearrange("(b q) e -> q b e", q=128), reads=["VH"], writes=["HV"])

            def mkblock(g, nk, kap, vap, c0, n, q0, keys):
                def sfn(S):
                    return [lambda e: e.matmul(S[0:nk, 0:n], lhsT=kap, rhs=QT[:, g, q0:q0 + n], start=True, stop=False),
                            lambda e: e.matmul(S[0:nk, 0:n], lhsT=IDB[0:nk, 0:nk], rhs=MSK[g][0:nk, c0:c0 + n], start=False, stop=True)]
                return (nk, sfn, keys + ["QTA", "MSK", "IDB"], [vap], keys)

            for (q0, q1) in ((0, 512), (512, 1024)):
                n = q1 - q0
                vq0 = 1024 * p + q0
                groups = []
                for g in range(3):
                    if p == 1:
                        ng = min(WIN[g], 1024)
                        for bb in range(ng // 128):
                            vk0 = 1024 - ng + bb * 128
                            if _blk_valid(g, vq0, n, vk0, 128):
                                groups.append(mkblock(g, 128, HK[:, hoff[g] + bb * 128:hoff[g] + (bb + 1) * 128], HV[:, hboff[g] + bb, :],
                                                      vq0 - vk0 + 384, n, q0, ["HK", "HV"]))
                    for tb in range(8):
                        vk0 = 1024 * p + tb * 128
                        if _blk_valid(g, vq0, n, vk0, 128):
                            groups.append(mkblock(g, 128, KT[:, g, tb * 128:(tb + 1) * 128], VT[:, g, tb, :],
                                                  vq0 - vk0 + 384, n, q0, ["KTA", "VT"]))
                attend(n, groups, [4, 5], scale, None, lambda ui: (OTA[:, h, q0:q1], ["OTA"]), "E")
            if p == 0:
                groups = []
                n = 8
                for g in range(3):
                    W = WIN[g]
                    nbk = W // 128
                    def prep(g=g, W=W, nbk=nbk):
                        cx.dma('sp', CS[:, 0:nbk, :], cw[g][j, :, h * 128:(h + 1) * 128].rearrange("(b q) e -> q b e", q=128), writes=["CS"])
                        for b4 in range(0, nbk, 4):
                            m4 = min(4, nbk - b4)
                            cx.mmg([(lambda e, k=k: e.transpose(out=PS[7][:, k * 128:(k + 1) * 128], in_=CS[:, b4 + k, :], identity=IDF[:]))
                                    for k in range(m4)], reads=["CS", "IDF"], writes=[("PS", 7)])
                            cx.op('act', lambda e: e.copy(out=SKT[:, b4 * 128:(b4 + m4) * 128], in_=PS[7][:, 0:m4 * 128]),
                                  reads=[("PS", 7)], writes=["SKT"])
                        cx.dma('sp', CS[:, 0:nbk, :], cw[g][j, :, 512 + h * 128:512 + (h + 1) * 128].rearrange("(b q) e -> q b e", q=128),
                               reads=[], writes=["CS"])
                        cx.op('pool', lambda e: e.tensor_copy(out=SVb[:, 0:nbk, :], in_=CS[:, 0:nbk, :]), reads=["CS"], writes=["SVB"])
                    first = True
                    for kb in range(nbk):
                        if _blk_valid(g, W, n, kb * 128, 128):
                            blk = mkblock(g, 128, SKT[:, kb * 128:(kb + 1) * 128], SVb[:, kb, :], W - kb * 128 + 384, n, SS0, ["SKT", "SVB"])
                            groups.append(blk + ((prep if first else None),))
                            first = False
                    blk = mkblock(g, 8, KT[:, g, SS0:SS1], VT[0:8, g, 8, :], 384, n, SS0, ["KTA", "VT"])
                    groups.append(blk + ((prep if first else None),))
                attend_lazy(n, groups, [4, 5], scale, lambda ui: (OTA[:, h, SS0:SS1], ["OTA"]))
        blocks = [(w_out_a[j, :, nn * 128:(nn + 1) * 128], 4) for nn in range(KC)]
        ws = WS(blocks)
        for nn in range(KC):
            wb, wkey = ws.get(nn)
            pset = nextset()
            proj_fm(wb, wkey, 4, lambda kc, a, m: OTA[:, kc, a:a + m], ["OTA"], pset)
            add_to_x(nn, pset)
        cx.barrier()

    def attend_lazy(n, groups, u_list, scale, out_fn):
        g2 = []
        nb = len(groups)
        zidx = u_list[-1]
        uidx = u_list[:-1]
        sbanks = [0, 1, 2]
        for bi, (nk, sfn, sreads, vaps, vreads, prep) in enumerate(groups):
            if prep is not None:
                prep()
            sbk = sbanks[bi % 3]
            S = PS[sbk]
            cx.mmg(sfn(S), reads=sreads, writes=[("PS", sbk)])
            ek = bi % 3
            cx.op('act', lambda e: e.activation(out=AT["eb"][ek][0:nk, 0:n], in_=S[0:nk, 0:n], func=AF.Exp, scale=scale),
                  reads=[("PS", sbk)], writes=[("E", ek)])
            for ui, u in enumerate(uidx):
                cx.mmg([lambda e: e.matmul(PS[u][:, 0:n], lhsT=vaps[ui], rhs=AT["eb"][ek][0:nk, 0:n], start=(bi == 0), stop=(bi == nb - 1))],
                       reads=[("E", ek)] + vreads, writes=[("PS", u)])
            cx.mmg([lambda e: e.matmul(PS[zidx][:, 0:n], lhsT=ONB[0:nk, :], rhs=AT["eb"][ek][0:nk, 0:n], start=(bi == 0), stop=(bi == nb - 1))],
                   reads=[("E", ek), "ONB"], writes=[("PS", zidx)])
        cx.op('dve', lambda e: e.reciprocal(out=AT["rz"][:, 0:n], in_=PS[zidx][:, 0:n]), reads=[("PS", zidx)], writes=["RZ"])
        for ui, u in enumerate(uidx):
            oap, okeys = out_fn(ui)
            cx.op('dve', lambda e: e.tensor_tensor(out=oap, in0=PS[u][:, 0:n], in1=AT["rz"][:, 0:n], op=ALU.mult),
                  reads=[("PS", u), "RZ"], writes=okeys)

    def final(p):
        rstd_cols(lambda c, a, b: X[:, c, a:b], CT, lambda a, b: RS[:, a:b], [("X", c) for c in range(KC)])
        YT = sb([128, KC, 128], F32)
        YS = [sb([128, D], F32) for _ in range(2)]
        XK = [("X", c) for c in range(KC)]
        units = [(tb * 128, 128) for tb in range(8)] + ([(SS0, 8)] if p == 0 else [])
        for ui, (c0, m) in enumerate(units):
            for c in range(KC):
                eng = 'dve'
                cx.op(eng, lambda e: e.scalar_tensor_tensor(out=YT[:, c, 0:m], in0=X[:, c, c0:c0 + m], scalar=G_[:, 16, c:c + 1],
                                                            in1=RS[:, c0:c0 + m], op0=ALU.mult, op1=ALU.mult),
                      reads=[("X", c), "RS", "G"], writes=["YT"])
            s = ui % 2
            for c4 in range(4):
                pb = 6 + c4 % 2
                cx.mmg([(lambda e, k=k: e.transpose(out=PS[pb][0:m, k * 128:(k + 1) * 128], in_=YT[:, c4 * 4 + k, 0:m], identity=IDF[:]))
                        for k in range(4)], reads=["YT", "IDF"], writes=[("PS", pb)])
                cx.op('act', lambda e: e.copy(out=YS[s][0:m, c4 * 512:(c4 + 1) * 512], in_=PS[pb][0:m, :]), reads=[("PS", pb)], writes=[("YS", s)])
            if m == 128:
                cx.dma('act', yp[p * 1024 + c0:p * 1024 + c0 + 128, :], YS[s][:], reads=[("YS", s)])
            else:
                cx.dma('act', ys, YS[s][0:8, :], reads=[("YS", s)])
        cx.barrier()

    cx.op('pool', lambda e: e.memset(EPSAP[:], EPS), writes=["EPS"])
    load_consts()
    for p in range(npass):
        mark = nc_mark(nc)
        load_x(p)
        nc_release(nc, mark)
        for layer in layers:
            if "mix" in phases:
                m2 = nc_mark(nc)
                if layer % 2 == 0:
                    mixer_a(p, layer)
                else:
                    mixer_b(p, layer)
                nc_release(nc, m2)
            if "mem" in phases:
                m2 = nc_mark(nc)
                mem_attn(p, layer)
                nc_release(nc, m2)
            if "ffn" in phases:
                m2 = nc_mark(nc)
                ffn(p, layer)
                nc_release(nc, m2)
        if "final" in phases:
            m2 = nc_mark(nc)
            final(p)
            nc_release(nc, m2)
    cx.barrier()
    return nc


_ALLOCS = []
STOPAT = [0]
DECL = set()
_CNT = [0]


def nc_mark(nc):
    return len(_ALLOCS)


def nc_release(nc, mark):
    while len(_ALLOCS) > mark:
        t = _ALLOCS.pop()
        t.__exit__(None, None, None)


def _patch_alloc(nc):
    def alloc(shape, dt):
        _CNT[0] += 1
        cm = nc.sbuf_tensor("t%d" % _CNT[0], list(shape), dt)
        t = cm.__enter__()
        _ALLOCS.append(cm)
        return t
    return alloc


def _fm(a, lead):
    return a


def _host_inputs(inp):
    f32 = np.float32
    bf = ml_dtypes.bfloat16
    gains = np.concatenate([inp["g_mix"], inp["g_mem_q"], inp["g_mem_kv"], inp["g_ffn"], inp["g_final"][None]], axis=0).astype(f32)
    gfm = np.ascontiguousarray(gains.reshape(17, 16, 128).transpose(2, 0, 1)).reshape(128, -1)
    cffn = np.ascontiguousarray(inp["conv_ffn"].astype(f32).reshape(4, 3, 88, 128).transpose(3, 0, 1, 2)).reshape(128, -1)
    cbw = np.ascontiguousarray(inp["conv_b"].astype(f32).reshape(2, 3, 16, 128).transpose(3, 0, 1, 2)).reshape(128, -1)
    common = {
        "gfm": gfm, "cffn": cffn, "cbw": cbw,
        "idf": np.eye(128, dtype=f32), "idb": np.eye(128, dtype=f32).astype(bf), "onb": np.ones((128, 128), f32).astype(bf),
    }
    for g in range(3):
        common["msk%d" % g] = np.ascontiguousarray(MASKNP[g][:, :MWG[g]]).astype(bf)
    for k in ("w_in_a", "w_out_a", "w_in_b", "w_out_b", "w_q_mem", "w_kv_mem", "w_o_mem", "w_up", "w_down"):
        common[k] = np.ascontiguousarray(inp[k], dtype=f32)
    maps = []
    for c in range(8):
        s, b = c % 4, c
        m = dict(common)
        m["xp"] = np.ascontiguousarray(inp["x_prompt"][s], dtype=f32)
        m["xs"] = np.ascontiguousarray(inp["x_sample"][b], dtype=f32)
        m["mem"] = np.ascontiguousarray(inp["mem_prompt"][s], dtype=f32)
        for g, nm in enumerate(("cache_win0_kv", "cache_win1_kv", "cache_win2_kv")):
            m["cw%d" % g] = np.ascontiguousarray(inp[nm][:, b], dtype=f32).reshape(2, WIN[g], 1024)
        m["cmem"] = np.ascontiguousarray(inp["cache_mem_kv"][:, b], dtype=f32).reshape(4, 256, 4096)
        sfc = inp["state_ffn_conv"][:, b].astype(f32)
        m["sfc"] = np.ascontiguousarray(sfc.reshape(4, 2, 88, 128).transpose(3, 0, 2, 1)).reshape(128, -1)
        scb = inp["state_conv_b"][:, b].astype(f32)
        m["scb"] = np.ascontiguousarray(scb.reshape(2, 2, 16, 128).transpose(3, 0, 2, 1)).reshape(128, -1)
        maps.append(m)
    return maps


_NC_CACHE = {}


def kernel(**inputs):
    inp = {k: np.asarray(v) for k, v in inputs.items()}
    if "nc" not in _NC_CACHE:
        _NC_CACHE["nc"] = build_program()
    nc = _NC_CACHE["nc"]
    maps = [{k: v for k, v in m.items() if k in DECL} for m in _host_inputs(inp)]
    res = run_bass_kernel_spmd(nc, maps, core_ids=list(range(8)))
    R = res.results
    f32 = np.float32
    y_prompt = np.stack([R[s]["yp"] for s in range(4)]).astype(f32)
    y_sample = np.stack([R[b]["ys"] for b in range(8)]).astype(f32)
    outs = [y_prompt, y_sample]
    for g in range(3):
        outs.append(np.stack([R[s]["pw%d" % g] for s in range(4)], axis=1).reshape(2, 4, WIN[g], 2, 4, 128).astype(f32))
    outs.append(np.stack([R[s]["pcb"] for s in range(4)], axis=1).astype(f32))
    outs.append(np.stack([R[s]["pfc"] for s in range(4)], axis=1).astype(f32))
    outs.append(np.stack([R[s]["pmk"] for s in range(4)], axis=1).reshape(4, 4, 256, 2, 4, 512).astype(f32))
    for g in range(3):
        outs.append(np.stack([R[b]["sw%d" % g] for b in range(8)], axis=1).reshape(2, 8, WIN[g], 2, 4, 128).astype(f32))
    outs.append(np.stack([R[b]["scbo"] for b in range(8)], axis=1).astype(f32))
    outs.append(np.stack([R[b]["sfco"] for b in range(8)], axis=1).astype(f32))
    return tuple(outs)
```

```python
import numpy as np
import ml_dtypes
import concourse.bass as bass
import concourse.mybir as mybir
from concourse.bass_utils import run_bass_kernel_spmd

F32 = mybir.dt.float32
BF16 = mybir.dt.bfloat16
AF = mybir.ActivationFunctionType
ALU = mybir.AluOpType

D = 2048
KC = 16
DFF = 5632
NCOL = 1050
CT = [(0, 350), (350, 700), (700, 1050)]
SP0, SS0, SS1 = 1024, 1026, 1034
WIN = (128, 512, 2048)
DIL = (1, 4, 16)
MW = 2448
MWG = (1152, 1536, 2448)
NEG = -30000.0
EPS = 1e-6
LIM = 4000


class Ctx:
    def __init__(self, nc):
        self.nc = nc
        self.engs = {'pe': nc.tensor, 'act': nc.scalar, 'dve': nc.vector, 'pool': nc.gpsimd, 'sp': nc.sync}
        self.count = {e: 0 for e in self.engs}
        self.sems = {e: [] for e in self.engs}
        self.waited = {e: {} for e in self.engs}
        self.semobj = {}
        self.lastw = {}
        self.readers = {}
        self.dsems = [nc.alloc_semaphore(f"dq{i}") for i in range(40)]
        self.dtot = [0] * len(self.dsems)
        self.dnext = 0
        for s in self.dsems:
            self.semobj[id(s)] = s

    def _tok_of(self, e):
        n = self.count[e]
        if n == 0:
            return None
        n -= 1
        return (id(self.sems[e][n // LIM]), n % LIM + 1)

    def _wait(self, e, tok):
        if tok is None:
            return
        sid, val = tok
        if self.waited[e].get(sid, 0) >= val:
            return
        self.engs[e].wait_ge(self.semobj[sid], val)
        self.waited[e][sid] = val

    def begin(self, e, reads, writes):
        for r in reads:
            self._wait(e, self.lastw.get(r))
        for w in writes:
            self._wait(e, self.lastw.get(w))
            rd = self.readers.get(w)
            if rd:
                for sid, val in rd.items():
                    self._wait(e, (sid, val))

    def _register(self, tok, reads, writes):
        sid, val = tok
        for r in reads:
            d = self.readers.setdefault(r, {})
            if d.get(sid, 0) < val:
                d[sid] = val
        for w in writes:
            self.lastw[w] = tok
            self.readers[w] = {}

    def end(self, e, inst, reads, writes):
        n = self.count[e]
        si = n // LIM
        if si >= len(self.sems[e]):
            s = self.nc.alloc_semaphore(f"e_{e}_{si}")
            self.sems[e].append(s)
            self.semobj[id(s)] = s
        sem = self.sems[e][si]
        val = n % LIM + 1
        inst.then_inc(sem, 1)
        self.count[e] = n + 1
        tok = (id(sem), val)
        if e == 'pe':
            self.waited[e][id(sem)] = val
        self._register(tok, reads, writes)

    def op(self, e, fn, reads=(), writes=()):
        self.begin(e, reads, writes)
        inst = fn(self.engs[e])
        self.end(e, inst, reads, writes)

    def mmg(self, fns, reads=(), writes=()):
        self.begin('pe', reads, writes)
        inst = None
        for f in fns:
            inst = f(self.nc.tensor)
        self.end('pe', inst, reads, writes)

    def dma(self, q, out, in_, reads=(), writes=()):
        self.begin(q, reads, writes)
        j = self.dnext
        self.dnext = (j + 1) % len(self.dsems)
        sem = self.dsems[j]
        if self.dtot[j] > 0:
            self._wait(q, (id(sem), self.dtot[j]))
        self.engs[q].dma_start(out=out, in_=in_).then_inc(sem, 16)
        self.dtot[j] += 16
        self._register((id(sem), self.dtot[j]), reads, writes)

    def barrier(self):
        for e in self.engs:
            for e2 in self.engs:
                if e2 != e:
                    self._wait(e, self._tok_of(e2))
            for j, s in enumerate(self.dsems):
                if self.dtot[j] > 0:
                    self._wait(e, (id(s), self.dtot[j]))


def _mask_np():
    m = np.full((3, 128, MW), NEG, np.float32)
    i = np.arange(128)[:, None]
    c = np.arange(MW)[None, :]
    delta = c - i - 384
    for g in range(3):
        d = DIL[g]
        valid = (delta >= 0) & (delta <= 128 * d) & (delta % d == 0)
        m[g][valid] = 0.0
    return m


MASKNP = _mask_np()


def _blk_valid(g, vq0, n, vk0, nk):
    c0 = vq0 - vk0 + 384
    if c0 < 0 or c0 + n > MW:
        return False
    return bool((MASKNP[g][:nk, c0:c0 + n] == 0.0).any())


def build_program(phases=("mix", "mem", "ffn", "final"), npass=2, layers=(0, 1, 2, 3)):
    nc = bass.Bass("TRN2", target_bir_lowering=False)
    cx = Ctx(nc)
    DECL.clear()

    def din(name, shape, dt=F32):
        DECL.add(name)
        return nc.dram_tensor(name, list(shape), dt, kind="ExternalInput").ap()

    class Lazy:
        def __init__(self, name, shape):
            self.name, self.shape, self.ap = name, shape, None

        def __getitem__(self, idx):
            if self.ap is None:
                self.ap = din(self.name, self.shape)
            return self.ap[idx]

    def dinl(name, shape):
        return Lazy(name, shape)

    def dout(name, shape):
        return nc.dram_tensor(name, list(shape), F32, kind="ExternalOutput").ap()

    xp = din("xp", [2048, D]); xs = din("xs", [8, D]); mem = din("mem", [256, D])
    cw = [din("cw0", [2, 128, 1024]), din("cw1", [2, 512, 1024]), din("cw2", [2, 2048, 1024])]
    cmem = din("cmem", [4, 256, 4096])
    gfm = din("gfm", [128, 17 * 16])
    cffn = din("cffn", [128, 4 * 3 * 88]); cb_in = din("cbw", [128, 2 * 3 * 16])
    sfc_in = din("sfc", [128, 4 * 88 * 2]); scb_in = din("scb", [128, 2 * 16 * 2])
    msk_in = [din("msk%d" % g, [128, MWG[g]], BF16) for g in range(3)]
    idf_in = din("idf", [128, 128]); idb_in = din("idb", [128, 128], BF16); onb_in = din("onb", [128, 128], BF16)
    w_in_a = dinl("w_in_a", [2, D, 4608]); w_out_a = dinl("w_out_a", [2, 512, D])
    w_in_b = dinl("w_in_b", [2, D, 3 * D]); w_out_b = dinl("w_out_b", [2, D, D])
    w_q_mem = dinl("w_q_mem", [4, D, D]); w_kv_mem = dinl("w_kv_mem", [4, D, 2 * D]); w_o_mem = dinl("w_o_mem", [4, D, D])
    w_up = dinl("w_up", [4, D, 2 * DFF]); w_down = dinl("w_down", [4, DFF, D])

    yp = dout("yp", [2048, D]); ys = dout("ys", [8, D])
    pw = [dout("pw0", [2, 128, 1024]), dout("pw1", [2, 512, 1024]), dout("pw2", [2, 2048, 1024])]
    pcb = dout("pcb", [2, 2, D]); pfc = dout("pfc", [4, 2, 2 * DFF]); pmk = dout("pmk", [4, 256, 4096])
    sw = [dout("sw0", [2, 128, 1024]), dout("sw1", [2, 512, 1024]), dout("sw2", [2, 2048, 1024])]
    scbo = dout("scbo", [2, 2, D]); sfco = dout("sfco", [4, 2, 2 * DFF])
    kth = nc.dram_tensor("kth", [2, 4, 3, 128, 1024], BF16).ap()
    vh = nc.dram_tensor("vh", [2, 4, 3, 1024, 128], BF16).ap()
    mks = nc.dram_tensor("mks", [4, 128, KC * 256], BF16).ap()
    mvs = nc.dram_tensor("mvs", [4, 128, 2 * D], BF16).ap()

    ph = _patch_alloc(nc)
    sb = ph
    X = sb([128, KC, NCOL], F32)
    H = sb([128, KC, NCOL + 2], BF16)
    WB = {"stg": [], "bfw": []}
    AT = {"eb": [], "rz": None}

    def alloc_wbuf(nstg, nbf):
        WB["stg"] = [sb([128, KC, 128], F32) for _ in range(nstg)]
        WB["bfw"] = [sb([128, KC, 128], BF16) for _ in range(nbf)]

    def alloc_attn():
        AT["eb"] = [sb([128, 512], BF16) for _ in range(3)]
        AT["rz"] = sb([128, 512], F32)
    RS = sb([128, NCOL], F32)
    SQ = [sb([128, 352], BF16) for _ in range(2)]
    G_ = sb([128, 17, 16], F32)
    CWF = sb([128, 4, 3, 88], F32)
    CWB = sb([128, 2, 3, 16], F32)
    SFC = sb([128, 4, 88, 2], F32)
    SCB = sb([128, 2, 16, 2], F32)
    IDF = sb([128, 128], F32); IDB = sb([128, 128], BF16); ONB = sb([128, 128], BF16)
    HS = sb([128, 12, KC, 2], BF16)
    CBF = sb([128, 2, 88], F32)
    CBO = sb([128, 2, 128], F32)
    PS = []
    for _ in range(8):
        _cm = nc.psum_tensor("ps%d" % len(PS), [128, 512], F32)
        PS.append(_cm.__enter__())
    SETS = [[0, 1, 2], [3, 4, 5]]
    HALL = [("H", c) for c in range(KC)]
    state = {"wload": 0, "wcast": 0, "pset": 0, "sq": 0}

    class WS:
        def __init__(self, blocks):
            self.blocks = blocks
            self.loaded = 0
            self.casted = 0
            self.slots = {}

        def _load(self, i):
            ap, kcn = self.blocks[i]
            s = state["wload"] % len(WB["stg"])
            state["wload"] += 1
            cx.dma('sp', WB["stg"][s][:, 0:kcn, :], ap.rearrange("(kc p) n -> p kc n", p=128), writes=[("STG", s)])
            self.slots[i] = [s, None]

        def _cast(self, i):
            ap, kcn = self.blocks[i]
            s = self.slots[i][0]
            b = state["wcast"] % len(WB["bfw"])
            state["wcast"] += 1
            cx.op('act', lambda e: e.copy(out=WB["bfw"][b][:, 0:kcn, :], in_=WB["stg"][s][:, 0:kcn, :]),
                  reads=[("STG", s)], writes=[("BFW", b)])
            self.slots[i][1] = b

        def get(self, i):
            n = len(self.blocks)
            while self.casted <= min(i + 1, n - 1):
                while self.loaded <= min(self.casted + len(WB["stg"]) - 1, n - 1):
                    self._load(self.loaded)
                    self.loaded += 1
                self._cast(self.casted)
                self.casted += 1
            b = self.slots[i][1]
            return WB["bfw"][b], ("BFW", b)

    def nextset():
        s = SETS[state["pset"] % 2]
        state["pset"] += 1
        return s

    def proj_fm(wb, wkey, kcn, rhs_fn, rkeys, pset, tiles=CT, ext=0):
        for ti, (a, b) in enumerate(tiles):
            n = b - a + ext
            ps = PS[pset[ti]]
            cx.mmg([(lambda e, kc=kc: e.matmul(ps[:, 0:n], lhsT=wb[:, kc, :], rhs=rhs_fn(kc, a, n),
                                               start=(kc == 0), stop=(kc == kcn - 1))) for kc in range(kcn)],
                   reads=[wkey] + rkeys, writes=[("PS", pset[ti])])

    def add_to_x(nchunk, pset):
        for ti, (a, b) in enumerate(CT):
            ps = PS[pset[ti]]
            cx.op('dve', lambda e: e.tensor_tensor(out=X[:, nchunk, a:b], in0=ps[:, 0:b - a], in1=X[:, nchunk, a:b], op=ALU.add),
                  reads=[("PS", pset[ti]), ("X", nchunk)], writes=[("X", nchunk)])

    def load_consts():
        for t, src, key in ((G_, gfm, "G"), (CWF, cffn, "CWF"), (CWB, cb_in, "CWB"), (SFC, sfc_in, "SFC"),
                            (SCB, scb_in, "SCB"), (IDF, idf_in, "IDF"), (IDB, idb_in, "IDB"), (ONB, onb_in, "ONB")):
            shp = list(t.shape)
            dst = t[:]
            if len(shp) == 3:
                srcv = src.rearrange("p (a b) -> p a b", a=shp[1])
            elif len(shp) == 4:
                srcv = src.rearrange("p (a b c) -> p a b c", a=shp[1], b=shp[2])
            else:
                srcv = src
            cx.dma('sp', dst, srcv, writes=[key])
        cx.op('pool', lambda e: e.memset(H[:], 0.0), writes=HALL)
        cx.op('pool', lambda e: e.memset(CBO[:], 0.0), writes=["CBO"])

    CONSTK = ["G", "CWF", "CWB", "SFC", "SCB", "IDF", "IDB", "ONB"]

    def load_x(p):
        XS = [sb([128, D], F32) for _ in range(2)]
        XSS = sb([8, D], F32)
        cx.op('pool', lambda e: e.memset(X[:, :, 1024:NCOL], 0.0), writes=[("X", c) for c in range(KC)])
        bank = 0
        for tb in range(8):
            s = tb % 2
            cx.dma('sp', XS[s][:], xp[p * 1024 + tb * 128: p * 1024 + (tb + 1) * 128, :], writes=[("XS", s)])
            for c4 in range(4):
                pb = 6 + bank % 2
                bank += 1
                cx.mmg([(lambda e, k=k: e.transpose(out=PS[pb][:, k * 128:(k + 1) * 128],
                                                    in_=XS[s][:, (c4 * 4 + k) * 128:(c4 * 4 + k + 1) * 128], identity=IDF[:]))
                        for k in range(4)], reads=[("XS", s), "IDF"], writes=[("PS", pb)])
                eng = 'act' if c4 % 2 == 0 else 'dve'
                outap = X[:, c4 * 4:c4 * 4 + 4, tb * 128:(tb + 1) * 128]
                inap = PS[pb][:].rearrange("p (a b) -> p a b", a=4)
                if eng == 'act':
                    cx.op('act', lambda e: e.copy(out=outap, in_=inap), reads=[("PS", pb)],
                          writes=[("X", c4 * 4 + k) for k in range(4)])
                else:
                    cx.op('dve', lambda e: e.tensor_copy(out=outap, in_=inap), reads=[("PS", pb)],
                          writes=[("X", c4 * 4 + k) for k in range(4)])
        if p == 0:
            cx.dma('sp', XSS[:], xs, writes=["XSS"])
            pb = 6
            cx.mmg([(lambda e, c=c: e.transpose(out=PS[pb][:, c * 8:(c + 1) * 8], in_=XSS[0:8, c * 128:(c + 1) * 128],
                                                identity=IDF[0:8, 0:8])) for c in range(KC)],
                   reads=["XSS", "IDF"], writes=[("PS", pb)])
            cx.op('act', lambda e: e.copy(out=X[:, :, SS0:SS1], in_=PS[pb][:, 0:128].rearrange("p (a b) -> p a b", a=KC)),
                  reads=[("PS", pb)], writes=[("X", c) for c in range(KC)])
        cx.barrier()

    def rstd_cols(src_fn, tiles, rs_ap_fn, skeys):
        for (a, b) in tiles:
            n = b - a
            for c in range(KC):
                q = state["sq"] % 2
                state["sq"] += 1
                cx.op('act', lambda e: e.activation(out=SQ[q][:, 0:n], in_=src_fn(c, a, b), func=AF.Square),
                      reads=[skeys[c]], writes=[("SQ", q)])
                cx.mmg([lambda e: e.matmul(PS[7][:, 0:n], lhsT=ONB[:], rhs=SQ[q][:, 0:n], start=(c == 0), stop=(c == KC - 1))],
                       reads=[("SQ", q), "ONB"], writes=[("PS", 7)])
            cx.op('act', lambda e: e.activation(out=rs_ap_fn(a, b), in_=PS[7][:, 0:n], func=AF.Sqrt, scale=1.0 / D, bias=EPSAP[:]),
                  reads=[("PS", 7), "EPS"], writes=["RS"])
            cx.op('dve', lambda e: e.reciprocal(out=rs_ap_fn(a, b), in_=rs_ap_fn(a, b)), reads=["RS"], writes=["RS"])

    EPSAP = sb([128, 1], F32)

    def norm(p, gi, save):
        rstd_cols(lambda c, a, b: X[:, c, a:b], CT, lambda a, b: RS[:, a:b], [("X", c) for c in range(KC)])
        k = 0
        for (a, b) in CT:
            for c in range(KC):
                eng = 'dve'
                k += 1
                cx.op(eng, lambda e: e.scalar_tensor_tensor(out=H[:, c, 2 + a:2 + b], in0=X[:, c, a:b], scalar=G_[:, gi, c:c + 1],
                                                            in1=RS[:, a:b], op0=ALU.mult, op1=ALU.mult),
                      reads=[("X", c), "RS", "G"], writes=[("H", c)])
        if save is not None:
            if p == 0:
                cx.op('act', lambda e: e.copy(out=HS[:, save, :, :], in_=H[:, :, 2 + 1022:2 + 1024]), reads=HALL, writes=[("HS", save)])
            else:
                cx.op('act', lambda e: e.copy(out=H[:, :, 0:2], in_=HS[:, save, :, :]), reads=[("HS", save)], writes=HALL)

    def emit_conv_out(dst_p, dst_s, p, li, nchunk, key):
        dst = dst_s if p == 0 else dst_p
        for r in range(2):
            cx.mmg([lambda e: e.transpose(out=PS[6][0:nchunk, 0:128], in_=CBF[:, r, 0:nchunk], identity=IDF[:])],
                   reads=[key, "IDF"], writes=[("PS", 6)])
            cx.op('act', lambda e: e.copy(out=CBO[0:nchunk, r, :], in_=PS[6][0:nchunk, 0:128]), reads=[("PS", 6)], writes=["CBO"])
            cx.dma('act', dst[li, r, :].rearrange("(c q) -> c q", q=128), CBO[0:nchunk, r, :], reads=["CBO"])

    def ffn(p, layer):
        alloc_wbuf(3, 3)
        norm(p, 12 + layer, 6 + layer)
        Gq = [sb([128, 11, NCOL], BF16) for _ in range(2)]
        TA = sb([128, 3, 350], F32); SA = sb([128, 3, 350], BF16); TB = sb([128, 3, 350], F32)
        blocks = []
        for q in range(4):
            for j in range(11):
                a = 11 * q + j
                blocks.append((w_up[layer, :, a * 128:(a + 1) * 128], KC))
                blocks.append((w_up[layer, :, DFF + a * 128:DFF + (a + 1) * 128], KC))
            for n in range(KC):
                blocks.append((w_down[layer, q * 1408:(q + 1) * 1408, n * 128:(n + 1) * 128], 11))
        ws = WS(blocks)
        bi = 0
        lo = 334 if p == 0 else 324
        for q in range(4):
            G = Gq[q % 2]
            for j in range(11):
                for half in range(2):
                    chunk = 11 * q + j + 44 * half
                    wb, wkey = ws.get(bi); bi += 1
                    pset = nextset()
                    proj_fm(wb, wkey, KC, lambda kc, a, n: H[:, kc, a:a + n], HALL, pset, ext=2)
                    T = TA if half == 0 else TB
                    tk = "TA" if half == 0 else "TB"
                    for ti, (a, b) in enumerate(CT):
                        ps = PS[pset[ti]]
                        pk = ("PS", pset[ti])
                        if ti == 2:
                            if p == 0:
                                cx.op('act', lambda e: e.copy(out=ps[:, 326:328], in_=SFC[:, layer, chunk, :]), reads=["SFC", pk], writes=[pk])
                            cx.op('act', lambda e: e.copy(out=CBF[:, :, chunk], in_=ps[:, lo:lo + 2]), reads=[pk], writes=["CBF"])
                        cx.op('act', lambda e: e.mul(out=T[:, ti, :], in_=ps[:, 2:352], mul=CWF[:, layer, 2, chunk:chunk + 1]),
                              reads=[pk, "CWF"], writes=[(tk, ti)])
                        cx.op('dve', lambda e: e.scalar_tensor_tensor(out=T[:, ti, :], in0=ps[:, 1:351], scalar=CWF[:, layer, 1, chunk:chunk + 1],
                                                                      in1=T[:, ti, :], op0=ALU.mult, op1=ALU.add),
                              reads=[pk, "CWF", (tk, ti)], writes=[(tk, ti)])
                        cx.op('dve', lambda e: e.scalar_tensor_tensor(out=T[:, ti, :], in0=ps[:, 0:350], scalar=CWF[:, layer, 0, chunk:chunk + 1],
                                                                      in1=T[:, ti, :], op0=ALU.mult, op1=ALU.add),
                              reads=[pk, "CWF", (tk, ti)], writes=[(tk, ti)])
                        if half == 0:
                            cx.op('act', lambda e: e.activation(out=SA[:, ti, :], in_=T[:, ti, :], func=AF.Silu),
                                  reads=[(tk, ti)], writes=[("SA", ti)])
                        else:
                            cx.op('pool', lambda e: e.tensor_tensor(out=G[:, j, a:b], in0=SA[:, ti, :], in1=TB[:, ti, :], op=ALU.mult),
                                  reads=[("SA", ti), ("TB", ti)], writes=[("G", q % 2, j)])
            for n in range(KC):
                wb, wkey = ws.get(bi); bi += 1
                pset = nextset()
                proj_fm(wb, wkey, 11, lambda kc, a, nn: G[:, kc, a:a + nn], [("G", q % 2, j) for j in range(11)], pset)
                add_to_x(n, pset)
        emit_conv_out(pfc, sfco, p, layer, 88, "CBF")
        cx.barrier()

    def mixer_b(p, layer):
        j = layer // 2
        alloc_wbuf(4, 3)
        norm(p, layer, layer)
        Y = sb([128, KC, NCOL], BF16)
        T1 = sb([128, 3, 352], F32); T2 = sb([128, 3, 352], F32); T3 = sb([128, 3, 350], F32)
        blocks = []
        for c in range(KC):
            blocks.append((w_in_b[j, :, D + c * 128:D + (c + 1) * 128], KC))
            blocks.append((w_in_b[j, :, 2 * D + c * 128:2 * D + (c + 1) * 128], KC))
            blocks.append((w_in_b[j, :, c * 128:(c + 1) * 128], KC))
        for n in range(KC):
            blocks.append((w_out_b[j, :, n * 128:(n + 1) * 128], KC))
        ws = WS(blocks)
        bi = 0
        lo = 334 if p == 0 else 324
        rf = lambda kc, a, n: H[:, kc, a:a + n]
        for c in range(KC):
            wb, wkey = ws.get(bi); bi += 1
            pset = nextset()
            proj_fm(wb, wkey, KC, rf, HALL, pset, ext=2)
            for ti in range(3):
                cx.op('act', lambda e: e.copy(out=T1[:, ti, :], in_=PS[pset[ti]][:, 0:352]), reads=[("PS", pset[ti])], writes=[("T1", ti)])
            wb, wkey = ws.get(bi); bi += 1
            pset = nextset()
            proj_fm(wb, wkey, KC, rf, HALL, pset, ext=2)
            for ti in range(3):
                cx.op('dve', lambda e: e.tensor_tensor(out=T2[:, ti, :], in0=PS[pset[ti]][:, 0:352], in1=T1[:, ti, :], op=ALU.mult),
                      reads=[("PS", pset[ti]), ("T1", ti)], writes=[("T2", ti)])
                if ti == 2:
                    if p == 0:
                        cx.op('act', lambda e: e.copy(out=T2[:, 2, 326:328], in_=SCB[:, j, c, :]), reads=["SCB", ("T2", 2)], writes=[("T2", 2)])
                    cx.op('act', lambda e: e.copy(out=CBF[:, :, c], in_=T2[:, 2, lo:lo + 2]), reads=[("T2", 2)], writes=["CBF"])
                cx.op('act', lambda e: e.mul(out=T3[:, ti, :], in_=T2[:, ti, 2:352], mul=CWB[:, j, 2, c:c + 1]),
                      reads=[("T2", ti), "CWB"], writes=[("T3", ti)])
                cx.op('dve', lambda e: e.scalar_tensor_tensor(out=T3[:, ti, :], in0=T2[:, ti, 1:351], scalar=CWB[:, j, 1, c:c + 1],
                                                               in1=T3[:, ti, :], op0=ALU.mult, op1=ALU.add),
                      reads=[("T2", ti), "CWB", ("T3", ti)], writes=[("T3", ti)])
                cx.op('dve', lambda e: e.scalar_tensor_tensor(out=T3[:, ti, :], in0=T2[:, ti, 0:350], scalar=CWB[:, j, 0, c:c + 1],
                                                               in1=T3[:, ti, :], op0=ALU.mult, op1=ALU.add),
                      reads=[("T2", ti), "CWB", ("T3", ti)], writes=[("T3", ti)])
            wb, wkey = ws.get(bi); bi += 1
            pset = nextset()
            proj_fm(wb, wkey, KC, rf, HALL, pset, ext=2)
            for ti, (a, b) in enumerate(CT):
                cx.op('dve', lambda e: e.tensor_tensor(out=Y[:, c, a:b], in0=PS[pset[ti]][:, 2:352], in1=T3[:, ti, :], op=ALU.mult),
                      reads=[("PS", pset[ti]), ("T3", ti)], writes=[("Y", c)])
        for n in range(KC):
            wb, wkey = ws.get(bi); bi += 1
            pset = nextset()
            proj_fm(wb, wkey, KC, lambda kc, a, nn: Y[:, kc, a:a + nn], [("Y", c) for c in range(KC)], pset)
            add_to_x(n, pset)
        emit_conv_out(pcb, scbo, p, j, 16, "CBF")
        cx.barrier()

    def attend(n, s_groups, u_list, scale, rz, out_fn, ekeybase):
        nb = len(s_groups)
        zidx = u_list[-1]
        uidx = u_list[:-1]
        sbanks = [b for b in range(8) if b not in u_list][:3]
        for bi, (nk, sfn, sreads, vaps, vreads) in enumerate(s_groups):
            sbk = sbanks[bi % len(sbanks)]
            S = PS[sbk]
            cx.mmg(sfn(S), reads=sreads, writes=[("PS", sbk)])
            ek = bi % 3
            cx.op('act', lambda e: e.activation(out=AT["eb"][ek][0:nk, 0:n], in_=S[0:nk, 0:n], func=AF.Exp, scale=scale),
                  reads=[("PS", sbk)], writes=[("E", ek)])
            for ui, u in enumerate(uidx):
                cx.mmg([lambda e: e.matmul(PS[u][:, 0:n], lhsT=vaps[ui], rhs=AT["eb"][ek][0:nk, 0:n], start=(bi == 0), stop=(bi == nb - 1))],
                       reads=[("E", ek)] + vreads, writes=[("PS", u)])
            cx.mmg([lambda e: e.matmul(PS[zidx][:, 0:n], lhsT=ONB[0:nk, :], rhs=AT["eb"][ek][0:nk, 0:n], start=(bi == 0), stop=(bi == nb - 1))],
                   reads=[("E", ek), "ONB"], writes=[("PS", zidx)])
        cx.op('dve', lambda e: e.reciprocal(out=AT["rz"][:, 0:n], in_=PS[zidx][:, 0:n]), reads=[("PS", zidx)], writes=["RZ"])
        for ui, u in enumerate(uidx):
            oap, okeys = out_fn(ui)
            cx.op('dve', lambda e: e.tensor_tensor(out=oap, in0=PS[u][:, 0:n], in1=AT["rz"][:, 0:n], op=ALU.mult),
                  reads=[("PS", u), "RZ"], writes=okeys)


    def mem_attn(p, layer):
        alloc_wbuf(2, 3)
        alloc_attn()
        MK = sb([128, KC, 256], BF16)
        MV = sb([128, 2, D], BF16)
        _mk = nc_mark(nc)
        if p == 1:
            cx.dma('sp', MK[:], mks[layer].rearrange("p (a b) -> p a b", a=KC), reads=["MKS"], writes=["MK"])
            cx.dma('sp', MV[:], mvs[layer].rearrange("p (a b) -> p a b", a=2), reads=["MKS"], writes=["MV"])
        else:
            mem_kv(p, layer, MK, MV)
        cx.barrier()
        nc_release(nc, _mk)
        mem_rest(p, layer, MK, MV)

    def mem_kv(p, layer, MK, MV):
        MS = sb([128, D], F32); MEMN = sb([128, KC, 256], F32); HM = sb([128, KC, 256], BF16)
        RSM = sb([128, 256], F32); TMO = [sb([128, 2, 128], F32) for _ in range(2)]
        for tb in range(2):
            cx.dma('sp', MS[:], mem[tb * 128:(tb + 1) * 128, :], writes=["MS"])
            for c4 in range(4):
                pb = 6
                cx.mmg([(lambda e, k=k: e.transpose(out=PS[pb][:, k * 128:(k + 1) * 128],
                                                    in_=MS[:, (c4 * 4 + k) * 128:(c4 * 4 + k + 1) * 128], identity=IDF[:]))
                        for k in range(4)], reads=["MS", "IDF"], writes=[("PS", pb)])
                cx.op('act', lambda e: e.copy(out=MEMN[:, c4 * 4:c4 * 4 + 4, tb * 128:(tb + 1) * 128],
                                              in_=PS[pb][:].rearrange("p (a b) -> p a b", a=4)),
                      reads=[("PS", pb)], writes=["MEMN"])
        rstd_cols(lambda c, a, b: MEMN[:, c, a:b], [(0, 256)], lambda a, b: RSM[:, a:b], ["MEMN"] * KC)
        for c in range(KC):
            cx.op('dve', lambda e: e.scalar_tensor_tensor(out=HM[:, c, :], in0=MEMN[:, c, :], scalar=G_[:, 8 + layer, c:c + 1],
                                                          in1=RSM[:, :], op0=ALU.mult, op1=ALU.mult),
                  reads=["MEMN", "RS", "G"], writes=["HM"])
        blocks = [(w_kv_mem[layer, :, cb * 128:(cb + 1) * 128], KC) for cb in range(32)]
        ws = WS(blocks)
        pm = pmk[layer].rearrange("(tb q) n -> q tb n", q=128)
        for cb in range(32):
            wb, wkey = ws.get(cb)
            if cb < 16:
                cx.mmg([(lambda e, kc=kc: e.matmul(PS[5][:, 0:256], lhsT=wb[:, kc, :], rhs=HM[:, kc, :], start=(kc == 0), stop=(kc == KC - 1)))
                        for kc in range(KC)], reads=[wkey, "HM"], writes=[("PS", 5)])
                cx.op('act', lambda e: e.copy(out=MK[:, cb, :], in_=PS[5][:, 0:256]), reads=[("PS", 5)], writes=["MK"])
            pb = 6 + cb % 2
            for tb in range(2):
                cx.mmg([(lambda e, kc=kc: e.matmul(PS[pb][:, tb * 128:(tb + 1) * 128], lhsT=HM[:, kc, tb * 128:(tb + 1) * 128], rhs=wb[:, kc, :],
                                                   start=(kc == 0), stop=(kc == KC - 1))) for kc in range(KC)],
                       reads=[wkey, "HM"], writes=[("PS", pb)])
            t = TMO[cb % 2]
            if p == 0 or cb >= 16:
                cx.op('act', lambda e: e.copy(out=t[:], in_=PS[pb][:, 0:256].rearrange("p (a b) -> p a b", a=2)),
                      reads=[("PS", pb)], writes=[("TMO", cb % 2)])
            if p == 0:
                cx.dma('act', pm[:, :, cb * 128:(cb + 1) * 128], t[:], reads=[("TMO", cb % 2)])
            if cb >= 16:
                cx.op('dve', lambda e: e.tensor_copy(out=MV[:, :, (cb - 16) * 128:(cb - 15) * 128], in_=t[:]),
                      reads=[("TMO", cb % 2)], writes=["MV"])
        cx.dma('sp', mks[layer].rearrange("p (a b) -> p a b", a=KC), MK[:], reads=["MK"], writes=["MKS"])
        cx.dma('sp', mvs[layer].rearrange("p (a b) -> p a b", a=2), MV[:], reads=["MV"], writes=["MKS"])

    def mem_rest(p, layer, MK, MV):
        norm(p, 4 + layer, None)
        QT = sb([128, KC, NCOL], BF16)
        blocks = [(w_q_mem[layer, :, n * 128:(n + 1) * 128], KC) for n in range(KC)]
        blocks += [(w_o_mem[layer, :, n * 128:(n + 1) * 128], KC) for n in range(KC)]
        ws = WS(blocks)
        for n in range(KC):
            wb, wkey = ws.get(n)
            pset = nextset()
            proj_fm(wb, wkey, KC, lambda kc, a, nn: H[:, kc, 2 + a:2 + a + nn], HALL, pset)
            for ti, (a, b) in enumerate(CT):
                cx.op('act', lambda e: e.copy(out=QT[:, n, a:b], in_=PS[pset[ti]][:, 0:350]), reads=[("PS", pset[ti])], writes=[("QT", n)])
        if STOPAT[0] == 2:
            cx.barrier()
            return
        scale = 512.0 ** -0.5
        QALL = [("QT", n) for n in range(KC)]

        def run(h, q0, q1, kfn, vfn, kkeys):
            n = q1 - q0
            groups = []
            for kb in range(2):
                def sfn(S, kb=kb):
                    return [(lambda e, ec=ec: e.matmul(S[:, 0:n], lhsT=kfn(ec, kb), rhs=QT[:, h * 4 + ec, q0:q1], start=(ec == 0), stop=(ec == 3)))
                            for ec in range(4)]
                groups.append((128, sfn, kkeys + QALL, [vfn(ec, kb) for ec in range(4)], kkeys))
            attend(n, groups, [3, 4, 5, 6, 7], scale, None,
                   lambda ui: (H[:, h * 4 + ui, 2 + q0:2 + q1], [("H", h * 4 + ui)]), "E")

        for h in range(4):
            for (q0, q1) in ((0, 512), (512, 1024)):
                run(h, q0, q1, lambda ec, kb: MK[:, h * 4 + ec, kb * 128:(kb + 1) * 128],
                    lambda ec, kb: MV[:, kb, h * 512 + ec * 128:h * 512 + (ec + 1) * 128], ["MK", "MV"])
        if STOPAT[0] == 3:
            cx.barrier()
            return
        if p == 0:
            CK = sb([128, 2, 512], F32); CV = CK
            SK = sb([128, 4, 256], BF16); SV = sb([128, 2, 512], BF16)
            cm = cmem[layer].rearrange("(kb q) n -> q kb n", q=128)
            for h in range(4):
                cx.dma('sp', CK[:], cm[:, :, h * 512:(h + 1) * 512], writes=["CK"])
                for kb in range(2):
                    cx.mmg([(lambda e, ec=ec: e.transpose(out=PS[0][:, ec * 128:(ec + 1) * 128], in_=CK[:, kb, ec * 128:(ec + 1) * 128], identity=IDF[:]))
                            for ec in range(4)], reads=["CK", "IDF"], writes=[("PS", 0)])
                    cx.op('act', lambda e: e.copy(out=SK[:, :, kb * 128:(kb + 1) * 128], in_=PS[0][:].rearrange("p (a b) -> p a b", a=4)),
                          reads=[("PS", 0)], writes=["SK"])
                cx.dma('sp', CV[:], cm[:, :, D + h * 512:D + (h + 1) * 512], writes=["CK"])
                cx.op('pool', lambda e: e.tensor_copy(out=SV[:], in_=CV[:]), reads=["CK"], writes=["SV"])
                run(h, SS0, SS1, lambda ec, kb: SK[:, ec, kb * 128:(kb + 1) * 128],
                    lambda ec, kb: SV[:, kb, ec * 128:(ec + 1) * 128], ["SK", "SV"])
        cx.op('pool', lambda e: e.memset(H[:, :, 0:2], 0.0), writes=HALL)
        z0 = 2 + (SP0 if p == 0 else 1024)
        cx.op('pool', lambda e: e.memset(H[:, :, z0:z0 + 2] if p == 0 else H[:, :, z0:NCOL + 2], 0.0), writes=HALL)
        if p == 0:
            cx.op('pool', lambda e: e.memset(H[:, :, 2 + SS1:NCOL + 2], 0.0), writes=HALL)
        for n in range(KC):
            wb, wkey = ws.get(KC + n)
            pset = nextset()
            proj_fm(wb, wkey, KC, lambda kc, a, nn: H[:, kc, 2 + a:2 + a + nn], HALL, pset)
            add_to_x(n, pset)
        cx.barrier()

    def mixer_a(p, layer):
        j = layer // 2
        alloc_wbuf(2, 3)
        alloc_attn()
        norm(p, layer, None)
        MSK = [sb([128, MWG[g]], BF16) for g in range(3)]
        OTA = sb([128, 4, NCOL], BF16)
        QT = sb([128, 3, NCOL], BF16); KT = sb([128, 3, NCOL], BF16)
        VT = sb([128, 3, 9, 128], BF16)
        TM = [sb([128, 9, 128], F32) for _ in range(1)]
        if p == 1:
            HK = sb([128, 1664], BF16); HV = sb([128, 13, 128], BF16)
        else:
            CS = sb([128, 16, 128], F32); SKT = sb([128, 2048], BF16); SVb = sb([128, 16, 128], BF16)
        for g in range(3):
            cx.dma('sp', MSK[g][:], msk_in[g], writes=["MSK"])
        cx.op('pool', lambda e: e.memset(OTA[:], 0.0), writes=["OTA"])
        for t in TM:
            cx.op('pool', lambda e, t=t: e.memset(t[:], 0.0), writes=["TM0", "TM1"])
        if p == 0:
            for g in range(3):
                W = WIN[g]
                cx.dma('sp', sw[g][j, 0:W - 8, :], cw[g][j, 8:W, :])
        scale = 128.0 ** -0.5
        hoff = (0, 128, 640)
        hboff = (0, 1, 5)
        tmi = 0
        ntb = 9 if p == 0 else 8

        def proj_tm(wb, wkey, g, kv, h):
            nonlocal tmi
            t = TM[0]
            tk = "TM0"
            tmi += 1
            for tb in range(ntb):
                pb = 6 + (tb // 4) % 2
                m = 128 if tb < 8 else 8
                c0 = 2 + (tb * 128 if tb < 8 else SS0)
                cx.mmg([(lambda e, kc=kc: e.matmul(PS[pb][0:m, (tb % 4) * 128:(tb % 4 + 1) * 128], lhsT=H[:, kc, c0:c0 + m], rhs=wb[:, kc, :],
                                                   start=(kc == 0), stop=(kc == KC - 1))) for kc in range(KC)],
                       reads=[wkey] + HALL, writes=[("PS", pb)])
                if tb % 4 == 3:
                    cx.op('act', lambda e: e.copy(out=t[:, tb - 3:tb + 1, :], in_=PS[pb][:].rearrange("p (a b) -> p a b", a=4)),
                          reads=[("PS", pb)], writes=[tk])
                elif tb == 8:
                    cx.op('act', lambda e: e.copy(out=t[0:8, 8, :], in_=PS[pb][0:8, 0:128]), reads=[("PS", pb)], writes=[tk])
            col0 = kv * 512 + h * 128
            W = WIN[g]
            first = 2048 - W
            tb0 = max(0, (first - p * 1024) // 128)
            if p * 1024 + 1024 > first and tb0 < 8:
                row0 = p * 1024 + tb0 * 128 - first
                nb = 8 - tb0
                cx.dma('act', pw[g][j, row0:row0 + nb * 128, col0:col0 + 128].rearrange("(b q) e -> q b e", q=128),
                       t[:, tb0:8, :], reads=[tk])
            if p == 0:
                cx.dma('act', sw[g][j, W - 8:W, col0:col0 + 128], t[0:8, 8, :], reads=[tk])
            if kv == 1:
                cx.op('dve', lambda e: e.tensor_copy(out=VT[:, g, 0:ntb, :], in_=t[:, 0:ntb, :]), reads=[tk], writes=["VT"])

        for h in range(4):
            blocks = []
            for g in range(3):
                for qkv in range(3):
                    c0 = g * 1536 + qkv * 512 + h * 128
                    blocks.append((w_in_a[j, :, c0:c0 + 128], KC))
            ws = WS(blocks)
            for g in range(3):
                for qkv in range(3):
                    wb, wkey = ws.get(g * 3 + qkv)
                    if qkv < 2:
                        pset = nextset()
                        proj_fm(wb, wkey, KC, lambda kc, a, nn: H[:, kc, 2 + a:2 + a + nn], HALL, pset)
                        dstT = QT if qkv == 0 else KT
                        dk = "QTA" if qkv == 0 else "KTA"
                        for ti, (a, b) in enumerate(CT):
                            cx.op('act', lambda e: e.copy(out=dstT[:, g, a:b], in_=PS[pset[ti]][:, 0:350]), reads=[("PS", pset[ti])], writes=[dk])
                    if qkv >= 1:
                        proj_tm(wb, wkey, g, qkv - 1, h)
            if p == 0:
                for g in range(3):
                    cx.dma('sp', kth[j, h, g], KT[:, g, 0:1024], reads=["KTA"], writes=["KTH"])
                    cx.dma('sp', vh[j, h, g].rearrange("(b q) e -> q b e", q=128), VT[:, g, 0:8, :], reads=["VT"], writes=["VH"])
            else:
                for g in range(3):
                    ng = min(WIN[g], 1024)
                    cx.dma('sp', HK[:, hoff[g]:hoff[g] + ng], kth[j, h, g][:, 1024 - ng:1024], reads=["KTH"], writes=["HK"])
                    cx.dma('sp', HV[:, hboff[g]:hboff[g] + ng // 128, :],
                           vh[j, h, g][1024 - ng:1024, :].rearrange("(b q) e -> q b e", q=128), reads=["VH"], writes=["HV"])

            def mkblock(g, nk, kap, vap, c0, n, q0, keys):
                def sfn(S):
                    return [lambda e: e.matmul(S[0:nk, 0:n], lhsT=kap, rhs=QT[:, g, q0:q0 + n], start=True, stop=False),
                            lambda e: e.matmul(S[0:nk, 0:n], lhsT=IDB[0:nk, 0:nk], rhs=MSK[g][0:nk, c0:c0 + n], start=False, stop=True)]
                return (nk, sfn, keys + ["QTA", "MSK", "IDB"], [vap], keys)

            for (q0, q1) in ((0, 512), (512, 1024)):
                n = q1 - q0
                vq0 = 1024 * p + q0
                groups = []
                for g in range(3):
                    if p == 1:
                        ng = min(WIN[g], 1024)
                        for bb in range(ng // 128):
                            vk0 = 1024 - ng + bb * 128
                            if _blk_valid(g, vq0, n, vk0, 128):
                                groups.append(mkblock(g, 128, HK[:, hoff[g] + bb * 128:hoff[g] + (bb + 1) * 128], HV[:, hboff[g] + bb, :],
                                                      vq0 - vk0 + 384, n, q0, ["HK", "HV"]))
                    for tb in range(8):
                        vk0 = 1024 * p + tb * 128
                        if _blk_valid(g, vq0, n, vk0, 128):
                            groups.append(mkblock(g, 128, KT[:, g, tb * 128:(tb + 1) * 128], VT[:, g, tb, :],
                                                  vq0 - vk0 + 384, n, q0, ["KTA", "VT"]))
                attend(n, groups, [4, 5], scale, None, lambda ui: (OTA[:, h, q0:q1], ["OTA"]), "E")
            if p == 0:
                groups = []
                n = 8
                for g in range(3):
                    W = WIN[g]
                    nbk = W // 128
                    def prep(g=g, W=W, nbk=nbk):
                        cx.dma('sp', CS[:, 0:nbk, :], cw[g][j, :, h * 128:(h + 1) * 128].rearrange("(b q) e -> q b e", q=128), writes=["CS"])
                        for b4 in range(0, nbk, 4):
                            m4 = min(4, nbk - b4)
                            cx.mmg([(lambda e, k=k: e.transpose(out=PS[7][:, k * 128:(k + 1) * 128], in_=CS[:, b4 + k, :], identity=IDF[:]))
                                    for k in range(m4)], reads=["CS", "IDF"], writes=[("PS", 7)])
                            cx.op('act', lambda e: e.copy(out=SKT[:, b4 * 128:(b4 + m4) * 128], in_=PS[7][:, 0:m4 * 128]),
                                  reads=[("PS", 7)], writes=["SKT"])
                        cx.dma('sp', CS[:, 0:nbk, :], cw[g][j, :, 512 + h * 128:512 + (h + 1) * 128].rearrange("(b q) e -> q b e", q=128),
                               reads=[], writes=["CS"])
                        cx.op('pool', lambda e: e.tensor_copy(out=SVb[:, 0:nbk, :], in_=CS[:, 0:nbk, :]), reads=["CS"], writes=["SVB"])
                    first = True
                    for kb in range(nbk):
                        if _blk_valid(g, W, n, kb * 128, 128):
                            blk = mkblock(g, 128, SKT[:, kb * 128:(kb + 1) * 128], SVb[:, kb, :], W - kb * 128 + 384, n, SS0, ["SKT", "SVB"])
                            groups.append(blk + ((prep if first else None),))
                            first = False
                    blk = mkblock(g, 8, KT[:, g, SS0:SS1], VT[0:8, g, 8, :], 384, n, SS0, ["KTA", "VT"])
                    groups.append(blk + ((prep if first else None),))
                attend_lazy(n, groups, [4, 5], scale, lambda ui: (OTA[:, h, SS0:SS1], ["OTA"]))
        blocks = [(w_out_a[j, :, nn * 128:(nn + 1) * 128], 4) for nn in range(KC)]
        ws = WS(blocks)
        for nn in range(KC):
            wb, wkey = ws.get(nn)
            pset = nextset()
            proj_fm(wb, wkey, 4, lambda kc, a, m: OTA[:, kc, a:a + m], ["OTA"], pset)
            add_to_x(nn, pset)
        cx.barrier()

    def attend_lazy(n, groups, u_list, scale, out_fn):
        g2 = []
        nb = len(groups)
        zidx = u_list[-1]
        uidx = u_list[:-1]
        sbanks = [0, 1, 2]
        for bi, (nk, sfn, sreads, vaps, vreads, prep) in enumerate(groups):
            if prep is not None:
                prep()
            sbk = sbanks[bi % 3]
            S = PS[sbk]
            cx.mmg(sfn(S), reads=sreads, writes=[("PS", sbk)])
            ek = bi % 3
            cx.op('act', lambda e: e.activation(out=AT["eb"][ek][0:nk, 0:n], in_=S[0:nk, 0:n], func=AF.Exp, scale=scale),
                  reads=[("PS", sbk)], writes=[("E", ek)])
            for ui, u in enumerate(uidx):
                cx.mmg([lambda e: e.matmul(PS[u][:, 0:n], lhsT=vaps[ui], rhs=AT["eb"][ek][0:nk, 0:n], start=(bi == 0), stop=(bi == nb - 1))],
                       reads=[("E", ek)] + vreads, writes=[("PS", u)])
            cx.mmg([lambda e: e.matmul(PS[zidx][:, 0:n], lhsT=ONB[0:nk, :], rhs=AT["eb"][ek][0:nk, 0:n], start=(bi == 0), stop=(bi == nb - 1))],
                   reads=[("E", ek), "ONB"], writes=[("PS", zidx)])
        cx.op('dve', lambda e: e.reciprocal(out=AT["rz"][:, 0:n], in_=PS[zidx][:, 0:n]), reads=[("PS", zidx)], writes=["RZ"])
        for ui, u in enumerate(uidx):
            oap, okeys = out_fn(ui)
            cx.op('dve', lambda e: e.tensor_tensor(out=oap, in0=PS[u][:, 0:n], in1=AT["rz"][:, 0:n], op=ALU.mult),
                  reads=[("PS", u), "RZ"], writes=okeys)

    def final(p):
        rstd_cols(lambda c, a, b: X[:, c, a:b], CT, lambda a, b: RS[:, a:b], [("X", c) for c in range(KC)])
        YT = sb([128, KC, 128], F32)
        YS = [sb([128, D], F32) for _ in range(2)]
        XK = [("X", c) for c in range(KC)]
        units = [(tb * 128, 128) for tb in range(8)] + ([(SS0, 8)] if p == 0 else [])
        for ui, (c0, m) in enumerate(units):
            for c in range(KC):
                eng = 'dve'
                cx.op(eng, lambda e: e.scalar_tensor_tensor(out=YT[:, c, 0:m], in0=X[:, c, c0:c0 + m], scalar=G_[:, 16, c:c + 1],
                                                            in1=RS[:, c0:c0 + m], op0=ALU.mult, op1=ALU.mult),
                      reads=[("X", c), "RS", "G"], writes=["YT"])
            s = ui % 2
            for c4 in range(4):
                pb = 6 + c4 % 2
                cx.mmg([(lambda e, k=k: e.transpose(out=PS[pb][0:m, k * 128:(k + 1) * 128], in_=YT[:, c4 * 4 + k, 0:m], identity=IDF[:]))
                        for k in range(4)], reads=["YT", "IDF"], writes=[("PS", pb)])
                cx.op('act', lambda e: e.copy(out=YS[s][0:m, c4 * 512:(c4 + 1) * 512], in_=PS[pb][0:m, :]), reads=[("PS", pb)], writes=[("YS", s)])
            if m == 128:
                cx.dma('act', yp[p * 1024 + c0:p * 1024 + c0 + 128, :], YS[s][:], reads=[("YS", s)])
            else:
                cx.dma('act', ys, YS[s][0:8, :], reads=[("YS", s)])
        cx.barrier()

    cx.op('pool', lambda e: e.memset(EPSAP[:], EPS), writes=["EPS"])
    load_consts()
    for p in range(npass):
        mark = nc_mark(nc)
        load_x(p)
        nc_release(nc, mark)
        for layer in layers:
            if "mix" in phases:
                m2 = nc_mark(nc)
                if layer % 2 == 0:
                    mixer_a(p, layer)
                else:
                    mixer_b(p, layer)
                nc_release(nc, m2)
            if "mem" in phases:
                m2 = nc_mark(nc)
                mem_attn(p, layer)
                nc_release(nc, m2)
            if "ffn" in phases:
                m2 = nc_mark(nc)
                ffn(p, layer)
                nc_release(nc, m2)
        if "final" in phases:
            m2 = nc_mark(nc)
            final(p)
            nc_release(nc, m2)
    cx.barrier()
    return nc


_ALLOCS = []
STOPAT = [0]
DECL = set()
_CNT = [0]


def nc_mark(nc):
    return len(_ALLOCS)


def nc_release(nc, mark):
    while len(_ALLOCS) > mark:
        t = _ALLOCS.pop()
        t.__exit__(None, None, None)


def _patch_alloc(nc):
    def alloc(shape, dt):
        _CNT[0] += 1
        cm = nc.sbuf_tensor("t%d" % _CNT[0], list(shape), dt)
        t = cm.__enter__()
        _ALLOCS.append(cm)
        return t
    return alloc


def _fm(a, lead):
    return a


def _host_inputs(inp):
    f32 = np.float32
    bf = ml_dtypes.bfloat16
    gains = np.concatenate([inp["g_mix"], inp["g_mem_q"], inp["g_mem_kv"], inp["g_ffn"], inp["g_final"][None]], axis=0).astype(f32)
    gfm = np.ascontiguousarray(gains.reshape(17, 16, 128).transpose(2, 0, 1)).reshape(128, -1)
    cffn = np.ascontiguousarray(inp["conv_ffn"].astype(f32).reshape(4, 3, 88, 128).transpose(3, 0, 1, 2)).reshape(128, -1)
    cbw = np.ascontiguousarray(inp["conv_b"].astype(f32).reshape(2, 3, 16, 128).transpose(3, 0, 1, 2)).reshape(128, -1)
    common = {
        "gfm": gfm, "cffn": cffn, "cbw": cbw,
        "idf": np.eye(128, dtype=f32), "idb": np.eye(128, dtype=f32).astype(bf), "onb": np.ones((128, 128), f32).astype(bf),
    }
    for g in range(3):
        common["msk%d" % g] = np.ascontiguousarray(MASKNP[g][:, :MWG[g]]).astype(bf)
    for k in ("w_in_a", "w_out_a", "w_in_b", "w_out_b", "w_q_mem", "w_kv_mem", "w_o_mem", "w_up", "w_down"):
        common[k] = np.ascontiguousarray(inp[k], dtype=f32)
    maps = []
    for c in range(8):
        s, b = c % 4, c
        m = dict(common)
        m["xp"] = np.ascontiguousarray(inp["x_prompt"][s], dtype=f32)
        m["xs"] = np.ascontiguousarray(inp["x_sample"][b], dtype=f32)
        m["mem"] = np.ascontiguousarray(inp["mem_prompt"][s], dtype=f32)
        for g, nm in enumerate(("cache_win0_kv", "cache_win1_kv", "cache_win2_kv")):
            m["cw%d" % g] = np.ascontiguousarray(inp[nm][:, b], dtype=f32).reshape(2, WIN[g], 1024)
        m["cmem"] = np.ascontiguousarray(inp["cache_mem_kv"][:, b], dtype=f32).reshape(4, 256, 4096)
        sfc = inp["state_ffn_conv"][:, b].astype(f32)
        m["sfc"] = np.ascontiguousarray(sfc.reshape(4, 2, 88, 128).transpose(3, 0, 2, 1)).reshape(128, -1)
        scb = inp["state_conv_b"][:, b].astype(f32)
        m["scb"] = np.ascontiguousarray(scb.reshape(2, 2, 16, 128).transpose(3, 0, 2, 1)).reshape(128, -1)
        maps.append(m)
    return maps


_NC_CACHE = {}


def kernel(**inputs):
    inp = {k: np.asarray(v) for k, v in inputs.items()}
    if "nc" not in _NC_CACHE:
        _NC_CACHE["nc"] = build_program()
    nc = _NC_CACHE["nc"]
    maps = [{k: v for k, v in m.items() if k in DECL} for m in _host_inputs(inp)]
    res = run_bass_kernel_spmd(nc, maps, core_ids=list(range(8)))
    R = res.results
    f32 = np.float32
    y_prompt = np.stack([R[s]["yp"] for s in range(4)]).astype(f32)
    y_sample = np.stack([R[b]["ys"] for b in range(8)]).astype(f32)
    outs = [y_prompt, y_sample]
    for g in range(3):
        outs.append(np.stack([R[s]["pw%d" % g] for s in range(4)], axis=1).reshape(2, 4, WIN[g], 2, 4, 128).astype(f32))
    outs.append(np.stack([R[s]["pcb"] for s in range(4)], axis=1).astype(f32))
    outs.append(np.stack([R[s]["pfc"] for s in range(4)], axis=1).astype(f32))
    outs.append(np.stack([R[s]["pmk"] for s in range(4)], axis=1).reshape(4, 4, 256, 2, 4, 512).astype(f32))
    for g in range(3):
        outs.append(np.stack([R[b]["sw%d" % g] for b in range(8)], axis=1).reshape(2, 8, WIN[g], 2, 4, 128).astype(f32))
    outs.append(np.stack([R[b]["scbo"] for b in range(8)], axis=1).astype(f32))
    outs.append(np.stack([R[b]["sfco"] for b in range(8)], axis=1).astype(f32))
    return tuple(outs)
```

```python
import numpy as np
import ml_dtypes
import concourse.bass as bass
import concourse.mybir as mybir
from concourse.bass_utils import run_bass_kernel_spmd

F32 = mybir.dt.float32
BF16 = mybir.dt.bfloat16
AF = mybir.ActivationFunctionType
ALU = mybir.AluOpType

D = 2048
KC = 16
DFF = 5632
NCOL = 1050
CT = [(0, 350), (350, 700), (700, 1050)]
SP0, SS0, SS1 = 1024, 1026, 1034
WIN = (128, 512, 2048)
DIL = (1, 4, 16)
MW = 2448
MWG = (1152, 1536, 2448)
NEG = -30000.0
EPS = 1e-6
LIM = 4000


class Ctx:
    def __init__(self, nc):
        self.nc = nc
        self.engs = {'pe': nc.tensor, 'act': nc.scalar, 'dve': nc.vector, 'pool': nc.gpsimd, 'sp': nc.sync}
        self.count = {e: 0 for e in self.engs}
        self.sems = {e: [] for e in self.engs}
        self.waited = {e: {} for e in self.engs}
        self.semobj = {}
        self.lastw = {}
        self.readers = {}
        self.dsems = [nc.alloc_semaphore(f"dq{i}") for i in range(40)]
        self.dtot = [0] * len(self.dsems)
        self.dnext = 0
        for s in self.dsems:
            self.semobj[id(s)] = s

    def _tok_of(self, e):
        n = self.count[e]
        if n == 0:
            return None
        n -= 1
        return (id(self.sems[e][n // LIM]), n % LIM + 1)

    def _wait(self, e, tok):
        if tok is None:
            return
        sid, val = tok
        if self.waited[e].get(sid, 0) >= val:
            return
        self.engs[e].wait_ge(self.semobj[sid], val)
        self.waited[e][sid] = val

    def begin(self, e, reads, writes):
        for r in reads:
            self._wait(e, self.lastw.get(r))
        for w in writes:
            self._wait(e, self.lastw.get(w))
            rd = self.readers.get(w)
            if rd:
                for sid, val in rd.items():
                    self._wait(e, (sid, val))

    def _register(self, tok, reads, writes):
        sid, val = tok
        for r in reads:
            d = self.readers.setdefault(r, {})
            if d.get(sid, 0) < val:
                d[sid] = val
        for w in writes:
            self.lastw[w] = tok
            self.readers[w] = {}

    def end(self, e, inst, reads, writes):
        n = self.count[e]
        si = n // LIM
        if si >= len(self.sems[e]):
            s = self.nc.alloc_semaphore(f"e_{e}_{si}")
            self.sems[e].append(s)
            self.semobj[id(s)] = s
        sem = self.sems[e][si]
        val = n % LIM + 1
        inst.then_inc(sem, 1)
        self.count[e] = n + 1
        tok = (id(sem), val)
        if e == 'pe':
            self.waited[e][id(sem)] = val
        self._register(tok, reads, writes)

    def op(self, e, fn, reads=(), writes=()):
        self.begin(e, reads, writes)
        inst = fn(self.engs[e])
        self.end(e, inst, reads, writes)

    def mmg(self, fns, reads=(), writes=()):
        self.begin('pe', reads, writes)
        inst = None
        for f in fns:
            inst = f(self.nc.tensor)
        self.end('pe', inst, reads, writes)

    def dma(self, q, out, in_, reads=(), writes=()):
        self.begin(q, reads, writes)
        j = self.dnext
        self.dnext = (j + 1) % len(self.dsems)
        sem = self.dsems[j]
        if self.dtot[j] > 0:
            self._wait(q, (id(sem), self.dtot[j]))
        self.engs[q].dma_start(out=out, in_=in_).then_inc(sem, 16)
        self.dtot[j] += 16
        self._register((id(sem), self.dtot[j]), reads, writes)

    def barrier(self):
        for e in self.engs:
            for e2 in self.engs:
                if e2 != e:
                    self._wait(e, self._tok_of(e2))
            for j, s in enumerate(self.dsems):
                if self.dtot[j] > 0:
                    self._wait(e, (id(s), self.dtot[j]))


def _mask_np():
    m = np.full((3, 128, MW), NEG, np.float32)
    i = np.arange(128)[:, None]
    c = np.arange(MW)[None, :]
    delta = c - i - 384
    for g in range(3):
        d = DIL[g]
        valid = (delta >= 0) & (delta <= 128 * d) & (delta % d == 0)
        m[g][valid] = 0.0
    return m


MASKNP = _mask_np()


def _blk_valid(g, vq0, n, vk0, nk):
    c0 = vq0 - vk0 + 384
    if c0 < 0 or c0 + n > MW:
        return False
    return bool((MASKNP[g][:nk, c0:c0 + n] == 0.0).any())


def build_program(phases=("mix", "mem", "ffn", "final"), npass=2, layers=(0, 1, 2, 3)):
    nc = bass.Bass("TRN2", target_bir_lowering=False)
    cx = Ctx(nc)
    DECL.clear()

    def din(name, shape, dt=F32):
        DECL.add(name)
        return nc.dram_tensor(name, list(shape), dt, kind="ExternalInput").ap()

    class Lazy:
        def __init__(self, name, shape):
            self.name, self.shape, self.ap = name, shape, None

        def __getitem__(self, idx):
            if self.ap is None:
                self.ap = din(self.name, self.shape)
            return self.ap[idx]

    def dinl(name, shape):
        return Lazy(name, shape)

    def dout(name, shape):
        return nc.dram_tensor(name, list(shape), F32, kind="ExternalOutput").ap()

    xp = din("xp", [2048, D]); xs = din("xs", [8, D]); mem = din("mem", [256, D])
    cw = [din("cw0", [2, 128, 1024]), din("cw1", [2, 512, 1024]), din("cw2", [2, 2048, 1024])]
    cmem = din("cmem", [4, 256, 4096])
    gfm = din("gfm", [128, 17 * 16])
    cffn = din("cffn", [128, 4 * 3 * 88]); cb_in = din("cbw", [128, 2 * 3 * 16])
    sfc_in = din("sfc", [128, 4 * 88 * 2]); scb_in = din("scb", [128, 2 * 16 * 2])
    msk_in = [din("msk%d" % g, [128, MWG[g]], BF16) for g in range(3)]
    idf_in = din("idf", [128, 128]); idb_in = din("idb", [128, 128], BF16); onb_in = din("onb", [128, 128], BF16)
    w_in_a = dinl("w_in_a", [2, D, 4608]); w_out_a = dinl("w_out_a", [2, 512, D])
    w_in_b = dinl("w_in_b", [2, D, 3 * D]); w_out_b = dinl("w_out_b", [2, D, D])
    w_q_mem = dinl("w_q_mem", [4, D, D]); w_kv_mem = dinl("w_kv_mem", [4, D, 2 * D]); w_o_mem = dinl("w_o_mem", [4, D, D])
    w_up = dinl("w_up", [4, D, 2 * DFF]); w_down = dinl("w_down", [4, DFF, D])

    yp = dout("yp", [2048, D]); ys = dout("ys", [8, D])
    pw = [dout("pw0", [2, 128, 1024]), dout("pw1", [2, 512, 1024]), dout("pw2", [2, 2048, 1024])]
    pcb = dout("pcb", [2, 2, D]); pfc = dout("pfc", [4, 2, 2 * DFF]); pmk = dout("pmk", [4, 256, 4096])
    sw = [dout("sw0", [2, 128, 1024]), dout("sw1", [2, 512, 1024]), dout("sw2", [2, 2048, 1024])]
    scbo = dout("scbo", [2, 2, D]); sfco = dout("sfco", [4, 2, 2 * DFF])
    kth = nc.dram_tensor("kth", [2, 4, 3, 128, 1024], BF16).ap()
    vh = nc.dram_tensor("vh", [2, 4, 3, 1024, 128], BF16).ap()
    mks = nc.dram_tensor("mks", [4, 128, KC * 256], BF16).ap()
    mvs = nc.dram_tensor("mvs", [4, 128, 2 * D], BF16).ap()

    ph = _patch_alloc(nc)
    sb = ph
    X = sb([128, KC, NCOL], F32)
    H = sb([128, KC, NCOL + 2], BF16)
    WB = {"stg": [], "bfw": []}
    AT = {"eb": [], "rz": None}

    def alloc_wbuf(nstg, nbf):
        WB["stg"] = [sb([128, KC, 128], F32) for _ in range(nstg)]
        WB["bfw"] = [sb([128, KC, 128], BF16) for _ in range(nbf)]

    def alloc_attn():
        AT["eb"] = [sb([128, 512], BF16) for _ in range(3)]
        AT["rz"] = sb([128, 512], F32)
    RS = sb([128, NCOL], F32)
    SQ = [sb([128, 352], BF16) for _ in range(2)]
    G_ = sb([128, 17, 16], F32)
    CWF = sb([128, 4, 3, 88], F32)
    CWB = sb([128, 2, 3, 16], F32)
    SFC = sb([128, 4, 88, 2], F32)
    SCB = sb([128, 2, 16, 2], F32)
    IDF = sb([128, 128], F32); IDB = sb([128, 128], BF16); ONB = sb([128, 128], BF16)
    HS = sb([128, 12, KC, 2], BF16)
    CBF = sb([128, 2, 88], F32)
    CBO = sb([128, 2, 128], F32)
    PS = []
    for _ in range(8):
        _cm = nc.psum_tensor("ps%d" % len(PS), [128, 512], F32)
        PS.append(_cm.__enter__())
    SETS = [[0, 1, 2], [3, 4, 5]]
    HALL = [("H", c) for c in range(KC)]
    state = {"wload": 0, "wcast": 0, "pset": 0, "sq": 0}

    class WS:
        def __init__(self, blocks):
            self.blocks = blocks
            self.loaded = 0
            self.casted = 0
            self.slots = {}

        def _load(self, i):
            ap, kcn = self.blocks[i]
            s = state["wload"] % len(WB["stg"])
            state["wload"] += 1
            cx.dma('sp', WB["stg"][s][:, 0:kcn, :], ap.rearrange("(kc p) n -> p kc n", p=128), writes=[("STG", s)])
            self.slots[i] = [s, None]

        def _cast(self, i):
            ap, kcn = self.blocks[i]
            s = self.slots[i][0]
            b = state["wcast"] % len(WB["bfw"])
            state["wcast"] += 1
            cx.op('act', lambda e: e.copy(out=WB["bfw"][b][:, 0:kcn, :], in_=WB["stg"][s][:, 0:kcn, :]),
                  reads=[("STG", s)], writes=[("BFW", b)])
            self.slots[i][1] = b

        def get(self, i):
            n = len(self.blocks)
            while self.casted <= min(i + 1, n - 1):
                while self.loaded <= min(self.casted + len(WB["stg"]) - 1, n - 1):
                    self._load(self.loaded)
                    self.loaded += 1
                self._cast(self.casted)
                self.casted += 1
            b = self.slots[i][1]
            return WB["bfw"][b], ("BFW", b)

    def nextset():
        s = SETS[state["pset"] % 2]
        state["pset"] += 1
        return s

    def proj_fm(wb, wkey, kcn, rhs_fn, rkeys, pset, tiles=CT, ext=0):
        for ti, (a, b) in enumerate(tiles):
            n = b - a + ext
            ps = PS[pset[ti]]
            cx.mmg([(lambda e, kc=kc: e.matmul(ps[:, 0:n], lhsT=wb[:, kc, :], rhs=rhs_fn(kc, a, n),
                                               start=(kc == 0), stop=(kc == kcn - 1))) for kc in range(kcn)],
                   reads=[wkey] + rkeys, writes=[("PS", pset[ti])])

    def add_to_x(nchunk, pset):
        for ti, (a, b) in enumerate(CT):
            ps = PS[pset[ti]]
            cx.op('dve', lambda e: e.tensor_tensor(out=X[:, nchunk, a:b], in0=ps[:, 0:b - a], in1=X[:, nchunk, a:b], op=ALU.add),
                  reads=[("PS", pset[ti]), ("X", nchunk)], writes=[("X", nchunk)])

    def load_consts():
        for t, src, key in ((G_, gfm, "G"), (CWF, cffn, "CWF"), (CWB, cb_in, "CWB"), (SFC, sfc_in, "SFC"),
                            (SCB, scb_in, "SCB"), (IDF, idf_in, "IDF"), (IDB, idb_in, "IDB"), (ONB, onb_in, "ONB")):
            shp = list(t.shape)
            dst = t[:]
            if len(shp) == 3:
                srcv = src.rearrange("p (a b) -> p a b", a=shp[1])
            elif len(shp) == 4:
                srcv = src.rearrange("p (a b c) -> p a b c", a=shp[1], b=shp[2])
            else:
                srcv = src
            cx.dma('sp', dst, srcv, writes=[key])
        cx.op('pool', lambda e: e.memset(H[:], 0.0), writes=HALL)
        cx.op('pool', lambda e: e.memset(CBO[:], 0.0), writes=["CBO"])

    CONSTK = ["G", "CWF", "CWB", "SFC", "SCB", "IDF", "IDB", "ONB"]

    def load_x(p):
        XS = [sb([128, D], F32) for _ in range(2)]
        XSS = sb([8, D], F32)
        cx.op('pool', lambda e: e.memset(X[:, :, 1024:NCOL], 0.0), writes=[("X", c) for c in range(KC)])
        bank = 0
        for tb in range(8):
            s = tb % 2
            cx.dma('sp', XS[s][:], xp[p * 1024 + tb * 128: p * 1024 + (tb + 1) * 128, :], writes=[("XS", s)])
            for c4 in range(4):
                pb = 6 + bank % 2
                bank += 1
                cx.mmg([(lambda e, k=k: e.transpose(out=PS[pb][:, k * 128:(k + 1) * 128],
                                                    in_=XS[s][:, (c4 * 4 + k) * 128:(c4 * 4 + k + 1) * 128], identity=IDF[:]))
                        for k in range(4)], reads=[("XS", s), "IDF"], writes=[("PS", pb)])
                eng = 'act' if c4 % 2 == 0 else 'dve'
                outap = X[:, c4 * 4:c4 * 4 + 4, tb * 128:(tb + 1) * 128]
                inap = PS[pb][:].rearrange("p (a b) -> p a b", a=4)
                if eng == 'act':
                    cx.op('act', lambda e: e.copy(out=outap, in_=inap), reads=[("PS", pb)],
                          writes=[("X", c4 * 4 + k) for k in range(4)])
                else:
                    cx.op('dve', lambda e: e.tensor_copy(out=outap, in_=inap), reads=[("PS", pb)],
                          writes=[("X", c4 * 4 + k) for k in range(4)])
        if p == 0:
            cx.dma('sp', XSS[:], xs, writes=["XSS"])
            pb = 6
            cx.mmg([(lambda e, c=c: e.transpose(out=PS[pb][:, c * 8:(c + 1) * 8], in_=XSS[0:8, c * 128:(c + 1) * 128],
                                                identity=IDF[0:8, 0:8])) for c in range(KC)],
                   reads=["XSS", "IDF"], writes=[("PS", pb)])
            cx.op('act', lambda e: e.copy(out=X[:, :, SS0:SS1], in_=PS[pb][:, 0:128].rearrange("p (a b) -> p a b", a=KC)),
                  reads=[("PS", pb)], writes=[("X", c) for c in range(KC)])
        cx.barrier()

    def rstd_cols(src_fn, tiles, rs_ap_fn, skeys):
        for (a, b) in tiles:
            n = b - a
            for c in range(KC):
                q = state["sq"] % 2
                state["sq"] += 1
                cx.op('act', lambda e: e.activation(out=SQ[q][:, 0:n], in_=src_fn(c, a, b), func=AF.Square),
                      reads=[skeys[c]], writes=[("SQ", q)])
                cx.mmg([lambda e: e.matmul(PS[7][:, 0:n], lhsT=ONB[:], rhs=SQ[q][:, 0:n], start=(c == 0), stop=(c == KC - 1))],
                       reads=[("SQ", q), "ONB"], writes=[("PS", 7)])
            cx.op('act', lambda e: e.activation(out=rs_ap_fn(a, b), in_=PS[7][:, 0:n], func=AF.Sqrt, scale=1.0 / D, bias=EPSAP[:]),
                  reads=[("PS", 7), "EPS"], writes=["RS"])
            cx.op('dve', lambda e: e.reciprocal(out=rs_ap_fn(a, b), in_=rs_ap_fn(a, b)), reads=["RS"], writes=["RS"])

    EPSAP = sb([128, 1], F32)

    def norm(p, gi, save):
        rstd_cols(lambda c, a, b: X[:, c, a:b], CT, lambda a, b: RS[:, a:b], [("X", c) for c in range(KC)])
        k = 0
        for (a, b) in CT:
            for c in range(KC):
                eng = 'dve'
                k += 1
                cx.op(eng, lambda e: e.scalar_tensor_tensor(out=H[:, c, 2 + a:2 + b], in0=X[:, c, a:b], scalar=G_[:, gi, c:c + 1],
                                                            in1=RS[:, a:b], op0=ALU.mult, op1=ALU.mult),
                      reads=[("X", c), "RS", "G"], writes=[("H", c)])
        if save is not None:
            if p == 0:
                cx.op('act', lambda e: e.copy(out=HS[:, save, :, :], in_=H[:, :, 2 + 1022:2 + 1024]), reads=HALL, writes=[("HS", save)])
            else:
                cx.op('act', lambda e: e.copy(out=H[:, :, 0:2], in_=HS[:, save, :, :]), reads=[("HS", save)], writes=HALL)

    def emit_conv_out(dst_p, dst_s, p, li, nchunk, key):
        dst = dst_s if p == 0 else dst_p
        for r in range(2):
            cx.mmg([lambda e: e.transpose(out=PS[6][0:nchunk, 0:128], in_=CBF[:, r, 0:nchunk], identity=IDF[:])],
                   reads=[key, "IDF"], writes=[("PS", 6)])
            cx.op('act', lambda e: e.copy(out=CBO[0:nchunk, r, :], in_=PS[6][0:nchunk, 0:128]), reads=[("PS", 6)], writes=["CBO"])
            cx.dma('act', dst[li, r, :].rearrange("(c q) -> c q", q=128), CBO[0:nchunk, r, :], reads=["CBO"])

    def ffn(p, layer):
        alloc_wbuf(3, 3)
        norm(p, 12 + layer, 6 + layer)
        Gq = [sb([128, 11, NCOL], BF16) for _ in range(2)]
        TA = sb([128, 3, 350], F32); SA = sb([128, 3, 350], BF16); TB = sb([128, 3, 350], F32)
        blocks = []
        for q in range(4):
            for j in range(11):
                a = 11 * q + j
                blocks.append((w_up[layer, :, a * 128:(a + 1) * 128], KC))
                blocks.append((w_up[layer, :, DFF + a * 128:DFF + (a + 1) * 128], KC))
            for n in range(KC):
                blocks.append((w_down[layer, q * 1408:(q + 1) * 1408, n * 128:(n + 1) * 128], 11))
        ws = WS(blocks)
        bi = 0
        lo = 334 if p == 0 else 324
        for q in range(4):
            G = Gq[q % 2]
            for j in range(11):
                for half in range(2):
                    chunk = 11 * q + j + 44 * half
                    wb, wkey = ws.get(bi); bi += 1
                    pset = nextset()
                    proj_fm(wb, wkey, KC, lambda kc, a, n: H[:, kc, a:a + n], HALL, pset, ext=2)
                    T = TA if half == 0 else TB
                    tk = "TA" if half == 0 else "TB"
                    for ti, (a, b) in enumerate(CT):
                        ps = PS[pset[ti]]
                        pk = ("PS", pset[ti])
                        if ti == 2:
                            if p == 0:
                                cx.op('act', lambda e: e.copy(out=ps[:, 326:328], in_=SFC[:, layer, chunk, :]), reads=["SFC", pk], writes=[pk])
                            cx.op('act', lambda e: e.copy(out=CBF[:, :, chunk], in_=ps[:, lo:lo + 2]), reads=[pk], writes=["CBF"])
                        cx.op('act', lambda e: e.mul(out=T[:, ti, :], in_=ps[:, 2:352], mul=CWF[:, layer, 2, chunk:chunk + 1]),
                              reads=[pk, "CWF"], writes=[(tk, ti)])
                        cx.op('dve', lambda e: e.scalar_tensor_tensor(out=T[:, ti, :], in0=ps[:, 1:351], scalar=CWF[:, layer, 1, chunk:chunk + 1],
                                                                      in1=T[:, ti, :], op0=ALU.mult, op1=ALU.add),
                              reads=[pk, "CWF", (tk, ti)], writes=[(tk, ti)])
                        cx.op('dve', lambda e: e.scalar_tensor_tensor(out=T[:, ti, :], in0=ps[:, 0:350], scalar=CWF[:, layer, 0, chunk:chunk + 1],
                                                                      in1=T[:, ti, :], op0=ALU.mult, op1=ALU.add),
                              reads=[pk, "CWF", (tk, ti)], writes=[(tk, ti)])
                        if half == 0:
                            cx.op('act', lambda e: e.activation(out=SA[:, ti, :], in_=T[:, ti, :], func=AF.Silu),
                                  reads=[(tk, ti)], writes=[("SA", ti)])
                        else:
                            cx.op('pool', lambda e: e.tensor_tensor(out=G[:, j, a:b], in0=SA[:, ti, :], in1=TB[:, ti, :], op=ALU.mult),
                                  reads=[("SA", ti), ("TB", ti)], writes=[("G", q % 2, j)])
            for n in range(KC):
                wb, wkey = ws.get(bi); bi += 1
                pset = nextset()
                proj_fm(wb, wkey, 11, lambda kc, a, nn: G[:, kc, a:a + nn], [("G", q % 2, j) for j in range(11)], pset)
                add_to_x(n, pset)
        emit_conv_out(pfc, sfco, p, layer, 88, "CBF")
        cx.barrier()

    def mixer_b(p, layer):
        j = layer // 2
        alloc_wbuf(4, 3)
        norm(p, layer, layer)
        Y = sb([128, KC, NCOL], BF16)
        T1 = sb([128, 3, 352], F32); T2 = sb([128, 3, 352], F32); T3 = sb([128, 3, 350], F32)
        blocks = []
        for c in range(KC):
            blocks.append((w_in_b[j, :, D + c * 128:D + (c + 1) * 128], KC))
            blocks.append((w_in_b[j, :, 2 * D + c * 128:2 * D + (c + 1) * 128], KC))
            blocks.append((w_in_b[j, :, c * 128:(c + 1) * 128], KC))
        for n in range(KC):
            blocks.append((w_out_b[j, :, n * 128:(n + 1) * 128], KC))
        ws = WS(blocks)
        bi = 0
        lo = 334 if p == 0 else 324
        rf = lambda kc, a, n: H[:, kc, a:a + n]
        for c in range(KC):
            wb, wkey = ws.get(bi); bi += 1
            pset = nextset()
            proj_fm(wb, wkey, KC, rf, HALL, pset, ext=2)
            for ti in range(3):
                cx.op('act', lambda e: e.copy(out=T1[:, ti, :], in_=PS[pset[ti]][:, 0:352]), reads=[("PS", pset[ti])], writes=[("T1", ti)])
            wb, wkey = ws.get(bi); bi += 1
            pset = nextset()
            proj_fm(wb, wkey, KC, rf, HALL, pset, ext=2)
            for ti in range(3):
                cx.op('dve', lambda e: e.tensor_tensor(out=T2[:, ti, :], in0=PS[pset[ti]][:, 0:352], in1=T1[:, ti, :], op=ALU.mult),
                      reads=[("PS", pset[ti]), ("T1", ti)], writes=[("T2", ti)])
                if ti == 2:
                    if p == 0:
                        cx.op('act', lambda e: e.copy(out=T2[:, 2, 326:328], in_=SCB[:, j, c, :]), reads=["SCB", ("T2", 2)], writes=[("T2", 2)])
                    cx.op('act', lambda e: e.copy(out=CBF[:, :, c], in_=T2[:, 2, lo:lo + 2]), reads=[("T2", 2)], writes=["CBF"])
                cx.op('act', lambda e: e.mul(out=T3[:, ti, :], in_=T2[:, ti, 2:352], mul=CWB[:, j, 2, c:c + 1]),
                      reads=[("T2", ti), "CWB"], writes=[("T3", ti)])
                cx.op('dve', lambda e: e.scalar_tensor_tensor(out=T3[:, ti, :], in0=T2[:, ti, 1:351], scalar=CWB[:, j, 1, c:c + 1],
                                                               in1=T3[:, ti, :], op0=ALU.mult, op1=ALU.add),
                      reads=[("T2", ti), "CWB", ("T3", ti)], writes=[("T3", ti)])
                cx.op('dve', lambda e: e.scalar_tensor_tensor(out=T3[:, ti, :], in0=T2[:, ti, 0:350], scalar=CWB[:, j, 0, c:c + 1],
                                                               in1=T3[:, ti, :], op0=ALU.mult, op1=ALU.add),
                      reads=[("T2", ti), "CWB", ("T3", ti)], writes=[("T3", ti)])
            wb, wkey = ws.get(bi); bi += 1
            pset = nextset()
            proj_fm(wb, wkey, KC, rf, HALL, pset, ext=2)
            for ti, (a, b) in enumerate(CT):
                cx.op('dve', lambda e: e.tensor_tensor(out=Y[:, c, a:b], in0=PS[pset[ti]][:, 2:352], in1=T3[:, ti, :], op=ALU.mult),
                      reads=[("PS", pset[ti]), ("T3", ti)], writes=[("Y", c)])
        for n in range(KC):
            wb, wkey = ws.get(bi); bi += 1
            pset = nextset()
            proj_fm(wb, wkey, KC, lambda kc, a, nn: Y[:, kc, a:a + nn], [("Y", c) for c in range(KC)], pset)
            add_to_x(n, pset)
        emit_conv_out(pcb, scbo, p, j, 16, "CBF")
        cx.barrier()

    def attend(n, groups, u_list, scale, rz, out_fn, ekeybase=None):
        nb = len(groups)
        zidx = u_list[-1]
        uidx = u_list[:-1]
        sbanks = [b for b in range(8) if b not in u_list][:3]

        def has_prep(bi):
            g = groups[bi]
            return len(g) > 5 and g[5] is not None

        def emit_s(bi):
            g = groups[bi]
            nk, sfn, sreads = g[0], g[1], g[2]
            if has_prep(bi):
                g[5]()
            sbk = sbanks[bi % 3]
            S = PS[sbk]
            cx.mmg(sfn(S), reads=sreads, writes=[("PS", sbk)])
            ek = bi % 3
            cx.op('act', lambda e: e.activation(out=AT["eb"][ek][0:nk, 0:n], in_=S[0:nk, 0:n], func=AF.Exp, scale=scale),
                  reads=[("PS", sbk)], writes=[("E", ek)])

        def emit_v(bi):
            g = groups[bi]
            nk, vaps, vreads = g[0], g[3], g[4]
            ek = bi % 3
            for ui, u in enumerate(uidx):
                cx.mmg([lambda e: e.matmul(PS[u][:, 0:n], lhsT=vaps[ui], rhs=AT["eb"][ek][0:nk, 0:n], start=(bi == 0), stop=(bi == nb - 1))],
                       reads=[("E", ek)] + vreads, writes=[("PS", u)])
            cx.mmg([lambda e: e.matmul(PS[zidx][:, 0:n], lhsT=ONB[0:nk, :], rhs=AT["eb"][ek][0:nk, 0:n], start=(bi == 0), stop=(bi == nb - 1))],
                   reads=[("E", ek), "ONB"], writes=[("PS", zidx)])

        emit_s(0)
        for bi in range(nb):
            ahead = bi + 1 < nb and not has_prep(bi + 1)
            if ahead:
                emit_s(bi + 1)
            emit_v(bi)
            if bi + 1 < nb and not ahead:
                emit_s(bi + 1)
        cx.op('dve', lambda e: e.reciprocal(out=AT["rz"][:, 0:n], in_=PS[zidx][:, 0:n]), reads=[("PS", zidx)], writes=["RZ"])
        for ui, u in enumerate(uidx):
            oap, okeys = out_fn(ui)
            cx.op('dve', lambda e: e.tensor_tensor(out=oap, in0=PS[u][:, 0:n], in1=AT["rz"][:, 0:n], op=ALU.mult),
                  reads=[("PS", u), "RZ"], writes=okeys)

    def mem_attn(p, layer):
        alloc_wbuf(2, 3)
        alloc_attn()
        MK = sb([128, KC, 256], BF16)
        MV = sb([128, 2, D], BF16)
        _mk = nc_mark(nc)
        if p == 1:
            cx.dma('sp', MK[:], mks[layer].rearrange("p (a b) -> p a b", a=KC), reads=["MKS"], writes=["MK"])
            cx.dma('sp', MV[:], mvs[layer].rearrange("p (a b) -> p a b", a=2), reads=["MKS"], writes=["MV"])
        else:
            mem_kv(p, layer, MK, MV)
        cx.barrier()
        nc_release(nc, _mk)
        mem_rest(p, layer, MK, MV)

    def mem_kv(p, layer, MK, MV):
        MS = sb([128, D], F32); MEMN = sb([128, KC, 256], F32); HM = sb([128, KC, 256], BF16)
        RSM = sb([128, 256], F32); TMO = [sb([128, 2, 128], F32) for _ in range(2)]
        for tb in range(2):
            cx.dma('sp', MS[:], mem[tb * 128:(tb + 1) * 128, :], writes=["MS"])
            for c4 in range(4):
                pb = 6
                cx.mmg([(lambda e, k=k: e.transpose(out=PS[pb][:, k * 128:(k + 1) * 128],
                                                    in_=MS[:, (c4 * 4 + k) * 128:(c4 * 4 + k + 1) * 128], identity=IDF[:]))
                        for k in range(4)], reads=["MS", "IDF"], writes=[("PS", pb)])
                cx.op('act', lambda e: e.copy(out=MEMN[:, c4 * 4:c4 * 4 + 4, tb * 128:(tb + 1) * 128],
                                              in_=PS[pb][:].rearrange("p (a b) -> p a b", a=4)),
                      reads=[("PS", pb)], writes=["MEMN"])
        rstd_cols(lambda c, a, b: MEMN[:, c, a:b], [(0, 256)], lambda a, b: RSM[:, a:b], ["MEMN"] * KC)
        for c in range(KC):
            cx.op('dve', lambda e: e.scalar_tensor_tensor(out=HM[:, c, :], in0=MEMN[:, c, :], scalar=G_[:, 8 + layer, c:c + 1],
                                                          in1=RSM[:, :], op0=ALU.mult, op1=ALU.mult),
                  reads=["MEMN", "RS", "G"], writes=["HM"])
        blocks = [(w_kv_mem[layer, :, cb * 128:(cb + 1) * 128], KC) for cb in range(32)]
        ws = WS(blocks)
        pm = pmk[layer].rearrange("(tb q) n -> q tb n", q=128)
        for cb in range(32):
            wb, wkey = ws.get(cb)
            if cb < 16:
                cx.mmg([(lambda e, kc=kc: e.matmul(PS[5][:, 0:256], lhsT=wb[:, kc, :], rhs=HM[:, kc, :], start=(kc == 0), stop=(kc == KC - 1)))
                        for kc in range(KC)], reads=[wkey, "HM"], writes=[("PS", 5)])
                cx.op('act', lambda e: e.copy(out=MK[:, cb, :], in_=PS[5][:, 0:256]), reads=[("PS", 5)], writes=["MK"])
            pb = 6 + cb % 2
            for tb in range(2):
                cx.mmg([(lambda e, kc=kc: e.matmul(PS[pb][:, tb * 128:(tb + 1) * 128], lhsT=HM[:, kc, tb * 128:(tb + 1) * 128], rhs=wb[:, kc, :],
                                                   start=(kc == 0), stop=(kc == KC - 1))) for kc in range(KC)],
                       reads=[wkey, "HM"], writes=[("PS", pb)])
            t = TMO[cb % 2]
            if p == 0 or cb >= 16:
                cx.op('act', lambda e: e.copy(out=t[:], in_=PS[pb][:, 0:256].rearrange("p (a b) -> p a b", a=2)),
                      reads=[("PS", pb)], writes=[("TMO", cb % 2)])
            if p == 0:
                cx.dma('act', pm[:, :, cb * 128:(cb + 1) * 128], t[:], reads=[("TMO", cb % 2)])
            if cb >= 16:
                cx.op('dve', lambda e: e.tensor_copy(out=MV[:, :, (cb - 16) * 128:(cb - 15) * 128], in_=t[:]),
                      reads=[("TMO", cb % 2)], writes=["MV"])
        cx.dma('sp', mks[layer].rearrange("p (a b) -> p a b", a=KC), MK[:], reads=["MK"], writes=["MKS"])
        cx.dma('sp', mvs[layer].rearrange("p (a b) -> p a b", a=2), MV[:], reads=["MV"], writes=["MKS"])

    def mem_rest(p, layer, MK, MV):
        norm(p, 4 + layer, None)
        QT = sb([128, KC, NCOL], BF16)
        blocks = [(w_q_mem[layer, :, n * 128:(n + 1) * 128], KC) for n in range(KC)]
        blocks += [(w_o_mem[layer, :, n * 128:(n + 1) * 128], KC) for n in range(KC)]
        ws = WS(blocks)
        for n in range(KC):
            wb, wkey = ws.get(n)
            pset = nextset()
            proj_fm(wb, wkey, KC, lambda kc, a, nn: H[:, kc, 2 + a:2 + a + nn], HALL, pset)
            for ti, (a, b) in enumerate(CT):
                cx.op('act', lambda e: e.copy(out=QT[:, n, a:b], in_=PS[pset[ti]][:, 0:350]), reads=[("PS", pset[ti])], writes=[("QT", n)])
        if STOPAT[0] == 2:
            cx.barrier()
            return
        scale = 512.0 ** -0.5
        QALL = [("QT", n) for n in range(KC)]

        def run(h, q0, q1, kfn, vfn, kkeys):
            n = q1 - q0
            groups = []
            for kb in range(2):
                def sfn(S, kb=kb):
                    return [(lambda e, ec=ec: e.matmul(S[:, 0:n], lhsT=kfn(ec, kb), rhs=QT[:, h * 4 + ec, q0:q1], start=(ec == 0), stop=(ec == 3)))
                            for ec in range(4)]
                groups.append((128, sfn, kkeys + QALL, [vfn(ec, kb) for ec in range(4)], kkeys))
            attend(n, groups, [3, 4, 5, 6, 7], scale, None,
                   lambda ui: (H[:, h * 4 + ui, 2 + q0:2 + q1], [("H", h * 4 + ui)]), "E")

        for h in range(4):
            for (q0, q1) in ((0, 512), (512, 1024)):
                run(h, q0, q1, lambda ec, kb: MK[:, h * 4 + ec, kb * 128:(kb + 1) * 128],
                    lambda ec, kb: MV[:, kb, h * 512 + ec * 128:h * 512 + (ec + 1) * 128], ["MK", "MV"])
        if STOPAT[0] == 3:
            cx.barrier()
            return
        if p == 0:
            CK = sb([128, 2, 512], F32); CV = CK
            SK = sb([128, 4, 256], BF16); SV = sb([128, 2, 512], BF16)
            cm = cmem[layer].rearrange("(kb q) n -> q kb n", q=128)
            for h in range(4):
                cx.dma('sp', CK[:], cm[:, :, h * 512:(h + 1) * 512], writes=["CK"])
                for kb in range(2):
                    cx.mmg([(lambda e, ec=ec: e.transpose(out=PS[0][:, ec * 128:(ec + 1) * 128], in_=CK[:, kb, ec * 128:(ec + 1) * 128], identity=IDF[:]))
                            for ec in range(4)], reads=["CK", "IDF"], writes=[("PS", 0)])
                    cx.op('act', lambda e: e.copy(out=SK[:, :, kb * 128:(kb + 1) * 128], in_=PS[0][:].rearrange("p (a b) -> p a b", a=4)),
                          reads=[("PS", 0)], writes=["SK"])
                cx.dma('sp', CV[:], cm[:, :, D + h * 512:D + (h + 1) * 512], writes=["CK"])
                cx.op('pool', lambda e: e.tensor_copy(out=SV[:], in_=CV[:]), reads=["CK"], writes=["SV"])
                run(h, SS0, SS1, lambda ec, kb: SK[:, ec, kb * 128:(kb + 1) * 128],
                    lambda ec, kb: SV[:, kb, ec * 128:(ec + 1) * 128], ["SK", "SV"])
        cx.op('pool', lambda e: e.memset(H[:, :, 0:2], 0.0), writes=HALL)
        z0 = 2 + (SP0 if p == 0 else 1024)
        cx.op('pool', lambda e: e.memset(H[:, :, z0:z0 + 2] if p == 0 else H[:, :, z0:NCOL + 2], 0.0), writes=HALL)
        if p == 0:
            cx.op('pool', lambda e: e.memset(H[:, :, 2 + SS1:NCOL + 2], 0.0), writes=HALL)
        for n in range(KC):
            wb, wkey = ws.get(KC + n)
            pset = nextset()
            proj_fm(wb, wkey, KC, lambda kc, a, nn: H[:, kc, 2 + a:2 + a + nn], HALL, pset)
            add_to_x(n, pset)
        cx.barrier()

    def mixer_a(p, layer):
        j = layer // 2
        alloc_wbuf(2, 3)
        alloc_attn()
        norm(p, layer, None)
        MSK = [sb([128, MWG[g]], BF16) for g in range(3)]
        OTA = sb([128, 4, NCOL], BF16)
        QT = sb([128, 3, NCOL], BF16); KT = sb([128, 3, NCOL], BF16)
        VT = sb([128, 3, 9, 128], BF16)
        TM = [sb([128, 9, 128], F32) for _ in range(1)]
        if p == 1:
            HK = sb([128, 1664], BF16); HV = sb([128, 13, 128], BF16)
        else:
            CS = sb([128, 16, 128], F32); SKT = sb([128, 2048], BF16); SVb = sb([128, 16, 128], BF16)
        for g in range(3):
            cx.dma('sp', MSK[g][:], msk_in[g], writes=["MSK"])
        cx.op('pool', lambda e: e.memset(OTA[:], 0.0), writes=["OTA"])
        for t in TM:
            cx.op('pool', lambda e, t=t: e.memset(t[:], 0.0), writes=["TM0", "TM1"])
        if p == 0:
            for g in range(3):
                W = WIN[g]
                cx.dma('sp', sw[g][j, 0:W - 8, :], cw[g][j, 8:W, :])
        scale = 128.0 ** -0.5
        hoff = (0, 128, 640)
        hboff = (0, 1, 5)
        tmi = 0
        ntb = 9 if p == 0 else 8

        def proj_tm(wb, wkey, g, kv, h):
            nonlocal tmi
            t = TM[0]
            tk = "TM0"
            tmi += 1
            for tb in range(ntb):
                pb = 6 + (tb // 4) % 2
                m = 128 if tb < 8 else 8
                c0 = 2 + (tb * 128 if tb < 8 else SS0)
                cx.mmg([(lambda e, kc=kc: e.matmul(PS[pb][0:m, (tb % 4) * 128:(tb % 4 + 1) * 128], lhsT=H[:, kc, c0:c0 + m], rhs=wb[:, kc, :],
                                                   start=(kc == 0), stop=(kc == KC - 1))) for kc in range(KC)],
                       reads=[wkey] + HALL, writes=[("PS", pb)])
                if tb % 4 == 3:
                    cx.op('act', lambda e: e.copy(out=t[:, tb - 3:tb + 1, :], in_=PS[pb][:].rearrange("p (a b) -> p a b", a=4)),
                          reads=[("PS", pb)], writes=[tk])
                elif tb == 8:
                    cx.op('act', lambda e: e.copy(out=t[0:8, 8, :], in_=PS[pb][0:8, 0:128]), reads=[("PS", pb)], writes=[tk])
            col0 = kv * 512 + h * 128
            W = WIN[g]
            first = 2048 - W
            tb0 = max(0, (first - p * 1024) // 128)
            if p * 1024 + 1024 > first and tb0 < 8:
                row0 = p * 1024 + tb0 * 128 - first
                nb = 8 - tb0
                cx.dma('act', pw[g][j, row0:row0 + nb * 128, col0:col0 + 128].rearrange("(b q) e -> q b e", q=128),
                       t[:, tb0:8, :], reads=[tk])
            if p == 0:
                cx.dma('act', sw[g][j, W - 8:W, col0:col0 + 128], t[0:8, 8, :], reads=[tk])
            if kv == 1:
                cx.op('dve', lambda e: e.tensor_copy(out=VT[:, g, 0:ntb, :], in_=t[:, 0:ntb, :]), reads=[tk], writes=["VT"])

        for h in range(4):
            blocks = []
            for g in range(3):
                for qkv in range(3):
                    c0 = g * 1536 + qkv * 512 + h * 128
                    blocks.append((w_in_a[j, :, c0:c0 + 128], KC))
            ws = WS(blocks)
            for g in range(3):
                for qkv in range(3):
                    wb, wkey = ws.get(g * 3 + qkv)
                    if qkv < 2:
                        pset = nextset()
                        proj_fm(wb, wkey, KC, lambda kc, a, nn: H[:, kc, 2 + a:2 + a + nn], HALL, pset)
                        dstT = QT if qkv == 0 else KT
                        dk = "QTA" if qkv == 0 else "KTA"
                        for ti, (a, b) in enumerate(CT):
                            cx.op('act', lambda e: e.copy(out=dstT[:, g, a:b], in_=PS[pset[ti]][:, 0:350]), reads=[("PS", pset[ti])], writes=[dk])
                    if qkv >= 1:
                        proj_tm(wb, wkey, g, qkv - 1, h)
            if p == 0:
                for g in range(3):
                    cx.dma('sp', kth[j, h, g], KT[:, g, 0:1024], reads=["KTA"], writes=["KTH"])
                    cx.dma('sp', vh[j, h, g].rearrange("(b q) e -> q b e", q=128), VT[:, g, 0:8, :], reads=["VT"], writes=["VH"])
            else:
                for g in range(3):
                    ng = min(WIN[g], 1024)
                    cx.dma('sp', HK[:, hoff[g]:hoff[g] + ng], kth[j, h, g][:, 1024 - ng:1024], reads=["KTH"], writes=["HK"])
                    cx.dma('sp', HV[:, hboff[g]:hboff[g] + ng // 128, :],
                           vh[j, h, g][1024 - ng:1024, :].rearrange("(b q) e -> q b e", q=128), reads=["VH"], writes=["HV"])

            def mkblock(g, nk, kap, vap, c0, n, q0, keys):
                def sfn(S):
                    return [lambda e: e.matmul(S[0:nk, 0:n], lhsT=kap, rhs=QT[:, g, q0:q0 + n], start=True, stop=False),
                            lambda e: e.matmul(S[0:nk, 0:n], lhsT=IDB[0:nk, 0:nk], rhs=MSK[g][0:nk, c0:c0 + n], start=False, stop=True)]
                return (nk, sfn, keys + ["QTA", "MSK", "IDB"], [vap], keys)

            for (q0, q1) in ((0, 512), (512, 1024)):
                n = q1 - q0
                vq0 = 1024 * p + q0
                groups = []
                for g in range(3):
                    if p == 1:
                        ng = min(WIN[g], 1024)
                        for bb in range(ng // 128):
                            vk0 = 1024 - ng + bb * 128
                            if _blk_valid(g, vq0, n, vk0, 128):
                                groups.append(mkblock(g, 128, HK[:, hoff[g] + bb * 128:hoff[g] + (bb + 1) * 128], HV[:, hboff[g] + bb, :],
                                                      vq0 - vk0 + 384, n, q0, ["HK", "HV"]))
                    for tb in range(8):
                        vk0 = 1024 * p + tb * 128
                        if _blk_valid(g, vq0, n, vk0, 128):
                            groups.append(mkblock(g, 128, KT[:, g, tb * 128:(tb + 1) * 128], VT[:, g, tb, :],
                                                  vq0 - vk0 + 384, n, q0, ["KTA", "VT"]))
                attend(n, groups, [4, 5], scale, None, lambda ui: (OTA[:, h, q0:q1], ["OTA"]), "E")
            if p == 0:
                groups = []
                n = 8
                for g in range(3):
                    W = WIN[g]
                    nbk = W // 128
                    def prep(g=g, W=W, nbk=nbk):
                        cx.dma('sp', CS[:, 0:nbk, :], cw[g][j, :, h * 128:(h + 1) * 128].rearrange("(b q) e -> q b e", q=128), writes=["CS"])
                        for b4 in range(0, nbk, 4):
                            m4 = min(4, nbk - b4)
                            cx.mmg([(lambda e, k=k: e.transpose(out=PS[7][:, k * 128:(k + 1) * 128], in_=CS[:, b4 + k, :], identity=IDF[:]))
                                    for k in range(m4)], reads=["CS", "IDF"], writes=[("PS", 7)])
                            cx.op('act', lambda e: e.copy(out=SKT[:, b4 * 128:(b4 + m4) * 128], in_=PS[7][:, 0:m4 * 128]),
                                  reads=[("PS", 7)], writes=["SKT"])
                        cx.dma('sp', CS[:, 0:nbk, :], cw[g][j, :, 512 + h * 128:512 + (h + 1) * 128].rearrange("(b q) e -> q b e", q=128),
                               reads=[], writes=["CS"])
                        cx.op('pool', lambda e: e.tensor_copy(out=SVb[:, 0:nbk, :], in_=CS[:, 0:nbk, :]), reads=["CS"], writes=["SVB"])
                    first = True
                    for kb in range(nbk):
                        if _blk_valid(g, W, n, kb * 128, 128):
                            blk = mkblock(g, 128, SKT[:, kb * 128:(kb + 1) * 128], SVb[:, kb, :], W - kb * 128 + 384, n, SS0, ["SKT", "SVB"])
                            groups.append(blk + ((prep if first else None),))
                            first = False
                    blk = mkblock(g, 8, KT[:, g, SS0:SS1], VT[0:8, g, 8, :], 384, n, SS0, ["KTA", "VT"])
                    groups.append(blk + ((prep if first else None),))
                attend_lazy(n, groups, [4, 5], scale, lambda ui: (OTA[:, h, SS0:SS1], ["OTA"]))
        blocks = [(w_out_a[j, :, nn * 128:(nn + 1) * 128], 4) for nn in range(KC)]
        ws = WS(blocks)
        for nn in range(KC):
            wb, wkey = ws.get(nn)
            pset = nextset()
            proj_fm(wb, wkey, 4, lambda kc, a, m: OTA[:, kc, a:a + m], ["OTA"], pset)
            add_to_x(nn, pset)
        cx.barrier()

    def attend_lazy(n, groups, u_list, scale, out_fn):
        attend(n, groups, u_list, scale, None, out_fn)

    def final(p):
        rstd_cols(lambda c, a, b: X[:, c, a:b], CT, lambda a, b: RS[:, a:b], [("X", c) for c in range(KC)])
        YT = sb([128, KC, 128], F32)
        YS = [sb([128, D], F32) for _ in range(2)]
        XK = [("X", c) for c in range(KC)]
        units = [(tb * 128, 128) for tb in range(8)] + ([(SS0, 8)] if p == 0 else [])
        for ui, (c0, m) in enumerate(units):
            for c in range(KC):
                eng = 'dve'
                cx.op(eng, lambda e: e.scalar_tensor_tensor(out=YT[:, c, 0:m], in0=X[:, c, c0:c0 + m], scalar=G_[:, 16, c:c + 1],
                                                            in1=RS[:, c0:c0 + m], op0=ALU.mult, op1=ALU.mult),
                      reads=[("X", c), "RS", "G"], writes=["YT"])
            s = ui % 2
            for c4 in range(4):
                pb = 6 + c4 % 2
                cx.mmg([(lambda e, k=k: e.transpose(out=PS[pb][0:m, k * 128:(k + 1) * 128], in_=YT[:, c4 * 4 + k, 0:m], identity=IDF[:]))
                        for k in range(4)], reads=["YT", "IDF"], writes=[("PS", pb)])
                cx.op('act', lambda e: e.copy(out=YS[s][0:m, c4 * 512:(c4 + 1) * 512], in_=PS[pb][0:m, :]), reads=[("PS", pb)], writes=[("YS", s)])
            if m == 128:
                cx.dma('act', yp[p * 1024 + c0:p * 1024 + c0 + 128, :], YS[s][:], reads=[("YS", s)])
            else:
                cx.dma('act', ys, YS[s][0:8, :], reads=[("YS", s)])
        cx.barrier()

    cx.op('pool', lambda e: e.memset(EPSAP[:], EPS), writes=["EPS"])
    load_consts()
    for p in range(npass):
        mark = nc_mark(nc)
        load_x(p)
        nc_release(nc, mark)
        for layer in layers:
            if "mix" in phases:
                m2 = nc_mark(nc)
                if layer % 2 == 0:
                    mixer_a(p, layer)
                else:
                    mixer_b(p, layer)
                nc_release(nc, m2)
            if "mem" in phases:
                m2 = nc_mark(nc)
                mem_attn(p, layer)
                nc_release(nc, m2)
            if "ffn" in phases:
                m2 = nc_mark(nc)
                ffn(p, layer)
                nc_release(nc, m2)
        if "final" in phases:
            m2 = nc_mark(nc)
            final(p)
            nc_release(nc, m2)
    cx.barrier()
    return nc


_ALLOCS = []
STOPAT = [0]
DECL = set()
_CNT = [0]


def nc_mark(nc):
    return len(_ALLOCS)


def nc_release(nc, mark):
    while len(_ALLOCS) > mark:
        t = _ALLOCS.pop()
        t.__exit__(None, None, None)


def _patch_alloc(nc):
    def alloc(shape, dt):
        _CNT[0] += 1
        cm = nc.sbuf_tensor("t%d" % _CNT[0], list(shape), dt)
        t = cm.__enter__()
        _ALLOCS.append(cm)
        return t
    return alloc


def _fm(a, lead):
    return a


def _host_inputs(inp):
    f32 = np.float32
    bf = ml_dtypes.bfloat16
    gains = np.concatenate([inp["g_mix"], inp["g_mem_q"], inp["g_mem_kv"], inp["g_ffn"], inp["g_final"][None]], axis=0).astype(f32)
    gfm = np.ascontiguousarray(gains.reshape(17, 16, 128).transpose(2, 0, 1)).reshape(128, -1)
    cffn = np.ascontiguousarray(inp["conv_ffn"].astype(f32).reshape(4, 3, 88, 128).transpose(3, 0, 1, 2)).reshape(128, -1)
    cbw = np.ascontiguousarray(inp["conv_b"].astype(f32).reshape(2, 3, 16, 128).transpose(3, 0, 1, 2)).reshape(128, -1)
    common = {
        "gfm": gfm, "cffn": cffn, "cbw": cbw,
        "idf": np.eye(128, dtype=f32), "idb": np.eye(128, dtype=f32).astype(bf), "onb": np.ones((128, 128), f32).astype(bf),
    }
    for g in range(3):
        common["msk%d" % g] = np.ascontiguousarray(MASKNP[g][:, :MWG[g]]).astype(bf)
    for k in ("w_in_a", "w_out_a", "w_in_b", "w_out_b", "w_q_mem", "w_kv_mem", "w_o_mem", "w_up", "w_down"):
        common[k] = np.ascontiguousarray(inp[k], dtype=f32)
    maps = []
    for c in range(8):
        s, b = c % 4, c
        m = dict(common)
        m["xp"] = np.ascontiguousarray(inp["x_prompt"][s], dtype=f32)
        m["xs"] = np.ascontiguousarray(inp["x_sample"][b], dtype=f32)
        m["mem"] = np.ascontiguousarray(inp["mem_prompt"][s], dtype=f32)
        for g, nm in enumerate(("cache_win0_kv", "cache_win1_kv", "cache_win2_kv")):
            m["cw%d" % g] = np.ascontiguousarray(inp[nm][:, b], dtype=f32).reshape(2, WIN[g], 1024)
        m["cmem"] = np.ascontiguousarray(inp["cache_mem_kv"][:, b], dtype=f32).reshape(4, 256, 4096)
        sfc = inp["state_ffn_conv"][:, b].astype(f32)
        m["sfc"] = np.ascontiguousarray(sfc.reshape(4, 2, 88, 128).transpose(3, 0, 2, 1)).reshape(128, -1)
        scb = inp["state_conv_b"][:, b].astype(f32)
        m["scb"] = np.ascontiguousarray(scb.reshape(2, 2, 16, 128).transpose(3, 0, 2, 1)).reshape(128, -1)
        maps.append(m)
    return maps


_NC_CACHE = {}


def kernel(**inputs):
    inp = {k: np.asarray(v) for k, v in inputs.items()}
    if "nc" not in _NC_CACHE:
        _NC_CACHE["nc"] = build_program()
    nc = _NC_CACHE["nc"]
    maps = [{k: v for k, v in m.items() if k in DECL} for m in _host_inputs(inp)]
    res = run_bass_kernel_spmd(nc, maps, core_ids=list(range(8)))
    R = res.results
    f32 = np.float32
    y_prompt = np.stack([R[s]["yp"] for s in range(4)]).astype(f32)
    y_sample = np.stack([R[b]["ys"] for b in range(8)]).astype(f32)
    outs = [y_prompt, y_sample]
    for g in range(3):
        outs.append(np.stack([R[s]["pw%d" % g] for s in range(4)], axis=1).reshape(2, 4, WIN[g], 2, 4, 128).astype(f32))
    outs.append(np.stack([R[s]["pcb"] for s in range(4)], axis=1).astype(f32))
    outs.append(np.stack([R[s]["pfc"] for s in range(4)], axis=1).astype(f32))
    outs.append(np.stack([R[s]["pmk"] for s in range(4)], axis=1).reshape(4, 4, 256, 2, 4, 512).astype(f32))
    for g in range(3):
        outs.append(np.stack([R[b]["sw%d" % g] for b in range(8)], axis=1).reshape(2, 8, WIN[g], 2, 4, 128).astype(f32))
    outs.append(np.stack([R[b]["scbo"] for b in range(8)], axis=1).astype(f32))
    outs.append(np.stack([R[b]["sfco"] for b in range(8)], axis=1).astype(f32))
    return tuple(outs)
```
